# Optimizing a Trainium2 kernel written in Bass

```python
import math
import jax, jax.numpy as jnp
from jax import lax
import numpy as np

D_MODEL = 2048
BATCH = 4
SEQ = 4096
DEPTH = 2

GRID_W = 64
CTX_LEN = 256
D_MIX = D_MODEL
D_CONV = D_MIX // 4
D_ATTN = D_MIX // 2
D_POOL = D_MIX // 4
N_HEADS = 8
V_DIM = D_ATTN // N_HEADS
QK_DIM = V_DIM // 2
ATTN_SCALE = QK_DIM ** -0.5
Q_BLOCK = 128
CONV_W = 3
POOL_WINDOWS = (2, 4, 8, 16)
N_POOL_GROUPS = 4
POOL_GROUP = D_POOL // N_POOL_GROUPS
ROPE_THETA = 10000.0
LN_EPS = 1e-5
RMS_EPS = 1e-5
DEEPNORM_ALPHA = (2 * DEPTH) ** 0.25
DEEPNORM_BETA = (8 * DEPTH) ** -0.25
SPLIT_SIZES = (D_CONV, D_CONV, D_CONV, D_CONV, D_ATTN, D_ATTN, D_ATTN, D_ATTN, D_POOL, D_POOL)
SPLIT_POINTS = tuple(int(s) for s in np.cumsum(SPLIT_SIZES)[:-1])
D_IN = sum(SPLIT_SIZES)
ATT_K_OFF = 4 * D_CONV + D_ATTN
ATT_V_END = 4 * D_CONV + 3 * D_ATTN

kernel_name = 'hybrid_parallel_heads_dit_block'


def _layernorm(x, g, b):
    xf = x.astype(jnp.float32)
    mu = jnp.mean(xf, axis=-1, keepdims=True)
    var = jnp.mean(jnp.square(xf - mu), axis=-1, keepdims=True)
    return ((xf - mu) * lax.rsqrt(var + LN_EPS) * g + b).astype(x.dtype)


def _qk_heads(t):
    return t.reshape(t.shape[0], t.shape[1], N_HEADS, 2, QK_DIM)


def _v_heads(t):
    return t.reshape(t.shape[0], t.shape[1], N_HEADS, V_DIM)


def _rope_axial(x, rows, cols):
    half = QK_DIM // 2
    n = half // 2
    inv = ROPE_THETA ** (-jnp.arange(n, dtype=jnp.float32) / n)

    def rot(v, pos):
        ang = pos.astype(jnp.float32)[:, None] * inv
        cos = jnp.cos(ang)[None, :, None, None, :]
        sin = jnp.sin(ang)[None, :, None, None, :]
        v1, v2 = v[..., :n], v[..., n:]
        return jnp.concatenate([v1 * cos - v2 * sin, v1 * sin + v2 * cos], axis=-1)

    xf = x.astype(jnp.float32)
    return jnp.concatenate([rot(xf[..., :half], rows), rot(xf[..., half:], cols)], axis=-1).astype(x.dtype)


def _diff_attn(qb, k_all, v_all, lam):
    s = jnp.einsum('bqhid,bkhid->bhiqk', qb.astype(jnp.float32) * ATTN_SCALE, k_all.astype(jnp.float32))
    p = jax.nn.softmax(s, axis=-1)
    a = p[:, :, 0] - lam * p[:, :, 1]
    return jnp.einsum('bhqk,bkhe->bqhe', a.astype(v_all.dtype), v_all)


def _diff_subln(o, g, lam_init):
    of = o.astype(jnp.float32)
    y = of * lax.rsqrt(jnp.mean(jnp.square(of), axis=-1, keepdims=True) + RMS_EPS) * g * (1.0 - lam_init)
    return y.reshape(o.shape[0], o.shape[1], D_ATTN).astype(o.dtype)


def _short_conv(u, b_gate, c_gate, g, w):
    v = c_gate * u
    T = u.shape[1]
    vp = jnp.pad(v, ((0, 0), (1, 1), (0, 0)))
    y = vp[:, :T] * w[0] + vp[:, 1:T + 1] * w[1] + vp[:, 2:] * w[2]
    return jax.nn.silu(g) * (b_gate * y)


def _multiscale_pool(u, g, pool_w_l, pool_scale_l):
    B, T, _ = u.shape
    ug = u.reshape(B, T, N_POOL_GROUPS, POOL_GROUP).astype(jnp.float32)
    csum = jnp.concatenate([jnp.zeros_like(ug[:, :1]), jnp.cumsum(ug, axis=1)], axis=1)
    t = jnp.arange(T)
    diffs = []
    for gi, w in enumerate(POOL_WINDOWS):
        lo = jnp.clip(t - w // 2, 0, T)
        hi = jnp.clip(t + w - w // 2, 0, T)
        mean = (csum[:, hi, gi] - csum[:, lo, gi]) / (hi - lo).astype(jnp.float32)[None, :, None]
        diffs.append(mean - ug[:, :, gi])
    d = jnp.stack(diffs, axis=2).astype(u.dtype)
    y = jnp.einsum('btgc,gcd->btgd', d, pool_w_l).reshape(B, T, D_POOL) * pool_scale_l
    return jax.nn.silu(g) * y


def _mix_output(parts, att, lam_init, conv_w_l, subln_g_l, pool_w_l, pool_scale_l, w_out_l):
    cu, cb, cc, cg, _q, _k, _v, ag, pu, pg = parts
    y_conv = _short_conv(cu, cb, cc, cg, conv_w_l)
    y_attn = jax.nn.silu(ag) * _diff_subln(att, subln_g_l, lam_init)
    y_pool = _multiscale_pool(pu, pg, pool_w_l, pool_scale_l)
    return jnp.concatenate([y_conv, y_attn, y_pool], axis=-1) @ w_out_l


def setup_inputs(seed: int = 0) -> dict:
    key = jax.random.key(seed)
    ks = jax.random.split(key, 18)
    f32 = jnp.float32

    def nrm(k, shape, s):
        return s * jax.random.normal(k, shape, f32)

    return {
        'x': nrm(ks[0], (BATCH, SEQ, D_MODEL), 1.0),
        'c': nrm(ks[1], (BATCH, D_MODEL), 1.0),
        'ctx': nrm(ks[2], (BATCH, CTX_LEN, D_MODEL), 1.0),
        'c_ctx': nrm(ks[3], (D_MODEL,), 1.0),
        'w_mod': nrm(ks[4], (DEPTH, D_MODEL, 3 * D_MODEL), 0.5 * D_MODEL ** -0.5),
        'b_mod': nrm(ks[5], (DEPTH, 3 * D_MODEL), 0.02),
        'w_in': nrm(ks[6], (DEPTH, D_MODEL, D_IN), D_MODEL ** -0.5),
        'conv_w': nrm(ks[7], (DEPTH, CONV_W, D_CONV), CONV_W ** -0.5),
        'lam_q1': nrm(ks[8], (DEPTH, QK_DIM), 0.1),
        'lam_k1': nrm(ks[9], (DEPTH, QK_DIM), 0.1),
        'lam_q2': nrm(ks[10], (DEPTH, QK_DIM), 0.1),
        'lam_k2': nrm(ks[11], (DEPTH, QK_DIM), 0.1),
        'subln_g': 1.0 + nrm(ks[12], (DEPTH, V_DIM), 0.02),
        'pool_w': nrm(ks[13], (DEPTH, N_POOL_GROUPS, POOL_GROUP, POOL_GROUP), POOL_GROUP ** -0.5),
        'pool_scale': 1.0 + nrm(ks[14], (DEPTH, D_POOL), 0.02),
        'w_out': nrm(ks[15], (DEPTH, D_MIX, D_MODEL), DEEPNORM_BETA * D_MIX ** -0.5),
        'ln_g': 1.0 + nrm(ks[16], (DEPTH, D_MODEL), 0.02),
        'ln_b': nrm(ks[17], (DEPTH, D_MODEL), 0.02),
    }


def reference(x, c, ctx, c_ctx, w_mod, b_mod, w_in, conv_w, lam_q1, lam_k1, lam_q2, lam_k2,
              subln_g, pool_w, pool_scale, w_out, ln_g, ln_b):
    B, L, _ = x.shape
    ROWS = L // GRID_W
    rows = jnp.repeat(jnp.arange(ROWS, dtype=jnp.int32), GRID_W)
    cols = jnp.broadcast_to(jnp.arange(GRID_W, dtype=jnp.int32)[None, :], (ROWS, GRID_W)).reshape(-1)
    n_blk = L // Q_BLOCK
    xl, xc = x, ctx
    for l in range(DEPTH):
        last = l == DEPTH - 1
        sh_l, sc_l, g_l = jnp.split(jax.nn.silu(c) @ w_mod[l] + b_mod[l], 3, axis=-1)
        sh_c, sc_c, g_c = jnp.split(jax.nn.silu(c_ctx) @ w_mod[l] + b_mod[l], 3, axis=-1)
        hl = xl * (1.0 + sc_l[:, None, :]) + sh_l[:, None, :]
        hc = xc * (1.0 + sc_c) + sh_c
        lam_init = 0.8 - 0.6 * math.exp(-0.3 * l)
        lam = (jnp.exp(jnp.sum(lam_q1[l].astype(jnp.float32) * lam_k1[l].astype(jnp.float32)))
               - jnp.exp(jnp.sum(lam_q2[l].astype(jnp.float32) * lam_k2[l].astype(jnp.float32)))
               + lam_init)
        parts_l = jnp.split(hl @ w_in[l], SPLIT_POINTS, axis=-1)
        ql = _rope_axial(_qk_heads(parts_l[4]), rows, cols)
        kl = _rope_axial(_qk_heads(parts_l[5]), rows, cols)
        vl = _v_heads(parts_l[6])
        if last:
            kc, vc = jnp.split(hc @ w_in[l][:, ATT_K_OFF:ATT_V_END], 2, axis=-1)
            parts_c = None
        else:
            parts_c = jnp.split(hc @ w_in[l], SPLIT_POINTS, axis=-1)
            kc, vc = parts_c[5], parts_c[6]
        kc = _qk_heads(kc)
        vc = _v_heads(vc)
        k_all = jnp.concatenate([kc, kl], axis=1)
        v_all = jnp.concatenate([vc, vl], axis=1)
        q_blocks = ql.reshape(B, n_blk, Q_BLOCK, N_HEADS, 2, QK_DIM).swapaxes(0, 1)
        att_l = lax.map(lambda qb: _diff_attn(qb, k_all, v_all, lam), q_blocks)
        att_l = att_l.swapaxes(0, 1).reshape(B, L, N_HEADS, V_DIM)
        out_l = _mix_output(parts_l, att_l, lam_init, conv_w[l], subln_g[l], pool_w[l], pool_scale[l], w_out[l])
        xl_new = _layernorm(DEEPNORM_ALPHA * xl + g_l[:, None, :] * out_l, ln_g[l], ln_b[l])
        if not last:
            att_c = _diff_attn(_qk_heads(parts_c[4]), kc, vc, lam)
            out_c = _mix_output(parts_c, att_c, lam_init, conv_w[l], subln_g[l], pool_w[l], pool_scale[l], w_out[l])
            xc = _layernorm(DEEPNORM_ALPHA * xc + g_c * out_c, ln_g[l], ln_b[l])
        xl = xl_new
    return xl
```

```python
import math
from contextlib import ExitStack

import numpy as np
import ml_dtypes

import concourse.bass as bass
import concourse.mybir as mybir
from concourse.bass_utils import run_bass_kernel_spmd

F32 = mybir.dt.float32
BF16 = mybir.dt.bfloat16
AF = mybir.ActivationFunctionType
ALU = mybir.AluOpType

P = 128
D = 2048
KC = 16
DEPTH = 2
BATCH = 4
SEQ = 4096
NCORES = 8
TL = 2048
HALO = 8
CTXN = 256
NT = TL + 2 * HALO + CTXN
COL_HL = TL
COL_HR = TL + HALO
COL_CTX = TL + 2 * HALO
PADW = 2336
PAD_HL, PAD_LAT, PAD_HR, PAD_Z1, PAD_CTX, PAD_Z2 = 0, 8, 2056, 2064, 2072, 2328
NQ = TL + CTXN
NKEY = CTXN + 2 * TL
NKT = NKEY // P
GRID_W = 64
N_HEADS = 8
POOL_WINDOWS = (2, 4, 8, 16)
ROPE_THETA = 10000.0
LN_EPS = 1e-5
RMS_EPS = 1e-5
ALPHA = (2 * DEPTH) ** 0.25
ATTN_SCALE = 64 ** -0.5
D_IN = 7168
NBLK = 48


def lam_init_of(l):
    return 0.8 - 0.6 * math.exp(-0.3 * l)


class Buf:
    __slots__ = ("name", "writers", "readers", "exclusive", "accum")

    def __init__(self, name="", exclusive=False, accum=False):
        self.name = name
        self.writers = {}
        self.readers = {}
        self.exclusive = exclusive
        self.accum = accum


class EngState:
    def __init__(self, name, eng, sem, semidx):
        self.name = name
        self.eng = eng
        self.sem = sem
        self.semidx = semidx
        self.count = 0
        self.waited = {}


class Tracker:
    def __init__(self, nc, es, n_dma_sems=12):
        self.nc = nc
        self.sems = []
        self.E = {}
        for name, eng in (("pe", nc.tensor), ("act", nc.scalar), ("dve", nc.vector),
                          ("pool", nc.gpsimd), ("sp", nc.sync)):
            sem = es.enter_context(nc.semaphore("c_" + name))
            self.sems.append(sem)
            self.E[name] = EngState(name, eng, sem, len(self.sems) - 1)
        self.dq = {}
        for q in ("sp", "pool"):
            lst = []
            for i in range(n_dma_sems):
                sem = es.enter_context(nc.semaphore("d_%s%d" % (q, i)))
                self.sems.append(sem)
                lst.append([len(self.sems) - 1, 0])
            self.dq[q] = {"sems": lst, "next": 0}
        sem = es.enter_context(nc.semaphore("cc_sem"))
        self.sems.append(sem)
        self.cc_sem = len(self.sems) - 1
        self.cc_count = 0

    def _wait(self, E, deps):
        for si, val in deps.items():
            if si == E.semidx and (E.name == "pe" or val > E.count):
                continue
            if E.waited.get(si, 0) >= val:
                continue
            E.eng.wait_ge(self.sems[si], val)
            E.waited[si] = val

    @staticmethod
    def _deps(reads, writes, own=None):
        deps = {}
        for b in reads:
            for si, v in b.writers.items():
                if deps.get(si, 0) < v:
                    deps[si] = v
            if b.exclusive:
                for si, v in b.readers.items():
                    if si != own and deps.get(si, 0) < v:
                        deps[si] = v
        for b in writes:
            srcs = () if (b.accum and not b.readers) else (b.writers, b.readers)
            for d in srcs:
                for si, v in d.items():
                    if deps.get(si, 0) < v:
                        deps[si] = v
        return deps

    @staticmethod
    def _record(reads, writes, si, val):
        for b in reads:
            if b.readers.get(si, 0) < val:
                b.readers[si] = val
        for b in writes:
            if b.accum and not b.readers:
                if b.writers.get(si, 0) < val:
                    b.writers[si] = val
            else:
                b.writers = {si: val}
                b.readers = {}

    def op(self, engname, fn, reads=(), writes=(), inc=True):
        E = self.E[engname]
        self._wait(E, self._deps(reads, writes, E.semidx))
        inst = fn(E.eng)
        if inc:
            E.count += 1
            inst.then_inc(E.sem, 1)
            val = E.count
        else:
            val = E.count + 1
        self._record(reads, writes, E.semidx, val)
        return inst

    def dma(self, q, out, in_, reads=(), writes=()):
        E = self.E[q]
        dq = self.dq[q]
        slot = dq["sems"][dq["next"]]
        dq["next"] = (dq["next"] + 1) % len(dq["sems"])
        deps = self._deps(reads, writes)
        if slot[1] > 0:
            deps[slot[0]] = max(deps.get(slot[0], 0), slot[1])
        self._wait(E, deps)
        slot[1] += 16
        E.eng.dma_start(out=out, in_=in_).then_inc(self.sems[slot[0]], 16)
        self._record(reads, writes, slot[0], slot[1])

    def collective(self, kind, groups, in_ap, out_ap, reads=(), writes=()):
        E = self.E["pool"]
        if self.cc_sem is None:
            raise RuntimeError("no cc semaphore")
        self._wait(E, self._deps(reads, writes))
        self.cc_count += 1
        E.eng.collective_compute(kind, ALU.bypass, replica_groups=groups,
                                 ins=[in_ap.opt()], outs=[out_ap.opt()]).then_inc(self.sems[self.cc_sem])
        self._record(reads, writes, self.cc_sem, self.cc_count)

    def barrier(self):
        deps = {}
        for E in self.E.values():
            if E.count > 0:
                deps[E.semidx] = E.count
        for dq in self.dq.values():
            for si, v in dq["sems"]:
                if v > 0:
                    deps[si] = v
        if self.cc_count > 0:
            deps[self.cc_sem] = self.cc_count
        for E in self.E.values():
            self._wait(E, deps)


_UNAME = [0]


def uname(name):
    _UNAME[0] += 1
    return "sb_%s_%d" % (name, _UNAME[0])


class Ring:
    def __init__(self, nc, es, name, n, shape, dtype):
        self.items = []
        for i in range(n):
            t = es.enter_context(nc.sbuf_tensor(uname("%s%d" % (name, i)), shape, dtype))
            self.items.append((t, Buf("%s%d" % (name, i))))
        self.i = 0

    def next(self):
        it = self.items[self.i]
        self.i = (self.i + 1) % len(self.items)
        return it


def _tile_w(wcols):
    C = wcols.shape[1]
    return np.ascontiguousarray(wcols.reshape(KC, P, C).transpose(1, 0, 2))


def block_col0():
    cols = []
    for h in range(8):
        cols.append(3072 + h * 128)
    for h in range(8):
        cols.append(2048 + h * 128)
        cols.append(5120 + h * 128)
    for j in range(4):
        cols += [j * 128, 1024 + j * 128, 512 + j * 128, 1536 + j * 128]
    for g in range(4):
        cols += [6144 + g * 128, 6656 + g * 128]
    return cols


def host_layer_weights(l, w_mod, b_mod, w_in, conv_w, lam_q1, lam_k1, lam_q2, lam_k2,
                       subln_g, pool_w, pool_scale, w_out, ln_g, ln_b):
    d = {}
    wm = w_mod[l]
    d["wmod"] = np.ascontiguousarray(
        wm.reshape(KC, P, 12, 512).transpose(2, 1, 0, 3))
    d["bmod"] = np.ascontiguousarray(np.stack([b_mod[l], b_mod[l]], 0))
    wi = w_in[l]
    cols = block_col0()
    d["wA"] = np.ascontiguousarray(
        np.stack([_tile_w(wi[:, c0:c0 + 128]) for c0 in cols], 0))
    d["wV"] = np.ascontiguousarray(
        np.stack([_tile_w(wi[:, 4096 + v * 512:4096 + (v + 1) * 512]) for v in range(2)], 0))
    d["convw"] = np.ascontiguousarray(conv_w[l].reshape(3, 4, P).transpose(2, 1, 0))
    d["poolw"] = np.ascontiguousarray(pool_w[l].transpose(1, 0, 2))
    d["pscale"] = np.ascontiguousarray(pool_scale[l].reshape(4, P).T)
    d["wout"] = _tile_w(w_out[l])
    d["lng"] = np.ascontiguousarray(ln_g[l].reshape(1, D))
    d["lnb"] = np.ascontiguousarray(ln_b[l].reshape(1, D))
    d["lamv"] = np.ascontiguousarray(
        np.stack([lam_q1[l], lam_k1[l], lam_q2[l], lam_k2[l]], 0).reshape(1, 256))
    d["subg"] = np.ascontiguousarray(subln_g[l].reshape(P, 1))
    return d


def host_core_consts(core):
    s = core % 2
    d = {}
    t = s * TL + np.arange(TL)
    rows = (t // GRID_W).astype(np.float64)
    colsp = (t % GRID_W).astype(np.float64)
    inv = ROPE_THETA ** (-np.arange(16, dtype=np.float64) / 16)
    cosT = np.zeros((P, TL), np.float64)
    sinT = np.zeros((P, TL), np.float64)
    for i in range(P):
        dd = i % 64
        pos = rows if dd < 32 else colsp
        e = dd % 32
        f = e % 16
        ang = (pos.astype(np.float32) * np.float32(inv[f])).astype(np.float64)
        cosT[i] = np.cos(ang)
        sinT[i] = np.sin(ang) * (-1.0 if e < 16 else 1.0)
    d["cosT"] = cosT.astype(np.float32)
    d["sinT"] = sinT.astype(np.float32)
    pm = np.zeros((P, P), np.float32)
    for i in range(P):
        e = i % 32
        j = i + 16 if e < 16 else i - 16
        pm[j, i] = 1.0
    d["pm"] = pm
    d["ident"] = np.eye(P, dtype=np.float32)
    d["masks"] = np.tile(np.array([[1.0 if s == 1 else 0.0, 1.0 if s == 0 else 0.0]], np.float32), (P, 1))
    rc = np.zeros((4, 4, 8), np.float32)
    for g, w in enumerate(POOL_WINDOWS):
        def cnt(tt, T):
            lo = np.clip(tt - w // 2, 0, T)
            hi = np.clip(tt + w - w // 2, 0, T)
            return (hi - lo).astype(np.float32)
        rc[g, 0] = 1.0 / cnt(s * TL + np.arange(8), SEQ)
        rc[g, 1] = 1.0 / cnt(s * TL + TL - 8 + np.arange(8), SEQ)
        rc[g, 2] = 1.0 / cnt(np.arange(8), CTXN)
        rc[g, 3] = 1.0 / cnt(CTXN - 8 + np.arange(8), CTXN)
    d["poolrc"] = np.ascontiguousarray(np.tile(rc.reshape(1, 128), (P, 1)))
    return d


def seg_tiles(with_ctx, with_halo):
    tiles = [(512 * t, 512, [(0, 512, "lat", 512 * t)]) for t in range(4)]
    if with_ctx and with_halo:
        tiles.append((TL, 272, [(0, 8, "hl", 0), (8, 8, "hr", 0), (16, 256, "ctx", 0)]))
    elif with_ctx:
        tiles.append((COL_CTX, 256, [(0, 256, "ctx", 0)]))
    elif with_halo:
        tiles.append((TL, 16, [(0, 8, "hl", 0), (8, 8, "hr", 0)]))
    return tiles


def pad_col(region, off):
    return {"lat": PAD_LAT, "hl": PAD_HL, "hr": PAD_HR, "ctx": PAD_CTX}[region] + off


def q_col(region, off):
    return {"lat": 0, "ctx": TL}[region] + off


def emit_phase_A(nc, tk, ps, psb, dr, layer, full_ctx, stop_after=None):
    es = ExitStack()
    with es:
        sb = lambda name, shape, dt: es.enter_context(nc.sbuf_tensor(uname(name), shape, dt))
        hT = sb("hT", [P, KC, NT], BF16)
        hbuf = {}

        def hb(tile_i, k):
            key = (tile_i, k)
            if key not in hbuf:
                hbuf[key] = Buf("hT%d_%d" % key)
            return hbuf[key]

        def hT_bufs(col0, n, k):
            out = []
            c = col0
            while c < col0 + n:
                if c < TL:
                    ti = c // 128
                    nxt = (ti + 1) * 128
                elif c < COL_CTX:
                    ti = 16
                    nxt = COL_CTX
                else:
                    ti = 17 + (c - COL_CTX) // 128
                    nxt = COL_CTX + (ti - 16) * 128
                out.append(hb(ti, k))
                c = nxt
            return out

        ident = sb("ident", [P, P], F32)
        b_ident = Buf("ident")
        modT = sb("modT", [P, 48, 2], F32)
        sc1T = sb("sc1T", [P, 16, 2], F32)
        b_mod = Buf("modT")
        tk.dma("sp", ident[:], dr["ident"][:, :], writes=[b_ident])

        with ExitStack() as es2:
            sb2 = lambda name, shape, dt: es2.enter_context(nc.sbuf_tensor(uname(name), shape, dt))
            bmod = sb2("bmodsb", [2, 6144], F32)
            b_bmod = Buf("bmod")
            modsb = sb2("modsb", [2, 6144], F32)
            b_modsb = Buf("modsb")
            tk.dma("sp", bmod[:], dr["bmod"][:, :], writes=[b_bmod])
            if dr.get("mod_dist"):
                md = dr["mod_dist"]
                csb = sb2("csb", [P, KC, 2], F32)
                b_csb = Buf("csb")
                msl = sb2("msl", [2, 3072], F32)
                b_msl = Buf("msl")
                wm_ring = Ring(nc, es2, "wm", 2, [P, KC, 512], F32)
                tk.dma("sp", csb[:], dr["cvec"][:, :, :], writes=[b_csb])
                tk.op("act", lambda e: e.activation(out=csb[:], in_=csb[:], func=AF.Silu),
                      reads=[b_csb], writes=[b_csb])
                for ct in range(6):
                    wm, b_wm = wm_ring.next()
                    tk.dma("sp", wm[:], dr["wmodH"][ct], writes=[b_wm])
                    bank = ct % 2
                    for k in range(KC):
                        tk.op("pe", lambda e, k=k, wm=wm, bank=bank: e.matmul(
                            ps[bank][0:2, :], csb[:, k, :], wm[:, k, :], start=(k == 0), stop=(k == KC - 1)),
                            reads=[b_csb, b_wm], writes=[psb[bank]], inc=(k == KC - 1))
                    tk.op("dve", lambda e, ct=ct, bank=bank: e.tensor_copy(
                        msl[0:2, ct * 512:(ct + 1) * 512], ps[bank][0:2, :]),
                        reads=[psb[bank]], writes=[b_msl])
                tk.dma("sp", md["part"][:, :], msl[:], reads=[b_msl], writes=[md["b_part"]])
                tk.collective("AllGather", PAIRS, md["part"], md["all"],
                              reads=[md["b_part"]], writes=[md["b_all"]])
                b_modsb.accum = True
                for r in range(2):
                    tk.dma("sp", modsb[0:2, r * 3072:(r + 1) * 3072], md["all"][2 * r:2 * r + 2, :],
                           reads=[md["b_all"]], writes=[b_modsb])
                tk.op("dve", lambda e: e.tensor_tensor(modsb[:], modsb[:], bmod[:], op=ALU.add),
                      reads=[b_modsb, b_bmod], writes=[b_modsb])
            else:
                csb = sb2("csb", [P, KC, 2], F32)
                b_csb = Buf("csb")
                wm_ring = Ring(nc, es2, "wm", 2, [P, KC, 512], F32)
                tk.dma("sp", csb[:], dr["cvec"][:, :, :], writes=[b_csb])
                tk.op("act", lambda e: e.activation(out=csb[:], in_=csb[:], func=AF.Silu),
                      reads=[b_csb], writes=[b_csb])
                for ct in range(12):
                    wm, b_wm = wm_ring.next()
                    tk.dma("sp", wm[:], dr["wmod"][ct], writes=[b_wm])
                    bank = ct % 2
                    for k in range(KC):
                        tk.op("pe", lambda e, k=k, wm=wm, bank=bank: e.matmul(
                            ps[bank][0:2, :], csb[:, k, :], wm[:, k, :], start=(k == 0), stop=(k == KC - 1)),
                            reads=[b_csb, b_wm], writes=[psb[bank]], inc=(k == KC - 1))
                    tk.op("dve", lambda e, ct=ct, bank=bank: e.tensor_tensor(
                        modsb[0:2, ct * 512:(ct + 1) * 512], ps[bank][0:2, :],
                        bmod[0:2, ct * 512:(ct + 1) * 512], op=ALU.add),
                        reads=[psb[bank], b_bmod], writes=[b_modsb])
            tk.dma("sp", dr["modv"][:, :], modsb[:], reads=[b_modsb], writes=[dr["b_modv"]])
            for i in range(48):
                tk.op("pe", lambda e, i=i: e.transpose(
                    ps[2][:, 2 * i:2 * i + 2], modsb[0:2, i * 128:(i + 1) * 128], ident[0:2, 0:2]),
                    reads=[b_modsb, b_ident], writes=[psb[2]], inc=(i == 47))
            tk.op("dve", lambda e: e.tensor_copy(modT[:].rearrange("p a b -> p (a b)"), ps[2][:, 0:96]),
                  reads=[psb[2]], writes=[b_mod])
            tk.op("dve", lambda e: e.tensor_scalar(
                sc1T[:].rearrange("p a b -> p (a b)"),
                modT[:, 16:32, :].rearrange("p a b -> p (a b)"), 1.0, None, op0=ALU.add),
                reads=[b_mod], writes=[b_mod])
            tk.barrier()
        if stop_after == "M":
            return

        with ExitStack() as es2:
            xs_ring = Ring(nc, es2, "xs", 2, [P, D], F32)
            row_tiles = [(i * 128, 128, i * 128, 0) for i in range(16)]
            row_tiles.append((TL, 16, TL, 0))
            row_tiles += [(TL + 16 + i * 128, 128, COL_CTX + i * 128, 1) for i in range(2)]
            ev = 0
            for ti, (r0, nr, c0, mc) in enumerate(row_tiles):
                xs, b_xs = xs_ring.next()
                tk.dma("sp", xs[0:nr, :], dr["xin"][r0:r0 + nr, :], reads=[dr["b_xin"]], writes=[b_xs])
                for jq in range(4):
                    bank = 4 + (jq % 2)
                    for q in range(4):
                        j = jq * 4 + q
                        tk.op("pe", lambda e, xs=xs, j=j, q=q, nr=nr, bank=bank: e.transpose(
                            ps[bank][:, q * nr:(q + 1) * nr], xs[0:nr, j * 128:(j + 1) * 128],
                            ident[0:nr, 0:nr]),
                            reads=[b_xs, b_ident], writes=[psb[bank]], inc=(q == 3))
                    for q in range(4):
                        j = jq * 4 + q
                        src = ps[bank][:, q * nr:(q + 1) * nr]
                        dst = hT[:, j, c0:c0 + nr]
                        if jq % 2 == 0:
                            tk.op("act", lambda e, src=src, dst=dst, j=j, mc=mc: e.activation(
                                out=dst, in_=src, func=AF.Identity,
                                scale=sc1T[:, j, mc:mc + 1], bias=modT[:, j, mc:mc + 1]),
                                reads=[psb[bank], b_mod], writes=[hb(ti, j)])
                        else:
                            tk.op("dve", lambda e, src=src, dst=dst, j=j, mc=mc: e.tensor_scalar(
                                dst, src, sc1T[:, j, mc:mc + 1], modT[:, j, mc:mc + 1],
                                op0=ALU.mult, op1=ALU.add),
                                reads=[psb[bank], b_mod], writes=[hb(ti, j)])
                        ev += 1
            tk.barrier()
        if stop_after == "A0":
            return

        w_ring = Ring(nc, es, "wr", 4, [P, KC, 128], BF16)
        w_slots = {}

        def load_w(blk):
            if blk >= NBLK or blk in w_slots:
                return
            wt, b_wt = w_ring.next()
            tk.dma("pool", wt[:], dr["wA"][blk], writes=[b_wt])
            w_slots[blk] = (wt, b_wt)

        proj_bank = [0]

        def project(blk, col0, ncols):
            wt, b_wt = w_slots[blk]
            bank = proj_bank[0]
            proj_bank[0] = (bank + 1) % 4
            for k in range(KC):
                tk.op("pe", lambda e, k=k: e.matmul(
                    ps[bank][:, 0:ncols], wt[:, k, :], hT[:, k, col0:col0 + ncols],
                    start=(k == 0), stop=(k == KC - 1)),
                    reads=[b_wt] + hT_bufs(col0, ncols, k), writes=[psb[bank]], inc=(k == KC - 1))
            return bank

        for blk in range(3):
            load_w(blk)

        with ExitStack() as es2:
            sb2 = lambda name, shape, dt: es2.enter_context(nc.sbuf_tensor(uname(name), shape, dt))
            cosT = sb2("cosT", [P, TL], F32)
            sinT = sb2("sinT", [P, TL], F32)
            pm = sb2("pm", [P, P], BF16)
            b_tab = Buf("tabs", accum=True)
            tk.dma("sp", cosT[:], dr["cosT"][:, :], writes=[b_tab])
            tk.dma("sp", sinT[:], dr["sinT"][:, :], writes=[b_tab])
            tk.dma("pool", pm[:], dr["pm"][:, :], writes=[b_tab])
            kb_ring = Ring(nc, es2, "kb", 2, [P, 512], BF16)
            t1_ring = Ring(nc, es2, "t1", 2, [P, 512], F32)
            t2_ring = Ring(nc, es2, "t2", 2, [P, 512], F32)
            ko_ring = Ring(nc, es2, "ko", 3, [P, 512], BF16)
            rope_bank = [0]

            def rope_block(blk, dst_lat, dst_ctx, b_dst, tiles):
                for (c0, n, segs) in tiles:
                    bank = project(blk, c0, n)
                    for (p0, sn, region, off) in segs:
                        if region == "lat":
                            kb, b_kb = kb_ring.next()
                            tk.op("act", lambda e, kb=kb: e.copy(out=kb[:, 0:sn], in_=ps[bank][:, p0:p0 + sn]),
                                  reads=[psb[bank]], writes=[b_kb])
                            rb = 4 + rope_bank[0]
                            rope_bank[0] = (rope_bank[0] + 1) % 2
                            tk.op("pe", lambda e, kb=kb, rb=rb: e.matmul(
                                ps[rb][:, 0:sn], pm[:], kb[:, 0:sn], start=True, stop=True),
                                reads=[b_kb, b_tab], writes=[psb[rb]])
                            t1, b_t1 = t1_ring.next()
                            t2, b_t2 = t2_ring.next()
                            tk.op("dve", lambda e, t1=t1: e.tensor_tensor(
                                t1[:, 0:sn], ps[bank][:, p0:p0 + sn], cosT[:, off:off + sn], op=ALU.mult),
                                reads=[psb[bank], b_tab], writes=[b_t1])
                            tk.op("dve", lambda e, t2=t2, rb=rb: e.tensor_tensor(
                                t2[:, 0:sn], ps[rb][:, 0:sn], sinT[:, off:off + sn], op=ALU.mult),
                                reads=[psb[rb], b_tab], writes=[b_t2])
                            ko, b_ko = ko_ring.next()
                            tk.op("dve", lambda e, t1=t1, t2=t2, ko=ko: e.tensor_tensor(
                                ko[:, 0:sn], t1[:, 0:sn], t2[:, 0:sn], op=ALU.add),
                                reads=[b_t1, b_t2], writes=[b_ko])
                            tk.dma("sp", dst_lat[:, off:off + sn], ko[:, 0:sn], reads=[b_ko], writes=[b_dst])
                        elif region == "ctx":
                            ko, b_ko = ko_ring.next()
                            tk.op("act", lambda e, ko=ko: e.copy(out=ko[:, 0:sn], in_=ps[bank][:, p0:p0 + sn]),
                                  reads=[psb[bank]], writes=[b_ko])
                            tk.dma("sp", dst_ctx[:, off:off + sn], ko[:, 0:sn], reads=[b_ko], writes=[b_dst])

            k_tiles = seg_tiles(True, False)
            for h in range(8):
                load_w(h + 3)
                rope_block(h, dr["KTg"][h], dr["KTc"][h], dr["b_kv"], k_tiles)

            if stop_after == "K":
                tk.barrier()
                return
            with ExitStack() as es3:
                wv_ring = Ring(nc, es3, "wv", 2, [P, KC, 512], BF16)
                vs_ring = Ring(nc, es3, "vs", 3, [P, 512], BF16)
                wvs = []
                for vb in range(2):
                    wv, b_wv = wv_ring.next()
                    tk.dma("pool", wv[:], dr["wV"][vb], writes=[b_wv])
                    wvs.append((wv, b_wv))
                vtiles = [(i * 128, dr["Vg"], i * 128) for i in range(16)]
                vtiles += [(COL_CTX + i * 128, dr["Vc"], i * 128) for i in range(2)]
                ev = 0
                for vb in range(2):
                    wv, b_wv = wvs[vb]
                    for (c0, dst, r0) in vtiles:
                        bank = proj_bank[0]
                        proj_bank[0] = (bank + 1) % 4
                        for k in range(KC):
                            tk.op("pe", lambda e, k=k, c0=c0, bank=bank, wv=wv: e.matmul(
                                ps[bank][:, :], hT[:, k, c0:c0 + 128], wv[:, k, :],
                                start=(k == 0), stop=(k == KC - 1)),
                                reads=[b_wv] + hT_bufs(c0, 128, k), writes=[psb[bank]], inc=(k == KC - 1))
                        vs, b_vs = vs_ring.next()
                        if ev % 2 == 0:
                            tk.op("act", lambda e, vs=vs, bank=bank: e.copy(out=vs[:], in_=ps[bank][:, :]),
                                  reads=[psb[bank]], writes=[b_vs])
                        else:
                            tk.op("dve", lambda e, vs=vs, bank=bank: e.tensor_copy(vs[:], ps[bank][:, :]),
                                  reads=[psb[bank]], writes=[b_vs])
                        ev += 1
                        tk.dma("sp", dst[r0:r0 + 128, vb * 512:(vb + 1) * 512], vs[:],
                               reads=[b_vs], writes=[dr["b_kv"]])
                tk.barrier()
            if dr.get("after_kv") is not None:
                dr["after_kv"]()

            if stop_after == "V":
                return
            q_tiles = seg_tiles(full_ctx, False)
            sg_ring = Ring(nc, es2, "sg", 3, [P, 512], BF16)
            for h in range(8):
                bq = 8 + 2 * h
                load_w(bq + 3)
                rope_block(bq, dr["QT"][h], dr["QT"][h][:, TL:NQ], dr["b_q"], q_tiles)
                load_w(bq + 4)
                for (c0, n, segs) in q_tiles:
                    bank = project(bq + 1, c0, n)
                    for (p0, sn, region, off) in segs:
                        sg, b_sg = sg_ring.next()
                        tk.op("act", lambda e, sg=sg, bank=bank: e.activation(
                            out=sg[:, 0:sn], in_=ps[bank][:, p0:p0 + sn], func=AF.Silu),
                            reads=[psb[bank]], writes=[b_sg])
                        qc = q_col(region, off)
                        tk.dma("sp", dr["SAG"][h][:, qc:qc + sn], sg[:, 0:sn], reads=[b_sg], writes=[dr["b_q"]])
            tk.barrier()

        if stop_after == "Q":
            return
        with ExitStack() as es2:
            sb2 = lambda name, shape, dt: es2.enter_context(nc.sbuf_tensor(uname(name), shape, dt))
            U = sb2("U", [P, PADW], F32)
            A = sb2("A", [P, PADW], F32)
            B = sb2("B", [P, PADW], F32)
            DT = sb2("DT", [P, PADW], BF16)
            b_U, b_A, b_B, b_DT = Buf("U"), Buf("A"), Buf("B"), Buf("DT")
            convw = sb2("convw", [P, 4, 3], F32)
            masks = sb2("masks", [P, 2], F32)
            poolrc = sb2("poolrc", [P, 128], F32)
            pscale = sb2("pscale", [P, 4], F32)
            poolw = sb2("poolw", [P, 4, 128], BF16)
            b_cst = Buf("cst", accum=True)
            tk.dma("sp", convw[:], dr["convw"][:, :, :], writes=[b_cst])
            tk.dma("sp", masks[:], dr["masks"][:, :], writes=[b_cst])
            tk.dma("sp", poolrc[:], dr["poolrc"][:, :], writes=[b_cst])
            tk.dma("sp", pscale[:], dr["pscale"][:, :], writes=[b_cst])
            tk.dma("pool", poolw[:], dr["poolw"][:, :, :], writes=[b_cst])
            tk.op("dve", lambda e: e.memset(U[:], 0.0), writes=[b_U])
            tk.op("dve", lambda e: e.memset(A[:], 0.0), writes=[b_A])
            tk.op("dve", lambda e: e.memset(B[:], 0.0), writes=[b_B])
            sf_ring = Ring(nc, es2, "sf", 3, [P, 512], F32)
            yo_ring = Ring(nc, es2, "yo", 3, [P, 512], BF16)
            c_tiles = seg_tiles(full_ctx, True)

            def mask_halos(T, b_T):
                tk.op("dve", lambda e: e.tensor_scalar(T[:, PAD_HL:PAD_HL + 8], T[:, PAD_HL:PAD_HL + 8],
                                                       masks[:, 0:1], None, op0=ALU.mult),
                      reads=[b_cst], writes=[b_T])
                tk.op("dve", lambda e: e.tensor_scalar(T[:, PAD_HR:PAD_HR + 8], T[:, PAD_HR:PAD_HR + 8],
                                                       masks[:, 1:2], None, op0=ALU.mult),
                      reads=[b_cst], writes=[b_T])

            def y_out(chunk, bank, p0, sn, region, off, mulsrc, b_mulsrc, scale_ap=None):
                yo, b_yo = yo_ring.next()
                if scale_ap is None:
                    tk.op("dve", lambda e: e.tensor_tensor(
                        yo[:, 0:sn], ps[bank][:, p0:p0 + sn], mulsrc, op=ALU.mult),
                        reads=[psb[bank], b_mulsrc], writes=[b_yo])
                else:
                    tk.op("dve", lambda e: e.scalar_tensor_tensor(
                        yo[:, 0:sn], ps[bank][:, p0:p0 + sn], scale_ap, mulsrc, op0=ALU.mult, op1=ALU.mult),
                        reads=[psb[bank], b_mulsrc, b_cst], writes=[b_yo])
                qc = q_col(region, off)
                tk.dma("sp", dr["YT"][chunk][:, qc:qc + sn], yo[:, 0:sn], reads=[b_yo], writes=[dr["b_y"]])

            for j in range(4):
                b0 = 24 + 4 * j
                load_w(b0 + 3)
                for (c0, n, segs) in c_tiles:
                    bank = project(b0, c0, n)
                    for (p0, sn, region, off) in segs:
                        pc = pad_col(region, off)
                        tk.op("act", lambda e, pc=pc: e.copy(out=U[:, pc:pc + sn], in_=ps[bank][:, p0:p0 + sn]),
                              reads=[psb[bank]], writes=[b_U])
                load_w(b0 + 4)
                for (c0, n, segs) in c_tiles:
                    bank = project(b0 + 1, c0, n)
                    for (p0, sn, region, off) in segs:
                        pc = pad_col(region, off)
                        tk.op("dve", lambda e, pc=pc: e.tensor_tensor(
                            U[:, pc:pc + sn], ps[bank][:, p0:p0 + sn], U[:, pc:pc + sn], op=ALU.mult),
                            reads=[psb[bank]], writes=[b_U])
                mask_halos(U, b_U)
                tk.op("dve", lambda e: e.tensor_scalar(A[:, 1:PADW - 1], U[:, 1:PADW - 1],
                                                       convw[:, j, 1:2], None, op0=ALU.mult),
                      reads=[b_U, b_cst], writes=[b_A])
                tk.op("dve", lambda e: e.scalar_tensor_tensor(
                    A[:, 1:PADW - 1], U[:, 0:PADW - 2], convw[:, j, 0:1], A[:, 1:PADW - 1],
                    op0=ALU.mult, op1=ALU.add), reads=[b_U, b_cst], writes=[b_A])
                tk.op("dve", lambda e: e.scalar_tensor_tensor(
                    A[:, 1:PADW - 1], U[:, 2:PADW], convw[:, j, 2:3], A[:, 1:PADW - 1],
                    op0=ALU.mult, op1=ALU.add), reads=[b_U, b_cst], writes=[b_A])
                load_w(b0 + 5)
                for (c0, n, segs) in c_tiles:
                    bank = project(b0 + 2, c0, n)
                    for (p0, sn, region, off) in segs:
                        if region in ("hl", "hr"):
                            continue
                        pc = pad_col(region, off)
                        tk.op("dve", lambda e, pc=pc: e.tensor_tensor(
                            A[:, pc:pc + sn], ps[bank][:, p0:p0 + sn], A[:, pc:pc + sn], op=ALU.mult),
                            reads=[psb[bank]], writes=[b_A])
                load_w(b0 + 6)
                for (c0, n, segs) in c_tiles:
                    bank = project(b0 + 3, c0, n)
                    for (p0, sn, region, off) in segs:
                        if region in ("hl", "hr"):
                            continue
                        pc = pad_col(region, off)
                        sf, b_sf = sf_ring.next()
                        tk.op("act", lambda e, sf=sf: e.activation(
                            out=sf[:, 0:sn], in_=ps[bank][:, p0:p0 + sn], func=AF.Silu),
                            reads=[psb[bank]], writes=[b_sf])
                        yo, b_yo = yo_ring.next()
                        tk.op("dve", lambda e, sf=sf, yo=yo, pc=pc: e.tensor_tensor(
                            yo[:, 0:sn], sf[:, 0:sn], A[:, pc:pc + sn], op=ALU.mult),
                            reads=[b_sf, b_A], writes=[b_yo])
                        qc = q_col(region, off)
                        tk.dma("sp", dr["YT"][j][:, qc:qc + sn], yo[:, 0:sn], reads=[b_yo], writes=[dr["b_y"]])

            for g in range(4):
                w = POOL_WINDOWS[g]
                b0 = 40 + 2 * g
                load_w(b0 + 3)
                for (c0, n, segs) in c_tiles:
                    bank = project(b0, c0, n)
                    for (p0, sn, region, off) in segs:
                        pc = pad_col(region, off)
                        tk.op("act", lambda e, pc=pc: e.copy(out=U[:, pc:pc + sn], in_=ps[bank][:, p0:p0 + sn]),
                              reads=[psb[bank]], writes=[b_U])
                mask_halos(U, b_U)
                src, b_src = U, b_U
                span = 1
                pp = [(A, b_A), (B, b_B)]
                pi = 0
                while span < w:
                    dst, b_dst = pp[pi]
                    pi ^= 1
                    n = PADW - span
                    tk.op("dve", lambda e, src=src, dst=dst, span=span, n=n: e.tensor_tensor(
                        dst[:, 0:n], src[:, 0:n], src[:, span:span + n], op=ALU.add),
                        reads=[b_src], writes=[b_dst])
                    src, b_src = dst, b_dst
                    span *= 2
                hw = w // 2
                tk.op("dve", lambda e, src=src, hw=hw, w=w: e.scalar_tensor_tensor(
                    DT[:, 8:PAD_Z2], src[:, 8 - hw:PAD_Z2 - hw], 1.0 / w, U[:, 8:PAD_Z2],
                    op0=ALU.mult, op1=ALU.subtract), reads=[b_src, b_U], writes=[b_DT])
                edges = [PAD_LAT, PAD_LAT + TL - 8, PAD_CTX, PAD_CTX + CTXN - 8]
                for ei, ec in enumerate(edges):
                    if ei >= 2 and not full_ctx:
                        continue
                    other, b_other = pp[pi]
                    ro = g * 32 + ei * 8
                    tk.op("dve", lambda e, src=src, other=other, ec=ec, ro=ro, hw=hw: e.tensor_tensor(
                        other[:, ec:ec + 8], src[:, ec - hw:ec - hw + 8], poolrc[:, ro:ro + 8], op=ALU.mult),
                        reads=[b_src, b_cst], writes=[b_other])
                    tk.op("dve", lambda e, other=other, ec=ec: e.tensor_tensor(
                        DT[:, ec:ec + 8], other[:, ec:ec + 8], U[:, ec:ec + 8], op=ALU.subtract),
                        reads=[b_other, b_U], writes=[b_DT])
                load_w(b0 + 4)
                for (c0, n, segs) in c_tiles:
                    segs2 = [sg_ for sg_ in segs if sg_[2] in ("lat", "ctx")]
                    if not segs2:
                        continue
                    bank = project(b0 + 1, c0, n)
                    for (p0, sn, region, off) in segs2:
                        pc = pad_col(region, off)
                        sf, b_sf = sf_ring.next()
                        tk.op("act", lambda e, sf=sf: e.activation(
                            out=sf[:, 0:sn], in_=ps[bank][:, p0:p0 + sn], func=AF.Silu),
                            reads=[psb[bank]], writes=[b_sf])
                        rb = 4 + (proj_bank[0] % 2)
                        tk.op("pe", lambda e, rb=rb, pc=pc: e.matmul(
                            ps[rb][:, 0:sn], poolw[:, g, :], DT[:, pc:pc + sn], start=True, stop=True),
                            reads=[b_DT, b_cst], writes=[psb[rb]])
                        y_out(12 + g, rb, 0, sn, region, off, sf[:, 0:sn], b_sf, scale_ap=pscale[:, g:g + 1])
            tk.barrier()
    tk.barrier()


def declare(nc, name, shape, dt, kind):
    return nc.dram_tensor(name, list(shape), dt, kind=kind).ap()


PHASE_A_INPUTS = [
    ("xin", (NT, D), F32), ("cvec", (P, KC, 2), F32), ("wmod", (12, P, KC, 512), F32),
    ("bmod", (2, 6144), F32), ("wA", (NBLK, P, KC, 128), F32), ("wV", (2, P, KC, 512), F32),
    ("cosT", (P, TL), F32), ("sinT", (P, TL), F32), ("pm", (P, P), F32), ("ident", (P, P), F32),
    ("convw", (P, 4, 3), F32), ("masks", (P, 2), F32), ("poolrc", (P, 128), F32),
    ("poolw", (P, 4, 128), F32), ("pscale", (P, 4), F32),
]
PHASE_A_OUTPUTS = [
    ("KTg", (8, P, TL), BF16), ("Vg", (TL, 1024), BF16), ("KTc", (8, P, CTXN), BF16),
    ("Vc", (CTXN, 1024), BF16), ("QT", (8, P, NQ), BF16), ("SAG", (8, P, NQ), BF16),
    ("YT", (16, P, NQ), BF16), ("modv", (2, 6144), F32),
]


def build_phase_A_program(layer, full_ctx, stop_after=None):
    nc = bass.Bass("TRN2", target_bir_lowering=False)
    dr = {}
    for name, shape, dt in PHASE_A_INPUTS:
        dr[name] = declare(nc, name, shape, dt, "ExternalInput")
    for name, shape, dt in PHASE_A_OUTPUTS:
        dr[name] = declare(nc, name, shape, dt, "ExternalOutput")
    for b in ("b_kv", "b_q", "b_y", "b_modv"):
        dr[b] = Buf(b, accum=True)
    dr["b_xin"] = Buf("b_xin")
    with ExitStack() as es:
        ps = [es.enter_context(nc.psum_tensor("ps%d" % i, [P, 512], F32)) for i in range(8)]
        psb = [Buf("ps%d" % i, exclusive=True) for i in range(8)]
        tk = Tracker(nc, es)
        emit_phase_A(nc, tk, ps, psb, dr, layer, full_ctx, stop_after)
    return nc


def emit_phase_B(nc, tk, ps, psb, dr, layer, with_ctx_q):
    lam_init = lam_init_of(layer)
    with ExitStack() as es:
        sb = lambda name, shape, dt: es.enter_context(nc.sbuf_tensor(uname(name), shape, dt))
        Vf = sb("Vf", [P, NKT, 1024], BF16)
        b_vf = [Buf("vf%d" % i) for i in range(9)]
        ones_bf = sb("ones_bf", [P, P], BF16)
        ones_f = sb("ones_f", [P, P], F32)
        epsb = sb("epsb", [P, 1], F32)
        lamb = sb("lamb", [P, 256], F32)
        lt = sb("lt", [P, 128], F32)
        sm = sb("sm", [P, 4], F32)
        neg_lam = sb("neg_lam", [P, 1], F32)
        gsub = sb("gsub", [P, 1], F32)
        b_c = Buf("constsB")
        tk.op("dve", lambda e: e.memset(ones_bf[:], 1.0), writes=[b_c])
        tk.op("dve", lambda e: e.memset(ones_f[:], 1.0 / 128.0), writes=[b_c])
        tk.op("dve", lambda e: e.memset(epsb[:], RMS_EPS), writes=[b_c])
        b_lam = Buf("lam")
        tk.dma("sp", lamb[:], dr["lamv"].partition_broadcast(P), writes=[b_lam])
        tk.dma("sp", gsub[:], dr["subg"][:, :], writes=[b_c])
        def vf_buf(kt):
            return b_vf[0] if kt < 2 else b_vf[1 + (kt - 2) // 4]

        tk.op("dve", lambda e: e.tensor_tensor(lt[:, 0:64], lamb[:, 0:64], lamb[:, 64:128], op=ALU.mult),
              reads=[b_lam], writes=[b_lam])
        tk.op("dve", lambda e: e.tensor_tensor(lt[:, 64:128], lamb[:, 128:192], lamb[:, 192:256], op=ALU.mult),
              reads=[b_lam], writes=[b_lam])
        tk.op("dve", lambda e: e.tensor_reduce(sm[:, 0:1], lt[:, 0:64], axis=mybir.AxisListType.X, op=ALU.add),
              reads=[b_lam], writes=[b_lam])
        tk.op("dve", lambda e: e.tensor_reduce(sm[:, 1:2], lt[:, 64:128], axis=mybir.AxisListType.X, op=ALU.add),
              reads=[b_lam], writes=[b_lam])
        tk.op("act", lambda e: e.activation(out=sm[:, 2:4], in_=sm[:, 0:2], func=AF.Exp),
              reads=[b_lam], writes=[b_lam])
        tk.op("dve", lambda e: e.tensor_tensor(neg_lam[:], sm[:, 3:4], sm[:, 2:3], op=ALU.subtract),
              reads=[b_lam], writes=[b_lam])
        tk.op("dve", lambda e: e.tensor_scalar(neg_lam[:], neg_lam[:], -lam_init, None, op0=ALU.add),
              reads=[b_lam], writes=[b_lam])
        tk.op("dve", lambda e: e.tensor_scalar(gsub[:], gsub[:], 1.0 - lam_init, None, op0=ALU.mult),
              reads=[b_c], writes=[b_c])

        kt_ring = Ring(nc, es, "KT", 2, [P, NKEY], BF16)
        q_ring = [Ring(nc, es, "Qp%d" % m, 2, [P, NQ], BF16) for m in range(2)]
        for m in range(2):
            for (t, b) in q_ring[m].items:
                tk.op("dve", lambda e, t=t: e.memset(t[:], 0.0), writes=[b])
        sag_ring = Ring(nc, es, "sagB", 3, [P, 512], BF16)
        e_ring = Ring(nc, es, "E", 10, [P, 512], BF16)
        es_ring = Ring(nc, es, "Esum", 6, [P, 512], BF16)
        pending_z = []
        f_ring = Ring(nc, es, "fB", 8, [P, 512], F32)
        yo_ring = Ring(nc, es, "yoB", 2, [P, 512], BF16)

        qtiles = [(512 * t, 512, list(range(NKT))) for t in range(4)]
        if with_ctx_q:
            qtiles.append((TL, CTXN, [0, 1]))

        heads = {}

        def load_head(h):
            if h >= N_HEADS or h in heads:
                return
            KT, b_KT = kt_ring.next()
            b_KT.accum = True
            tk.dma("sp", KT[:, 0:CTXN], dr["KTc"][h], reads=[dr["b_kv"]], writes=[b_KT])
            for r in range(2):
                tk.dma("sp", KT[:, CTXN + r * TL:CTXN + (r + 1) * TL], dr["kt_src"](r, h),
                       reads=[dr["b_kvall"]], writes=[b_KT])
            Q0, b_Q0 = q_ring[0].next()
            Q1, b_Q1 = q_ring[1].next()
            tk.dma("sp", Q0[0:64, :], dr["QT"][h][0:64, :], reads=[dr["b_q"]], writes=[b_Q0])
            tk.dma("sp", Q1[64:128, :], dr["QT"][h][64:128, :], reads=[dr["b_q"]], writes=[b_Q1])
            heads[h] = (KT, b_KT, (Q0, Q1), (b_Q0, b_Q1), None, None)

        load_head(0)
        tk.dma("sp", Vf[:, 0:2, :], dr["Vc"].rearrange("(t p) c -> p t c", p=P), reads=[dr["b_kv"]], writes=[b_vf[0]])
        for r in range(2):
            for q4 in range(4):
                t0 = 2 + 16 * r + 4 * q4
                tk.dma("sp", Vf[:, t0:t0 + 4, :],
                       dr["v_src"](r, q4).rearrange("(t p) c -> p t c", p=P),
                       reads=[dr["b_kvall"]], writes=[b_vf[1 + 4 * r + q4]])

        sidx = [0]
        pending_tail = []
        for h in range(N_HEADS):
            need_load = [h + 1]
            if not pending_tail:
                load_head(h + 1)
                need_load = []
            KT, b_KT, Qp, b_Qp, SG, b_SG = heads[h]
            for (q0, nq, kts) in qtiles:
                SG, b_SG = sag_ring.next()
                tk.dma("sp", SG[:, 0:nq], dr["SAG"][h][:, q0:q0 + nq], reads=[dr["b_q"]], writes=[b_SG])

                def s_mm(kt):
                    banks = []
                    for m in range(2):
                        bank = 2 * (sidx[0] % 2) + m
                        tk.op("pe", lambda e, m=m, bank=bank: e.matmul(
                            ps[bank][:, 0:nq], KT[:, kt * P:(kt + 1) * P], Qp[m][:, q0:q0 + nq],
                            start=True, stop=True),
                            reads=[b_KT, b_Qp[m]], writes=[psb[bank]])
                        banks.append(bank)
                    sidx[0] += 1
                    return banks

                pend = s_mm(kts[0])
                for i, kt in enumerate(kts):
                    if pending_tail and (i == 12 or (i == len(kts) - 1 and len(kts) < 13)):
                        pending_tail.pop(0)()
                    if need_load and not pending_tail:
                        load_head(need_load.pop())
                    cur = pend
                    if i + 1 < len(kts):
                        pend = s_mm(kts[i + 1])
                    es_ = []
                    for m in range(2):
                        E, b_E = e_ring.next()
                        tk.op("act", lambda e, E=E, m=m: e.activation(
                            out=E[:, 0:nq], in_=ps[cur[m]][:, 0:nq], func=AF.Exp, scale=ATTN_SCALE),
                            reads=[psb[cur[m]]], writes=[b_E])
                        es_.append((E, b_E))
                    first, last = (i == 0), (i == len(kts) - 1)
                    for m in range(2):
                        E, b_E = es_[m]
                        tk.op("pe", lambda e, E=E, m=m: e.matmul(
                            ps[4 + m][:, 0:nq], Vf[:, kt, h * P:(h + 1) * P], E[:, 0:nq],
                            start=first, stop=last),
                            reads=[b_E, vf_buf(kt)], writes=[psb[4 + m]], inc=last)
                    if len(pending_z) > 1:
                        pending_z.pop(0)()
                    if i % 2 == 0 and not last:
                        prev_es = es_
                    else:
                        zsrc = []
                        if i % 2 == 1:
                            for m in range(2):
                                S_, b_S = es_ring.next()
                                eng = "dve" if m == 0 else "pool"
                                tk.op(eng, lambda e, S_=S_, m=m: e.tensor_tensor(
                                    S_[:, 0:nq], prev_es[m][0][:, 0:nq], es_[m][0][:, 0:nq], op=ALU.add),
                                    reads=[prev_es[m][1], es_[m][1]], writes=[b_S])
                                zsrc.append((S_, b_S))
                        else:
                            zsrc = es_
                        gfirst, glast = (i // 2 == 0), last

                        def z_mm(zsrc=zsrc, gfirst=gfirst, glast=glast):
                            for m in range(2):
                                Zs, b_Zs = zsrc[m]
                                tk.op("pe", lambda e, Zs=Zs, m=m: e.matmul(
                                    ps[6 + m][:, 0:nq], ones_bf[:], Zs[:, 0:nq], start=gfirst, stop=glast),
                                    reads=[b_Zs, b_c], writes=[psb[6 + m]], inc=glast)
                        pending_z.append(z_mm)
                while pending_z:
                    pending_z.pop(0)()
                oc, zc = [], []
                for m in range(2):
                    o_, b_o = f_ring.next()
                    tk.op("act", lambda e, o_=o_, m=m: e.copy(out=o_[:, 0:nq], in_=ps[4 + m][:, 0:nq]),
                          reads=[psb[4 + m]], writes=[b_o])
                    oc.append((o_, b_o))
                    z_, b_z = f_ring.next()
                    tk.op("dve", lambda e, z_=z_, m=m: e.tensor_copy(z_[:, 0:nq], ps[6 + m][:, 0:nq]),
                          reads=[psb[6 + m]], writes=[b_z])
                    zc.append((z_, b_z))
                for m in range(2):
                    z_, b_z = zc[m]
                    o_, b_o = oc[m]
                    tk.op("dve", lambda e, z_=z_: e.reciprocal(z_[:, 0:nq], z_[:, 0:nq]),
                          reads=[b_z], writes=[b_z])
                    tk.op("pool", lambda e, z_=z_, o_=o_: e.tensor_tensor(
                        o_[:, 0:nq], o_[:, 0:nq], z_[:, 0:nq], op=ALU.mult),
                        reads=[b_z, b_o], writes=[b_o])
                att, b_att = oc[0]
                tk.op("dve", lambda e: e.scalar_tensor_tensor(
                    att[:, 0:nq], oc[1][0][:, 0:nq], neg_lam[:, 0:1], att[:, 0:nq],
                    op0=ALU.mult, op1=ALU.add),
                    reads=[oc[1][1], b_lam], writes=[b_att])
                def make_tail(att=att, b_att=b_att, oc=oc, zc=zc, nq=nq, q0=q0, h=h, SG=SG, b_SG=b_SG):
                    sq, b_sq = oc[1]
                    tk.op("pool", lambda e: e.tensor_tensor(sq[:, 0:nq], att[:, 0:nq], att[:, 0:nq], op=ALU.mult),
                          reads=[b_att], writes=[b_sq])
                    mbank = 2 * (sidx[0] % 2)
                    tk.op("pe", lambda e: e.matmul(ps[mbank][:, 0:nq], ones_f[:], sq[:, 0:nq], start=True, stop=True),
                          reads=[b_sq, b_c], writes=[psb[mbank]])
                    lnv, b_lnv = zc[0]
                    tk.op("act", lambda e: e.activation(out=lnv[:, 0:nq], in_=ps[mbank][:, 0:nq], func=AF.Ln,
                                                        bias=epsb[:, 0:1]),
                          reads=[psb[mbank], b_c], writes=[b_lnv])
                    tk.op("act", lambda e: e.activation(out=lnv[:, 0:nq], in_=lnv[:, 0:nq], func=AF.Exp, scale=-0.5),
                          reads=[b_lnv], writes=[b_lnv])
                    tk.op("dve", lambda e: e.scalar_tensor_tensor(
                        att[:, 0:nq], att[:, 0:nq], gsub[:, 0:1], lnv[:, 0:nq], op0=ALU.mult, op1=ALU.mult),
                        reads=[b_att, b_lnv, b_c], writes=[b_att])
                    yo, b_yo = yo_ring.next()
                    tk.op("pool", lambda e: e.tensor_tensor(yo[:, 0:nq], att[:, 0:nq], SG[:, 0:nq], op=ALU.mult),
                          reads=[b_att, b_SG], writes=[b_yo])
                    tk.dma("sp", dr["YTat"][h][:, q0:q0 + nq], yo[:, 0:nq], reads=[b_yo], writes=[dr["b_yat"]])
                    return None
                pending_tail.append(make_tail)
        while pending_tail:
            pending_tail.pop(0)()
        tk.barrier()


def load_wout(nc, tk, dr, Wo):
    b_wo = [Buf("wo%d" % i) for i in range(4)]
    for i in range(4):
        tk.dma("pool", Wo[:, 4 * i:4 * i + 4, :], dr["wout"][:, 4 * i:4 * i + 4, :], writes=[b_wo[i]])
    return b_wo


def emit_phase_C(nc, tk, ps, psb, dr, layer, with_ctx):
    with ExitStack() as es:
        sb = lambda name, shape, dt: es.enter_context(nc.sbuf_tensor(uname(name), shape, dt))
        if dr.get("Wo") is not None:
            Wo, b_wo = dr["Wo"]
        else:
            Wo = sb("Wo", [P, KC, D], BF16)
            b_wo = load_wout(nc, tk, dr, Wo)
        gB = [sb("gB%d" % i, [P, D], F32) for i in range(2)]
        lnG = sb("lnG", [P, D], F32)
        lnB = sb("lnB", [P, D], F32)
        epsb = sb("epsC", [P, 1], F32)
        b_c = Buf("constsC", accum=True)
        tk.op("dve", lambda e: e.memset(epsb[:], LN_EPS), writes=[b_c])
        y_ring = Ring(nc, es, "Ys", 2, [P, KC, 512], BF16)
        x_ring = Ring(nc, es, "xC", 3, [P, D], F32)
        T_ring = Ring(nc, es, "TC", 2, [P, D], F32)
        o_ring = Ring(nc, es, "oC", 3, [P, D], F32)
        st_ring = Ring(nc, es, "stC", 2, [P, 32], F32)

        groups = [(512 * t, 512, 512 * t, 512 * t, 0) for t in range(4)]
        if with_ctx:
            groups.append((TL, CTXN, dr["xin_ctx_row0"], dr["xout_ctx_row0"], 1))
        pbank = [0]
        ys_loaded = {}

        def load_ys(g):
            if g >= len(groups) or g in ys_loaded:
                return
            c0_, ncol_ = groups[g][0], groups[g][1]
            Ys_, b_Ys_ = y_ring.next()
            b_Ys_.accum = True
            for k in range(KC):
                src_ap, src_buf = dr["y_src"](k)
                tk.dma("sp", Ys_[:, k, 0:ncol_], src_ap[:, c0_:c0_ + ncol_], reads=[src_buf], writes=[b_Ys_])
            ys_loaded[g] = (Ys_, b_Ys_)

        tiles = []
        for g, (c0, ncol, xr0, or0, gi) in enumerate(groups):
            for st in range(ncol // P):
                tiles.append((g, st, xr0, or0, gi))
        xs_loaded = {}

        def load_x(i):
            if i >= len(tiles) or i in xs_loaded:
                return
            g_, st_, xr0_, or0_, gi_ = tiles[i]
            xs_, b_xs_ = x_ring.next()
            tk.dma("sp", xs_[:], dr["xin"][xr0_ + st_ * P:xr0_ + (st_ + 1) * P, :], reads=[dr["b_xin"]], writes=[b_xs_])
            xs_loaded[i] = (xs_, b_xs_)

        load_ys(0)
        load_x(0)
        tk.dma("sp", gB[0][:], dr["modv"][0:1, 4096:6144].partition_broadcast(P), reads=[dr["b_modv"]], writes=[b_c])
        if with_ctx:
            tk.dma("sp", gB[1][:], dr["modv"][1:2, 4096:6144].partition_broadcast(P), reads=[dr["b_modv"]], writes=[b_c])
        tk.dma("sp", lnG[:], dr["lng"].partition_broadcast(P), writes=[b_c])
        tk.dma("sp", lnB[:], dr["lnb"].partition_broadcast(P), writes=[b_c])
        for ti, (g, st, xr0, or0, gi) in enumerate(tiles):
            if True:
                load_x(ti + 1)
                if st == 0:
                    load_ys(g + 1)
                Ys, b_Ys = ys_loaded[g]
                xs, b_xs = xs_loaded[ti]
                T, b_T = T_ring.next()
                half = pbank[0] % 2
                pbank[0] += 1
                for cg in range(4):
                    bank = 4 * half + cg
                    for k in range(KC):
                        tk.op("pe", lambda e, k=k, bank=bank, cg=cg: e.matmul(
                            ps[bank][:, :], Ys[:, k, st * P:(st + 1) * P], Wo[:, k, cg * 512:(cg + 1) * 512],
                            start=(k == 0), stop=(k == KC - 1)),
                            reads=[b_Ys, b_wo[k // 4]], writes=[psb[bank]], inc=(k == KC - 1))
                    tk.op("dve", lambda e, bank=bank, cg=cg: e.tensor_tensor(
                        T[:, cg * 512:(cg + 1) * 512], ps[bank][:, :], gB[gi][:, cg * 512:(cg + 1) * 512],
                        op=ALU.mult), reads=[psb[bank], b_c], writes=[b_T])
                tk.op("dve", lambda e: e.scalar_tensor_tensor(T[:], xs[:], ALPHA, T[:], op0=ALU.mult, op1=ALU.add),
                      reads=[b_xs, b_T], writes=[b_T])
                stt, b_st = st_ring.next()
                for cg in range(4):
                    tk.op("dve", lambda e, cg=cg: e.bn_stats(stt[:, cg * 6:(cg + 1) * 6], T[:, cg * 512:(cg + 1) * 512]),
                          reads=[b_T], writes=[b_st])
                tk.op("dve", lambda e: e.bn_aggr(stt[:, 24:26], stt[:, 0:24]), reads=[b_st], writes=[b_st])
                tk.op("act", lambda e: e.activation(out=stt[:, 26:27], in_=stt[:, 25:26], func=AF.Ln, bias=epsb[:, 0:1]),
                      reads=[b_st, b_c], writes=[b_st])
                tk.op("act", lambda e: e.activation(out=stt[:, 26:27], in_=stt[:, 26:27], func=AF.Exp, scale=-0.5),
                      reads=[b_st], writes=[b_st])
                tk.op("dve", lambda e: e.scalar_tensor_tensor(
                    stt[:, 27:28], stt[:, 24:25], -1.0, stt[:, 26:27], op0=ALU.mult, op1=ALU.mult),
                    reads=[b_st], writes=[b_st])
                o, b_o = o_ring.next()
                tk.op("act", lambda e: e.activation(out=o[:], in_=T[:], func=AF.Identity,
                                                    scale=stt[:, 26:27], bias=stt[:, 27:28]),
                      reads=[b_T, b_st], writes=[b_o])
                tk.op("pool", lambda e: e.tensor_tensor(o[:], o[:], lnG[:], op=ALU.mult),
                      reads=[b_o, b_c], writes=[b_o])
                tk.op("pool", lambda e: e.tensor_tensor(o[:], o[:], lnB[:], op=ALU.add),
                      reads=[b_o, b_c], writes=[b_o])
                tk.dma("sp", dr["xout"][or0 + st * P:or0 + (st + 1) * P, :], o[:], reads=[b_o], writes=[dr["b_xout"]])
                if dr["hx"] is not None and gi == 0:
                    if or0 + st * P == 0:
                        tk.dma("sp", dr["hx"][0:HALO, :], o[0:HALO, :], reads=[b_o], writes=[dr["b_hx"]])
                    if or0 + (st + 1) * P == TL:
                        tk.dma("sp", dr["hx"][HALO:2 * HALO, :], o[P - HALO:P, :], reads=[b_o], writes=[dr["b_hx"]])
        tk.barrier()


PHASE_BC_INPUTS = [
    ("KTall", (2, 8, P, TL), BF16), ("Vall", (2, TL, 1024), BF16), ("KTc", (8, P, CTXN), BF16),
    ("Vc", (CTXN, 1024), BF16), ("QT", (8, P, NQ), BF16), ("SAG", (8, P, NQ), BF16),
    ("YT", (16, P, NQ), BF16), ("modv", (2, 6144), F32), ("xin", (NT, D), F32),
    ("wout", (P, KC, D), F32), ("lng", (1, D), F32), ("lnb", (1, D), F32),
    ("lamv", (1, 256), F32), ("subg", (P, 1), F32),
]


def build_phase_BC_program(layer, with_ctx, do_B=True, do_C=True):
    nc = bass.Bass("TRN2", target_bir_lowering=False)
    dr = {}
    for name, shape, dt in PHASE_BC_INPUTS:
        dr[name] = declare(nc, name, shape, dt, "ExternalInput")
    dr["xout"] = declare(nc, "xout", (NQ, D), F32, "ExternalOutput")
    dr["YTat"] = declare(nc, "YTat", (8, P, NQ), BF16, "ExternalOutput")
    for b in ("b_kv", "b_kvall", "b_q", "b_y", "b_modv", "b_yat", "b_xout"):
        dr[b] = Buf(b, accum=True)
    dr["b_xin"] = Buf("b_xin")
    dr["xin_ctx_row0"] = COL_CTX
    dr["xout_ctx_row0"] = TL
    dr["hx"] = None

    def y_src(k):
        if 4 <= k < 12:
            return dr["YTat"][k - 4], dr["b_yat"]
        return dr["YT"][k], dr["b_y"]
    dr["y_src"] = y_src
    dr["kt_src"] = lambda r, h: dr["KTall"][r][h]
    dr["v_src"] = lambda r, q4: dr["Vall"][r][q4 * 512:(q4 + 1) * 512, :]
    with ExitStack() as es:
        ps = [es.enter_context(nc.psum_tensor("ps%d" % i, [P, 512], F32)) for i in range(8)]
        psb = [Buf("ps%d" % i, exclusive=True) for i in range(8)]
        tk = Tracker(nc, es)
        if do_B:
            emit_phase_B(nc, tk, ps, psb, dr, layer, with_ctx)
        if do_C:
            emit_phase_C(nc, tk, ps, psb, dr, layer, with_ctx)
        tk.barrier()
    return nc


WKEYS = ["w_mod", "b_mod", "w_in", "conv_w", "lam_q1", "lam_k1", "lam_q2", "lam_k2",
         "subln_g", "pool_w", "pool_scale", "w_out", "ln_g", "ln_b"]


def make_xin(core, xfull_b, ctx_b):
    s = core % 2
    z = np.zeros((HALO, D), np.float32)
    hl = xfull_b[s * TL - HALO:s * TL] if s == 1 else z
    hr = xfull_b[(s + 1) * TL:(s + 1) * TL + HALO] if s == 0 else z
    return np.ascontiguousarray(np.concatenate([xfull_b[s * TL:(s + 1) * TL], hl, hr, ctx_b], 0))


_PROGS = {}


def get_prog(kind, layer, flag):
    key = (kind, layer, flag)
    if key not in _PROGS:
        if kind == "A":
            _PROGS[key] = build_phase_A_program(layer, flag)
        else:
            _PROGS[key] = build_phase_BC_program(layer, flag)
    return _PROGS[key]


def kernel_unfused(**inputs):
    inp = {k: np.asarray(v) for k, v in inputs.items()}
    cores = list(range(NCORES))
    consts = [host_core_consts(c) for c in cores]
    x_cur = [inp["x"][b] for b in range(BATCH)]
    ctx_cur = [inp["ctx"][b] for b in range(BATCH)]
    for l in range(DEPTH):
        last = (l == DEPTH - 1)
        lw = host_layer_weights(l, *[inp[k] for k in WKEYS])
        xins = [make_xin(c, x_cur[c // 2], ctx_cur[c // 2]) for c in cores]
        in_maps = []
        for c in cores:
            b = c // 2
            cvec = np.ascontiguousarray(
                np.stack([inp["c"][b], inp["c_ctx"]], -1).reshape(KC, P, 2).transpose(1, 0, 2))
            m = dict(xin=xins[c], cvec=cvec)
            for k in ("wmod", "bmod", "wA", "wV", "convw", "poolw", "pscale"):
                m[k] = lw[k]
            for k in ("cosT", "sinT", "pm", "ident", "masks", "poolrc"):
                m[k] = consts[c][k]
            in_maps.append(m)
        resA = run_bass_kernel_spmd(get_prog("A", l, not last), in_maps, core_ids=cores).results
        in_maps = []
        for c in cores:
            pr = (c // 2) * 2
            r = resA[c]
            m = dict(KTall=np.stack([np.asarray(resA[pr]["KTg"]), np.asarray(resA[pr + 1]["KTg"])], 0),
                     Vall=np.stack([np.asarray(resA[pr]["Vg"]), np.asarray(resA[pr + 1]["Vg"])], 0),
                     KTc=np.asarray(r["KTc"]), Vc=np.asarray(r["Vc"]), QT=np.asarray(r["QT"]),
                     SAG=np.asarray(r["SAG"]), YT=np.asarray(r["YT"]), modv=np.asarray(r["modv"]),
                     xin=xins[c])
            for k in ("wout", "lng", "lnb", "lamv", "subg"):
                m[k] = lw[k]
            in_maps.append(m)
        resBC = run_bass_kernel_spmd(get_prog("BC", l, not last), in_maps, core_ids=cores).results
        x_cur = [np.concatenate([np.asarray(resBC[2 * b]["xout"])[:TL], np.asarray(resBC[2 * b + 1]["xout"])[:TL]], 0)
                 for b in range(BATCH)]
        if not last:
            ctx_cur = [np.asarray(resBC[2 * b]["xout"])[TL:NQ] for b in range(BATCH)]
    return np.ascontiguousarray(np.stack(x_cur, 0).astype(np.float32))


PAIRS = [[0, 1], [2, 3], [4, 5], [6, 7]]
LAYER_INPUTS = [
    ("wmodH", (6, P, KC, 512), F32), ("bmod", (2, 6144), F32), ("wA", (NBLK, P, KC, 128), F32),
    ("wV", (2, P, KC, 512), F32), ("convw", (P, 4, 3), F32), ("poolw", (P, 4, 128), F32),
    ("pscale", (P, 4), F32), ("wout", (P, KC, D), F32), ("lng", (1, D), F32), ("lnb", (1, D), F32),
    ("lamv", (1, 256), F32), ("subg", (P, 1), F32),
]
CORE_INPUTS = [
    ("xin0", (NT, D), F32), ("cvec", (P, KC, 2), F32), ("cosT", (P, TL), F32), ("sinT", (P, TL), F32),
    ("pm", (P, P), F32), ("ident", (P, P), F32), ("masks", (P, 2), F32), ("poolrc", (P, 128), F32),
]


def build_fused_program():
    nc = bass.Bass("TRN2", target_bir_lowering=False)
    base = {}
    for name, shape, dt in CORE_INPUTS:
        base[name] = declare(nc, name, shape, dt, "ExternalInput")
    lay = []
    for l in range(DEPTH):
        d = {}
        for name, shape, dt in LAYER_INPUTS:
            d[name] = declare(nc, "%s%d" % (name, l), shape, dt, "ExternalInput")
        lay.append(d)
    out = declare(nc, "out", (TL, D), F32, "ExternalOutput")
    internal = lambda name, shape, dt: nc.dram_tensor(name, list(shape), dt).ap()
    it = {
        "KTg": internal("KTg", (8, P, TL), BF16), "Vg": internal("Vg", (TL, 1024), BF16),
        "KTall": internal("KTall", (2, 2, 4, P, TL), BF16), "Vall": internal("Vall", (2, 2, TL // 2, 1024), BF16),
        "KTc": internal("KTc", (8, P, CTXN), BF16), "Vc": internal("Vc", (CTXN, 1024), BF16),
        "QT": internal("QT", (8, P, NQ), BF16), "SAG": internal("SAG", (8, P, NQ), BF16),
        "YT": internal("YT", (16, P, NQ), BF16), "modv": internal("modv", (2, 6144), F32),
        "X1": internal("X1", (NT, D), F32), "hx": internal("hx", (2 * HALO, D), F32),
        "hxall": internal("hxall", (4 * HALO, D), F32),
    }
    mod_dist = {"part": internal("modpart", (2, 3072), F32), "all": internal("modall", (4, 3072), F32),
                "b_part": Buf("b_modpart"), "b_all": Buf("b_modall")}
    bufs = {b: Buf(b, accum=True) for b in ("b_kv", "b_kvall", "b_q", "b_y", "b_modv", "b_yat", "b_x1", "b_hx",
                                             "b_hxall", "b_out")}
    with ExitStack() as es:
        ps = [es.enter_context(nc.psum_tensor("ps%d" % i, [P, 512], F32)) for i in range(8)]
        psb = [Buf("ps%d" % i, exclusive=True) for i in range(8)]
        tk = Tracker(nc, es)
        for l in range(DEPTH):
            last = (l == DEPTH - 1)
            dr = dict(base)
            dr.update(lay[l])
            dr.update(it)
            dr.update(bufs)
            dr["b_yat"] = bufs["b_y"]
            dr["mod_dist"] = mod_dist
            dr["YTat"] = it["YT"][4:12]
            dr["y_src"] = lambda k: (it["YT"][k], bufs["b_y"])
            dr["kt_src"] = lambda r, h: it["KTall"][h // 4][r][h % 4]
            dr["v_src"] = lambda r, q4: it["Vall"][q4 // 2][r][(q4 % 2) * 512:(q4 % 2 + 1) * 512, :]
            if l == 0:
                dr["xin"] = base["xin0"]
                dr["b_xin"] = Buf("b_xin0")
                dr["xout"] = it["X1"]
                dr["b_xout"] = bufs["b_x1"]
                dr["xout_ctx_row0"] = COL_CTX
            else:
                dr["xin"] = it["X1"]
                dr["b_xin"] = bufs["b_x1"]
                dr["xout"] = out
                dr["b_xout"] = bufs["b_out"]
                dr["xout_ctx_row0"] = TL
                dr["hx"] = None
            dr["xin_ctx_row0"] = COL_CTX
            def exchange_kv():
                for ch in range(2):
                    tk.collective("AllGather", PAIRS, it["KTg"][4 * ch:4 * ch + 4].rearrange("h p t -> (h p) t"),
                                  it["KTall"][ch].rearrange("r h p t -> (r h p) t"),
                                  reads=[bufs["b_kv"]], writes=[bufs["b_kvall"]])
                for ch in range(2):
                    tk.collective("AllGather", PAIRS, it["Vg"][ch * (TL // 2):(ch + 1) * (TL // 2), :],
                                  it["Vall"][ch].rearrange("r t c -> (r t) c"),
                                  reads=[bufs["b_kv"]], writes=[bufs["b_kvall"]])
            dr["after_kv"] = exchange_kv
            emit_phase_A(nc, tk, ps, psb, dr, l, not last)
            with ExitStack() as esl:
                Wo = esl.enter_context(nc.sbuf_tensor(uname("Wo"), [P, KC, D], BF16))
                dr["Wo"] = (Wo, load_wout(nc, tk, dr, Wo))
                emit_phase_B(nc, tk, ps, psb, dr, l, not last)
                emit_phase_C(nc, tk, ps, psb, dr, l, not last)
            if not last:
                tk.collective("AllGather", PAIRS, it["hx"], it["hxall"],
                              reads=[bufs["b_hx"]], writes=[bufs["b_hxall"]])
                tk.dma("sp", it["X1"][COL_HL:COL_HL + HALO, :], it["hxall"][HALO:2 * HALO, :],
                       reads=[bufs["b_hxall"]], writes=[bufs["b_x1"]])
                tk.dma("sp", it["X1"][COL_HR:COL_HR + HALO, :], it["hxall"][2 * HALO:3 * HALO, :],
                       reads=[bufs["b_hxall"]], writes=[bufs["b_x1"]])
                tk.barrier()
        tk.barrier()
    return nc


_FUSED = []


def kernel_fused(**inputs):
    inp = {k: np.asarray(v) for k, v in inputs.items()}
    cores = list(range(NCORES))
    lws = [host_layer_weights(l, *[inp[k] for k in WKEYS]) for l in range(DEPTH)]
    in_maps = []
    for c in cores:
        b = c // 2
        cc = host_core_consts(c)
        m = {"xin0": make_xin(c, inp["x"][b], inp["ctx"][b]),
             "cvec": np.ascontiguousarray(
                 np.stack([inp["c"][b], inp["c_ctx"]], -1).reshape(KC, P, 2).transpose(1, 0, 2))}
        for k in ("cosT", "sinT", "pm", "ident", "masks", "poolrc"):
            m[k] = cc[k]
        for l in range(DEPTH):
            for name, _, _ in LAYER_INPUTS:
                if name == "wmodH":
                    m["wmodH%d" % l] = lws[l]["wmod"][6 * (c % 2):6 * (c % 2) + 6]
                else:
                    m["%s%d" % (name, l)] = lws[l][name]
        in_maps.append(m)
    if not _FUSED:
        _FUSED.append(build_fused_program())
    res = run_bass_kernel_spmd(_FUSED[0], in_maps, core_ids=cores).results
    out = np.stack([np.concatenate([np.asarray(res[2 * b]["out"]), np.asarray(res[2 * b + 1]["out"])], 0)
                    for b in range(BATCH)], 0)
    return np.ascontiguousarray(out.astype(np.float32))


def kernel(**inputs):
    return kernel_fused(**inputs)
```

```python
import math
from contextlib import ExitStack

import numpy as np
import ml_dtypes

import concourse.bass as bass
import concourse.mybir as mybir
from concourse.bass_utils import run_bass_kernel_spmd

F32 = mybir.dt.float32
BF16 = mybir.dt.bfloat16
AF = mybir.ActivationFunctionType
ALU = mybir.AluOpType

P = 128
D = 2048
KC = 16
DEPTH = 2
BATCH = 4
SEQ = 4096
NCORES = 8
TL = 2048
HALO = 8
CTXN = 256
NT = TL + 2 * HALO + CTXN
COL_HL = TL
COL_HR = TL + HALO
COL_CTX = TL + 2 * HALO
PADW = 2336
PAD_HL, PAD_LAT, PAD_HR, PAD_Z1, PAD_CTX, PAD_Z2 = 0, 8, 2056, 2064, 2072, 2328
NQ = TL + CTXN
NKEY = CTXN + 2 * TL
NKT = NKEY // P
GRID_W = 64
N_HEADS = 8
POOL_WINDOWS = (2, 4, 8, 16)
ROPE_THETA = 10000.0
LN_EPS = 1e-5
RMS_EPS = 1e-5
ALPHA = (2 * DEPTH) ** 0.25
ATTN_SCALE = 64 ** -0.5
D_IN = 7168
NBLK = 48


def lam_init_of(l):
    return 0.8 - 0.6 * math.exp(-0.3 * l)


class Buf:
    __slots__ = ("name", "writers", "readers", "exclusive", "accum")

    def __init__(self, name="", exclusive=False, accum=False):
        self.name = name
        self.writers = {}
        self.readers = {}
        self.exclusive = exclusive
        self.accum = accum


class EngState:
    def __init__(self, name, eng, sem, semidx):
        self.name = name
        self.eng = eng
        self.sem = sem
        self.semidx = semidx
        self.count = 0
        self.waited = {}


class Tracker:
    def __init__(self, nc, es, n_dma_sems=12):
        self.nc = nc
        self.sems = []
        self.E = {}
        for name, eng in (("pe", nc.tensor), ("act", nc.scalar), ("dve", nc.vector),
                          ("pool", nc.gpsimd), ("sp", nc.sync)):
            sem = es.enter_context(nc.semaphore("c_" + name))
            self.sems.append(sem)
            self.E[name] = EngState(name, eng, sem, len(self.sems) - 1)
        self.dq = {}
        for q in ("sp", "pool"):
            lst = []
            for i in range(n_dma_sems):
                sem = es.enter_context(nc.semaphore("d_%s%d" % (q, i)))
                self.sems.append(sem)
                lst.append([len(self.sems) - 1, 0])
            self.dq[q] = {"sems": lst, "next": 0}
        sem = es.enter_context(nc.semaphore("cc_sem"))
        self.sems.append(sem)
        self.cc_sem = len(self.sems) - 1
        self.cc_count = 0

    def _wait(self, E, deps):
        for si, val in deps.items():
            if si == E.semidx and (E.name == "pe" or val > E.count):
                continue
            if E.waited.get(si, 0) >= val:
                continue
            E.eng.wait_ge(self.sems[si], val)
            E.waited[si] = val

    @staticmethod
    def _deps(reads, writes, own=None):
        deps = {}
        for b in reads:
            for si, v in b.writers.items():
                if deps.get(si, 0) < v:
                    deps[si] = v
            if b.exclusive:
                for si, v in b.readers.items():
                    if si != own and deps.get(si, 0) < v:
                        deps[si] = v
        for b in writes:
            srcs = () if (b.accum and not b.readers) else (b.writers, b.readers)
            for d in srcs:
                for si, v in d.items():
                    if deps.get(si, 0) < v:
                        deps[si] = v
        return deps

    @staticmethod
    def _record(reads, writes, si, val):
        for b in reads:
            if b.readers.get(si, 0) < val:
                b.readers[si] = val
        for b in writes:
            if b.accum and not b.readers:
                if b.writers.get(si, 0) < val:
                    b.writers[si] = val
            else:
                b.writers = {si: val}
                b.readers = {}

    def op(self, engname, fn, reads=(), writes=(), inc=True):
        E = self.E[engname]
        self._wait(E, self._deps(reads, writes, E.semidx))
        inst = fn(E.eng)
        if inc:
            E.count += 1
            inst.then_inc(E.sem, 1)
            val = E.count
        else:
            val = E.count + 1
        self._record(reads, writes, E.semidx, val)
        return inst

    def dma(self, q, out, in_, reads=(), writes=()):
        E = self.E[q]
        dq = self.dq[q]
        slot = dq["sems"][dq["next"]]
        dq["next"] = (dq["next"] + 1) % len(dq["sems"])
        deps = self._deps(reads, writes)
        if slot[1] > 0:
            deps[slot[0]] = max(deps.get(slot[0], 0), slot[1])
        self._wait(E, deps)
        slot[1] += 16
        E.eng.dma_start(out=out, in_=in_).then_inc(self.sems[slot[0]], 16)
        self._record(reads, writes, slot[0], slot[1])

    def collective(self, kind, groups, in_ap, out_ap, reads=(), writes=()):
        E = self.E["pool"]
        if self.cc_sem is None:
            raise RuntimeError("no cc semaphore")
        self._wait(E, self._deps(reads, writes))
        self.cc_count += 1
        E.eng.collective_compute(kind, ALU.bypass, replica_groups=groups,
                                 ins=[in_ap.opt()], outs=[out_ap.opt()]).then_inc(self.sems[self.cc_sem])
        self._record(reads, writes, self.cc_sem, self.cc_count)

    def barrier(self):
        deps = {}
        for E in self.E.values():
            if E.count > 0:
                deps[E.semidx] = E.count
        for dq in self.dq.values():
            for si, v in dq["sems"]:
                if v > 0:
                    deps[si] = v
        if self.cc_count > 0:
            deps[self.cc_sem] = self.cc_count
        for E in self.E.values():
            self._wait(E, deps)


_UNAME = [0]


def uname(name):
    _UNAME[0] += 1
    return "sb_%s_%d" % (name, _UNAME[0])


class Ring:
    def __init__(self, nc, es, name, n, shape, dtype):
        self.items = []
        for i in range(n):
            t = es.enter_context(nc.sbuf_tensor(uname("%s%d" % (name, i)), shape, dtype))
            self.items.append((t, Buf("%s%d" % (name, i))))
        self.i = 0

    def next(self):
        it = self.items[self.i]
        self.i = (self.i + 1) % len(self.items)
        return it


def _tile_w(wcols):
    C = wcols.shape[1]
    return np.ascontiguousarray(wcols.reshape(KC, P, C).transpose(1, 0, 2))


def block_col0():
    cols = []
    for h in range(8):
        cols.append(3072 + h * 128)
    for h in range(8):
        cols.append(2048 + h * 128)
        cols.append(5120 + h * 128)
    for j in range(4):
        cols += [j * 128, 1024 + j * 128, 512 + j * 128, 1536 + j * 128]
    for g in range(4):
        cols += [6144 + g * 128, 6656 + g * 128]
    return cols


def host_layer_weights(l, w_mod, b_mod, w_in, conv_w, lam_q1, lam_k1, lam_q2, lam_k2,
                       subln_g, pool_w, pool_scale, w_out, ln_g, ln_b):
    d = {}
    wm = w_mod[l]
    d["wmod"] = np.ascontiguousarray(
        wm.reshape(KC, P, 12, 512).transpose(2, 1, 0, 3))
    d["bmod"] = np.ascontiguousarray(np.stack([b_mod[l], b_mod[l]], 0))
    wi = w_in[l]
    cols = block_col0()
    d["wA"] = np.ascontiguousarray(
        np.stack([_tile_w(wi[:, c0:c0 + 128]) for c0 in cols], 0))
    d["wV"] = np.ascontiguousarray(
        np.stack([_tile_w(wi[:, 4096 + v * 512:4096 + (v + 1) * 512]) for v in range(2)], 0))
    d["convw"] = np.ascontiguousarray(conv_w[l].reshape(3, 4, P).transpose(2, 1, 0))
    d["poolw"] = np.ascontiguousarray(pool_w[l].transpose(1, 0, 2))
    d["pscale"] = np.ascontiguousarray(pool_scale[l].reshape(4, P).T)
    d["wout"] = _tile_w(w_out[l])
    d["lng"] = np.ascontiguousarray(ln_g[l].reshape(1, D))
    d["lnb"] = np.ascontiguousarray(ln_b[l].reshape(1, D))
    d["lamv"] = np.ascontiguousarray(
        np.stack([lam_q1[l], lam_k1[l], lam_q2[l], lam_k2[l]], 0).reshape(1, 256))
    d["subg"] = np.ascontiguousarray(subln_g[l].reshape(P, 1))
    return d


def host_core_consts(core):
    s = core % 2
    d = {}
    t = s * TL + np.arange(TL)
    rows = (t // GRID_W).astype(np.float64)
    colsp = (t % GRID_W).astype(np.float64)
    inv = ROPE_THETA ** (-np.arange(16, dtype=np.float64) / 16)
    cosT = np.zeros((P, TL), np.float64)
    sinT = np.zeros((P, TL), np.float64)
    for i in range(P):
        dd = i % 64
        pos = rows if dd < 32 else colsp
        e = dd % 32
        f = e % 16
        ang = (pos.astype(np.float32) * np.float32(inv[f])).astype(np.float64)
        cosT[i] = np.cos(ang)
        sinT[i] = np.sin(ang) * (-1.0 if e < 16 else 1.0)
    d["cosT"] = cosT.astype(np.float32)
    d["sinT"] = sinT.astype(np.float32)
    pm = np.zeros((P, P), np.float32)
    for i in range(P):
        e = i % 32
        j = i + 16 if e < 16 else i - 16
        pm[j, i] = 1.0
    d["pm"] = pm
    d["ident"] = np.eye(P, dtype=np.float32)
    d["masks"] = np.tile(np.array([[1.0 if s == 1 else 0.0, 1.0 if s == 0 else 0.0]], np.float32), (P, 1))
    rc = np.zeros((4, 4, 8), np.float32)
    for g, w in enumerate(POOL_WINDOWS):
        def cnt(tt, T):
            lo = np.clip(tt - w // 2, 0, T)
            hi = np.clip(tt + w - w // 2, 0, T)
            return (hi - lo).astype(np.float32)
        rc[g, 0] = 1.0 / cnt(s * TL + np.arange(8), SEQ)
        rc[g, 1] = 1.0 / cnt(s * TL + TL - 8 + np.arange(8), SEQ)
        rc[g, 2] = 1.0 / cnt(np.arange(8), CTXN)
        rc[g, 3] = 1.0 / cnt(CTXN - 8 + np.arange(8), CTXN)
    d["poolrc"] = np.ascontiguousarray(np.tile(rc.reshape(1, 128), (P, 1)))
    return d


def seg_tiles(with_ctx, with_halo):
    tiles = [(512 * t, 512, [(0, 512, "lat", 512 * t)]) for t in range(4)]
    if with_ctx and with_halo:
        tiles.append((TL, 272, [(0, 8, "hl", 0), (8, 8, "hr", 0), (16, 256, "ctx", 0)]))
    elif with_ctx:
        tiles.append((COL_CTX, 256, [(0, 256, "ctx", 0)]))
    elif with_halo:
        tiles.append((TL, 16, [(0, 8, "hl", 0), (8, 8, "hr", 0)]))
    return tiles


def pad_col(region, off):
    return {"lat": PAD_LAT, "hl": PAD_HL, "hr": PAD_HR, "ctx": PAD_CTX}[region] + off


def q_col(region, off):
    return {"lat": 0, "ctx": TL}[region] + off


def emit_phase_A(nc, tk, ps, psb, dr, layer, full_ctx, stop_after=None):
    es = ExitStack()
    with es:
        sb = lambda name, shape, dt: es.enter_context(nc.sbuf_tensor(uname(name), shape, dt))
        hT = sb("hT", [P, KC, NT], BF16)
        hbuf = {}

        def hb(tile_i, k):
            key = (tile_i, k)
            if key not in hbuf:
                hbuf[key] = Buf("hT%d_%d" % key)
            return hbuf[key]

        def hT_bufs(col0, n, k):
            out = []
            c = col0
            while c < col0 + n:
                if c < TL:
                    ti = c // 128
                    nxt = (ti + 1) * 128
                elif c < COL_CTX:
                    ti = 16
                    nxt = COL_CTX
                else:
                    ti = 17 + (c - COL_CTX) // 128
                    nxt = COL_CTX + (ti - 16) * 128
                out.append(hb(ti, k))
                c = nxt
            return out

        ident = sb("ident", [P, P], F32)
        b_ident = Buf("ident")
        modT = sb("modT", [P, 48, 2], F32)
        sc1T = sb("sc1T", [P, 16, 2], F32)
        b_mod = Buf("modT")
        tk.dma("sp", ident[:], dr["ident"][:, :], writes=[b_ident])

        with ExitStack() as es2:
            sb2 = lambda name, shape, dt: es2.enter_context(nc.sbuf_tensor(uname(name), shape, dt))
            bmod = sb2("bmodsb", [2, 6144], F32)
            b_bmod = Buf("bmod")
            modsb = sb2("modsb", [2, 6144], F32)
            b_modsb = Buf("modsb")
            tk.dma("sp", bmod[:], dr["bmod"][:, :], writes=[b_bmod])
            if dr.get("mod_dist"):
                md = dr["mod_dist"]
                csb = sb2("csb", [P, KC, 2], F32)
                b_csb = Buf("csb")
                msl = modsb
                b_msl = b_modsb
                wm_ring = Ring(nc, es2, "wm", 2, [P, KC, 256], F32)
                tk.dma("sp", csb[:], dr["cvec"][:, :, :], writes=[b_csb])
                tk.op("act", lambda e: e.activation(out=csb[:], in_=csb[:], func=AF.Silu),
                      reads=[b_csb], writes=[b_csb])
                for ct in range(12):
                    wm, b_wm = wm_ring.next()
                    tk.dma("sp", wm[:], dr["wmodH"][ct // 2][:, :, (ct % 2) * 256:(ct % 2 + 1) * 256], writes=[b_wm])
                    bank = ct % 2
                    for k in range(KC):
                        tk.op("pe", lambda e, k=k, wm=wm, bank=bank: e.matmul(
                            ps[bank][0:2, 0:256], csb[:, k, :], wm[:, k, :], start=(k == 0), stop=(k == KC - 1)),
                            reads=[b_csb, b_wm], writes=[psb[bank]], inc=(k == KC - 1))
                    tk.op("dve", lambda e, ct=ct, bank=bank: e.tensor_copy(
                        msl[0:2, ct * 256:(ct + 1) * 256], ps[bank][0:2, 0:256]),
                        reads=[psb[bank]], writes=[b_msl])
                tk.dma("sp", md["part"][:, :], msl[0:2, 0:3072], reads=[b_msl], writes=[md["b_part"]])
                tk.collective("AllGather", PAIRS, md["part"], md["all"],
                              reads=[md["b_part"]], writes=[md["b_all"]])
                b_modsb.accum = True
                for r in range(2):
                    tk.dma("sp", modsb[0:2, r * 3072:(r + 1) * 3072], md["all"][2 * r:2 * r + 2, :],
                           reads=[md["b_all"]], writes=[b_modsb])
                tk.op("dve", lambda e: e.tensor_tensor(modsb[:], modsb[:], bmod[:], op=ALU.add),
                      reads=[b_modsb, b_bmod], writes=[b_modsb])
            else:
                csb = sb2("csb", [P, KC, 2], F32)
                b_csb = Buf("csb")
                wm_ring = Ring(nc, es2, "wm", 2, [P, KC, 512], F32)
                tk.dma("sp", csb[:], dr["cvec"][:, :, :], writes=[b_csb])
                tk.op("act", lambda e: e.activation(out=csb[:], in_=csb[:], func=AF.Silu),
                      reads=[b_csb], writes=[b_csb])
                for ct in range(12):
                    wm, b_wm = wm_ring.next()
                    tk.dma("sp", wm[:], dr["wmod"][ct], writes=[b_wm])
                    bank = ct % 2
                    for k in range(KC):
                        tk.op("pe", lambda e, k=k, wm=wm, bank=bank: e.matmul(
                            ps[bank][0:2, :], csb[:, k, :], wm[:, k, :], start=(k == 0), stop=(k == KC - 1)),
                            reads=[b_csb, b_wm], writes=[psb[bank]], inc=(k == KC - 1))
                    tk.op("dve", lambda e, ct=ct, bank=bank: e.tensor_tensor(
                        modsb[0:2, ct * 512:(ct + 1) * 512], ps[bank][0:2, :],
                        bmod[0:2, ct * 512:(ct + 1) * 512], op=ALU.add),
                        reads=[psb[bank], b_bmod], writes=[b_modsb])
            tk.dma("sp", dr["modv"][:, :], modsb[:], reads=[b_modsb], writes=[dr["b_modv"]])
            for i in range(48):
                tk.op("pe", lambda e, i=i: e.transpose(
                    ps[2][:, 2 * i:2 * i + 2], modsb[0:2, i * 128:(i + 1) * 128], ident[0:2, 0:2]),
                    reads=[b_modsb, b_ident], writes=[psb[2]], inc=(i == 47))
            tk.op("dve", lambda e: e.tensor_copy(modT[:].rearrange("p a b -> p (a b)"), ps[2][:, 0:96]),
                  reads=[psb[2]], writes=[b_mod])
            tk.op("dve", lambda e: e.tensor_scalar(
                sc1T[:].rearrange("p a b -> p (a b)"),
                modT[:, 16:32, :].rearrange("p a b -> p (a b)"), 1.0, None, op0=ALU.add),
                reads=[b_mod], writes=[b_mod])
            tk.barrier()
        if stop_after == "M":
            return

        with ExitStack() as es2:
            xs_ring = Ring(nc, es2, "xs", 2, [P, D], F32)
            row_tiles = [(i * 128, 128, i * 128, 0) for i in range(16)]
            row_tiles.append((TL, 16, TL, 0))
            row_tiles += [(TL + 16 + i * 128, 128, COL_CTX + i * 128, 1) for i in range(2)]
            ev = 0
            for ti, (r0, nr, c0, mc) in enumerate(row_tiles):
                xs, b_xs = xs_ring.next()
                tk.dma("sp", xs[0:nr, :], dr["xin"][r0:r0 + nr, :], reads=[dr["b_xin"]], writes=[b_xs])
                for jq in range(4):
                    bank = 4 + (jq % 2)
                    for q in range(4):
                        j = jq * 4 + q
                        tk.op("pe", lambda e, xs=xs, j=j, q=q, nr=nr, bank=bank: e.transpose(
                            ps[bank][:, q * nr:(q + 1) * nr], xs[0:nr, j * 128:(j + 1) * 128],
                            ident[0:nr, 0:nr]),
                            reads=[b_xs, b_ident], writes=[psb[bank]], inc=(q == 3))
                    for q in range(4):
                        j = jq * 4 + q
                        src = ps[bank][:, q * nr:(q + 1) * nr]
                        dst = hT[:, j, c0:c0 + nr]
                        if jq % 2 == 0:
                            tk.op("act", lambda e, src=src, dst=dst, j=j, mc=mc: e.activation(
                                out=dst, in_=src, func=AF.Identity,
                                scale=sc1T[:, j, mc:mc + 1], bias=modT[:, j, mc:mc + 1]),
                                reads=[psb[bank], b_mod], writes=[hb(ti, j)])
                        else:
                            tk.op("dve", lambda e, src=src, dst=dst, j=j, mc=mc: e.tensor_scalar(
                                dst, src, sc1T[:, j, mc:mc + 1], modT[:, j, mc:mc + 1],
                                op0=ALU.mult, op1=ALU.add),
                                reads=[psb[bank], b_mod], writes=[hb(ti, j)])
                        ev += 1
            tk.barrier()
        if stop_after == "A0":
            return

        w_ring = Ring(nc, es, "wr", 4, [P, KC, 128], BF16)
        w_slots = {}

        def load_w(blk):
            if blk >= NBLK or blk in w_slots:
                return
            wt, b_wt = w_ring.next()
            tk.dma("pool", wt[:], dr["wA"][blk], writes=[b_wt])
            w_slots[blk] = (wt, b_wt)

        proj_bank = [0]

        def project(blk, col0, ncols):
            wt, b_wt = w_slots[blk]
            bank = proj_bank[0]
            proj_bank[0] = (bank + 1) % 4
            for k in range(KC):
                tk.op("pe", lambda e, k=k: e.matmul(
                    ps[bank][:, 0:ncols], wt[:, k, :], hT[:, k, col0:col0 + ncols],
                    start=(k == 0), stop=(k == KC - 1)),
                    reads=[b_wt] + hT_bufs(col0, ncols, k), writes=[psb[bank]], inc=(k == KC - 1))
            return bank

        for blk in range(3):
            load_w(blk)

        with ExitStack() as es2:
            sb2 = lambda name, shape, dt: es2.enter_context(nc.sbuf_tensor(uname(name), shape, dt))
            cosT = sb2("cosT", [P, TL], F32)
            sinT = sb2("sinT", [P, TL], F32)
            pm = sb2("pm", [P, P], BF16)
            b_tab = Buf("tabs", accum=True)
            tk.dma("sp", cosT[:], dr["cosT"][:, :], writes=[b_tab])
            tk.dma("sp", sinT[:], dr["sinT"][:, :], writes=[b_tab])
            tk.dma("pool", pm[:], dr["pm"][:, :], writes=[b_tab])
            kb_ring = Ring(nc, es2, "kb", 2, [P, 512], BF16)
            t1_ring = Ring(nc, es2, "t1", 2, [P, 512], F32)
            t2_ring = Ring(nc, es2, "t2", 2, [P, 512], F32)
            ko_ring = Ring(nc, es2, "ko", 3, [P, 512], BF16)
            rope_bank = [0]

            def rope_block(blk, dst_lat, dst_ctx, b_dst, tiles):
                for (c0, n, segs) in tiles:
                    bank = project(blk, c0, n)
                    for (p0, sn, region, off) in segs:
                        if region == "lat":
                            kb, b_kb = kb_ring.next()
                            tk.op("act", lambda e, kb=kb: e.copy(out=kb[:, 0:sn], in_=ps[bank][:, p0:p0 + sn]),
                                  reads=[psb[bank]], writes=[b_kb])
                            rb = 4 + rope_bank[0]
                            rope_bank[0] = (rope_bank[0] + 1) % 2
                            tk.op("pe", lambda e, kb=kb, rb=rb: e.matmul(
                                ps[rb][:, 0:sn], pm[:], kb[:, 0:sn], start=True, stop=True),
                                reads=[b_kb, b_tab], writes=[psb[rb]])
                            t1, b_t1 = t1_ring.next()
                            t2, b_t2 = t2_ring.next()
                            tk.op("dve", lambda e, t1=t1: e.tensor_tensor(
                                t1[:, 0:sn], ps[bank][:, p0:p0 + sn], cosT[:, off:off + sn], op=ALU.mult),
                                reads=[psb[bank], b_tab], writes=[b_t1])
                            tk.op("dve", lambda e, t2=t2, rb=rb: e.tensor_tensor(
                                t2[:, 0:sn], ps[rb][:, 0:sn], sinT[:, off:off + sn], op=ALU.mult),
                                reads=[psb[rb], b_tab], writes=[b_t2])
                            ko, b_ko = ko_ring.next()
                            tk.op("dve", lambda e, t1=t1, t2=t2, ko=ko: e.tensor_tensor(
                                ko[:, 0:sn], t1[:, 0:sn], t2[:, 0:sn], op=ALU.add),
                                reads=[b_t1, b_t2], writes=[b_ko])
                            tk.dma("sp", dst_lat[:, off:off + sn], ko[:, 0:sn], reads=[b_ko], writes=[b_dst])
                        elif region == "ctx":
                            ko, b_ko = ko_ring.next()
                            tk.op("act", lambda e, ko=ko: e.copy(out=ko[:, 0:sn], in_=ps[bank][:, p0:p0 + sn]),
                                  reads=[psb[bank]], writes=[b_ko])
                            tk.dma("sp", dst_ctx[:, off:off + sn], ko[:, 0:sn], reads=[b_ko], writes=[b_dst])

            k_tiles = seg_tiles(True, False)
            for h in range(8):
                load_w(h + 3)
                rope_block(h, dr["KTg"][h], dr["KTc"][h], dr["b_kv"], k_tiles)

            if stop_after == "K":
                tk.barrier()
                return
            with ExitStack() as es3:
                wv_ring = Ring(nc, es3, "wv", 2, [P, KC, 512], BF16)
                vs_ring = Ring(nc, es3, "vs", 3, [P, 512], BF16)
                wvs = []
                for vb in range(2):
                    wv, b_wv = wv_ring.next()
                    tk.dma("pool", wv[:], dr["wV"][vb], writes=[b_wv])
                    wvs.append((wv, b_wv))
                vtiles = [(i * 128, dr["Vg"], i * 128) for i in range(16)]
                vtiles += [(COL_CTX + i * 128, dr["Vc"], i * 128) for i in range(2)]
                ev = 0
                for vb in range(2):
                    wv, b_wv = wvs[vb]
                    for (c0, dst, r0) in vtiles:
                        bank = proj_bank[0]
                        proj_bank[0] = (bank + 1) % 4
                        for k in range(KC):
                            tk.op("pe", lambda e, k=k, c0=c0, bank=bank, wv=wv: e.matmul(
                                ps[bank][:, :], hT[:, k, c0:c0 + 128], wv[:, k, :],
                                start=(k == 0), stop=(k == KC - 1)),
                                reads=[b_wv] + hT_bufs(c0, 128, k), writes=[psb[bank]], inc=(k == KC - 1))
                        vs, b_vs = vs_ring.next()
                        if ev % 2 == 0:
                            tk.op("act", lambda e, vs=vs, bank=bank: e.copy(out=vs[:], in_=ps[bank][:, :]),
                                  reads=[psb[bank]], writes=[b_vs])
                        else:
                            tk.op("dve", lambda e, vs=vs, bank=bank: e.tensor_copy(vs[:], ps[bank][:, :]),
                                  reads=[psb[bank]], writes=[b_vs])
                        ev += 1
                        tk.dma("sp", dst[r0:r0 + 128, vb * 512:(vb + 1) * 512], vs[:],
                               reads=[b_vs], writes=[dr["b_kv"]])
                tk.barrier()
            if dr.get("after_kv") is not None:
                dr["after_kv"]()

            if stop_after == "V":
                return
            q_tiles = seg_tiles(full_ctx, False)
            sg_ring = Ring(nc, es2, "sg", 3, [P, 512], BF16)
            for h in range(8):
                bq = 8 + 2 * h
                load_w(bq + 3)
                rope_block(bq, dr["QT"][h], dr["QT"][h][:, TL:NQ], dr["b_q"], q_tiles)
                load_w(bq + 4)
                for (c0, n, segs) in q_tiles:
                    bank = project(bq + 1, c0, n)
                    for (p0, sn, region, off) in segs:
                        sg, b_sg = sg_ring.next()
                        tk.op("act", lambda e, sg=sg, bank=bank: e.activation(
                            out=sg[:, 0:sn], in_=ps[bank][:, p0:p0 + sn], func=AF.Silu),
                            reads=[psb[bank]], writes=[b_sg])
                        qc = q_col(region, off)
                        tk.dma("sp", dr["SAG"][h][:, qc:qc + sn], sg[:, 0:sn], reads=[b_sg], writes=[dr["b_q"]])
            tk.barrier()

        if stop_after == "Q":
            return
        if dr.get("pre_pool") is not None:
            dr["pre_pool"]()
        with ExitStack() as es2:
            sb2 = lambda name, shape, dt: es2.enter_context(nc.sbuf_tensor(uname(name), shape, dt))
            U = sb2("U", [P, PADW], F32)
            A = sb2("A", [P, PADW], F32)
            B = sb2("B", [P, PADW], F32)
            DT = sb2("DT", [P, PADW], BF16)
            b_U, b_A, b_B, b_DT = Buf("U"), Buf("A"), Buf("B"), Buf("DT")
            convw = sb2("convw", [P, 4, 3], F32)
            masks = sb2("masks", [P, 2], F32)
            poolrc = sb2("poolrc", [P, 128], F32)
            pscale = sb2("pscale", [P, 4], F32)
            poolw = sb2("poolw", [P, 4, 128], BF16)
            b_cst = Buf("cst", accum=True)
            tk.dma("sp", convw[:], dr["convw"][:, :, :], writes=[b_cst])
            tk.dma("sp", masks[:], dr["masks"][:, :], writes=[b_cst])
            tk.dma("sp", poolrc[:], dr["poolrc"][:, :], writes=[b_cst])
            tk.dma("sp", pscale[:], dr["pscale"][:, :], writes=[b_cst])
            tk.dma("pool", poolw[:], dr["poolw"][:, :, :], writes=[b_cst])
            tk.op("dve", lambda e: e.memset(U[:], 0.0), writes=[b_U])
            tk.op("dve", lambda e: e.memset(A[:], 0.0), writes=[b_A])
            tk.op("dve", lambda e: e.memset(B[:], 0.0), writes=[b_B])
            sf_ring = Ring(nc, es2, "sf", 3, [P, 512], F32)
            yo_ring = Ring(nc, es2, "yo", 3, [P, 512], BF16)
            c_tiles = seg_tiles(full_ctx, True)

            def mask_halos(T, b_T):
                tk.op("dve", lambda e: e.tensor_scalar(T[:, PAD_HL:PAD_HL + 8], T[:, PAD_HL:PAD_HL + 8],
                                                       masks[:, 0:1], None, op0=ALU.mult),
                      reads=[b_cst], writes=[b_T])
                tk.op("dve", lambda e: e.tensor_scalar(T[:, PAD_HR:PAD_HR + 8], T[:, PAD_HR:PAD_HR + 8],
                                                       masks[:, 1:2], None, op0=ALU.mult),
                      reads=[b_cst], writes=[b_T])

            def y_out(chunk, bank, p0, sn, region, off, mulsrc, b_mulsrc, scale_ap=None):
                yo, b_yo = yo_ring.next()
                if scale_ap is None:
                    tk.op("dve", lambda e: e.tensor_tensor(
                        yo[:, 0:sn], ps[bank][:, p0:p0 + sn], mulsrc, op=ALU.mult),
                        reads=[psb[bank], b_mulsrc], writes=[b_yo])
                else:
                    tk.op("dve", lambda e: e.scalar_tensor_tensor(
                        yo[:, 0:sn], ps[bank][:, p0:p0 + sn], scale_ap, mulsrc, op0=ALU.mult, op1=ALU.mult),
                        reads=[psb[bank], b_mulsrc, b_cst], writes=[b_yo])
                qc = q_col(region, off)
                tk.dma("sp", dr["YT"][chunk][:, qc:qc + sn], yo[:, 0:sn], reads=[b_yo], writes=[dr["b_y"]])

            for j in range(4):
                b0 = 24 + 4 * j
                load_w(b0 + 3)
                for (c0, n, segs) in c_tiles:
                    bank = project(b0, c0, n)
                    for (p0, sn, region, off) in segs:
                        pc = pad_col(region, off)
                        tk.op("act", lambda e, pc=pc: e.copy(out=U[:, pc:pc + sn], in_=ps[bank][:, p0:p0 + sn]),
                              reads=[psb[bank]], writes=[b_U])
                load_w(b0 + 4)
                for (c0, n, segs) in c_tiles:
                    bank = project(b0 + 1, c0, n)
                    for (p0, sn, region, off) in segs:
                        pc = pad_col(region, off)
                        tk.op("dve", lambda e, pc=pc: e.tensor_tensor(
                            U[:, pc:pc + sn], ps[bank][:, p0:p0 + sn], U[:, pc:pc + sn], op=ALU.mult),
                            reads=[psb[bank]], writes=[b_U])
                mask_halos(U, b_U)
                tk.op("dve", lambda e: e.tensor_scalar(A[:, 1:PADW - 1], U[:, 1:PADW - 1],
                                                       convw[:, j, 1:2], None, op0=ALU.mult),
                      reads=[b_U, b_cst], writes=[b_A])
                tk.op("dve", lambda e: e.scalar_tensor_tensor(
                    A[:, 1:PADW - 1], U[:, 0:PADW - 2], convw[:, j, 0:1], A[:, 1:PADW - 1],
                    op0=ALU.mult, op1=ALU.add), reads=[b_U, b_cst], writes=[b_A])
                tk.op("dve", lambda e: e.scalar_tensor_tensor(
                    A[:, 1:PADW - 1], U[:, 2:PADW], convw[:, j, 2:3], A[:, 1:PADW - 1],
                    op0=ALU.mult, op1=ALU.add), reads=[b_U, b_cst], writes=[b_A])
                load_w(b0 + 5)
                for (c0, n, segs) in c_tiles:
                    bank = project(b0 + 2, c0, n)
                    for (p0, sn, region, off) in segs:
                        if region in ("hl", "hr"):
                            continue
                        pc = pad_col(region, off)
                        tk.op("dve", lambda e, pc=pc: e.tensor_tensor(
                            A[:, pc:pc + sn], ps[bank][:, p0:p0 + sn], A[:, pc:pc + sn], op=ALU.mult),
                            reads=[psb[bank]], writes=[b_A])
                load_w(b0 + 6)
                for (c0, n, segs) in c_tiles:
                    bank = project(b0 + 3, c0, n)
                    for (p0, sn, region, off) in segs:
                        if region in ("hl", "hr"):
                            continue
                        pc = pad_col(region, off)
                        sf, b_sf = sf_ring.next()
                        tk.op("act", lambda e, sf=sf: e.activation(
                            out=sf[:, 0:sn], in_=ps[bank][:, p0:p0 + sn], func=AF.Silu),
                            reads=[psb[bank]], writes=[b_sf])
                        yo, b_yo = yo_ring.next()
                        tk.op("dve", lambda e, sf=sf, yo=yo, pc=pc: e.tensor_tensor(
                            yo[:, 0:sn], sf[:, 0:sn], A[:, pc:pc + sn], op=ALU.mult),
                            reads=[b_sf, b_A], writes=[b_yo])
                        qc = q_col(region, off)
                        tk.dma("sp", dr["YT"][j][:, qc:qc + sn], yo[:, 0:sn], reads=[b_yo], writes=[dr["b_y"]])

            for g in range(4):
                w = POOL_WINDOWS[g]
                b0 = 40 + 2 * g
                load_w(b0 + 3)
                for (c0, n, segs) in c_tiles:
                    bank = project(b0, c0, n)
                    for (p0, sn, region, off) in segs:
                        pc = pad_col(region, off)
                        tk.op("act", lambda e, pc=pc: e.copy(out=U[:, pc:pc + sn], in_=ps[bank][:, p0:p0 + sn]),
                              reads=[psb[bank]], writes=[b_U])
                mask_halos(U, b_U)
                src, b_src = U, b_U
                span = 1
                pp = [(A, b_A), (B, b_B)]
                pi = 0
                while span < w:
                    dst, b_dst = pp[pi]
                    pi ^= 1
                    n = PADW - span
                    tk.op("dve", lambda e, src=src, dst=dst, span=span, n=n: e.tensor_tensor(
                        dst[:, 0:n], src[:, 0:n], src[:, span:span + n], op=ALU.add),
                        reads=[b_src], writes=[b_dst])
                    src, b_src = dst, b_dst
                    span *= 2
                hw = w // 2
                tk.op("dve", lambda e, src=src, hw=hw, w=w: e.scalar_tensor_tensor(
                    DT[:, 8:PAD_Z2], src[:, 8 - hw:PAD_Z2 - hw], 1.0 / w, U[:, 8:PAD_Z2],
                    op0=ALU.mult, op1=ALU.subtract), reads=[b_src, b_U], writes=[b_DT])
                edges = [PAD_LAT, PAD_LAT + TL - 8, PAD_CTX, PAD_CTX + CTXN - 8]
                for ei, ec in enumerate(edges):
                    if ei >= 2 and not full_ctx:
                        continue
                    other, b_other = pp[pi]
                    ro = g * 32 + ei * 8
                    tk.op("dve", lambda e, src=src, other=other, ec=ec, ro=ro, hw=hw: e.tensor_tensor(
                        other[:, ec:ec + 8], src[:, ec - hw:ec - hw + 8], poolrc[:, ro:ro + 8], op=ALU.mult),
                        reads=[b_src, b_cst], writes=[b_other])
                    tk.op("dve", lambda e, other=other, ec=ec: e.tensor_tensor(
                        DT[:, ec:ec + 8], other[:, ec:ec + 8], U[:, ec:ec + 8], op=ALU.subtract),
                        reads=[b_other, b_U], writes=[b_DT])
                load_w(b0 + 4)
                for (c0, n, segs) in c_tiles:
                    segs2 = [sg_ for sg_ in segs if sg_[2] in ("lat", "ctx")]
                    if not segs2:
                        continue
                    bank = project(b0 + 1, c0, n)
                    for (p0, sn, region, off) in segs2:
                        pc = pad_col(region, off)
                        sf, b_sf = sf_ring.next()
                        tk.op("act", lambda e, sf=sf: e.activation(
                            out=sf[:, 0:sn], in_=ps[bank][:, p0:p0 + sn], func=AF.Silu),
                            reads=[psb[bank]], writes=[b_sf])
                        rb = 4 + (proj_bank[0] % 2)
                        tk.op("pe", lambda e, rb=rb, pc=pc: e.matmul(
                            ps[rb][:, 0:sn], poolw[:, g, :], DT[:, pc:pc + sn], start=True, stop=True),
                            reads=[b_DT, b_cst], writes=[psb[rb]])
                        y_out(12 + g, rb, 0, sn, region, off, sf[:, 0:sn], b_sf, scale_ap=pscale[:, g:g + 1])
            tk.barrier()
    tk.barrier()


def declare(nc, name, shape, dt, kind):
    return nc.dram_tensor(name, list(shape), dt, kind=kind).ap()


PHASE_A_INPUTS = [
    ("xin", (NT, D), F32), ("cvec", (P, KC, 2), F32), ("wmod", (12, P, KC, 512), F32),
    ("bmod", (2, 6144), F32), ("wA", (NBLK, P, KC, 128), F32), ("wV", (2, P, KC, 512), F32),
    ("cosT", (P, TL), F32), ("sinT", (P, TL), F32), ("pm", (P, P), F32), ("ident", (P, P), F32),
    ("convw", (P, 4, 3), F32), ("masks", (P, 2), F32), ("poolrc", (P, 128), F32),
    ("poolw", (P, 4, 128), F32), ("pscale", (P, 4), F32),
]
PHASE_A_OUTPUTS = [
    ("KTg", (8, P, TL), BF16), ("Vg", (TL, 1024), BF16), ("KTc", (8, P, CTXN), BF16),
    ("Vc", (CTXN, 1024), BF16), ("QT", (8, P, NQ), BF16), ("SAG", (8, P, NQ), BF16),
    ("YT", (16, P, NQ), BF16), ("modv", (2, 6144), F32),
]


def build_phase_A_program(layer, full_ctx, stop_after=None):
    nc = bass.Bass("TRN2", target_bir_lowering=False)
    dr = {}
    for name, shape, dt in PHASE_A_INPUTS:
        dr[name] = declare(nc, name, shape, dt, "ExternalInput")
    for name, shape, dt in PHASE_A_OUTPUTS:
        dr[name] = declare(nc, name, shape, dt, "ExternalOutput")
    for b in ("b_kv", "b_q", "b_y", "b_modv"):
        dr[b] = Buf(b, accum=True)
    dr["b_xin"] = Buf("b_xin")
    with ExitStack() as es:
        ps = [es.enter_context(nc.psum_tensor("ps%d" % i, [P, 512], F32)) for i in range(8)]
        psb = [Buf("ps%d" % i, exclusive=True) for i in range(8)]
        tk = Tracker(nc, es)
        emit_phase_A(nc, tk, ps, psb, dr, layer, full_ctx, stop_after)
    return nc


def load_v_half(tk, dr, Vt, bufs, half):
    cs = slice(half * 512, (half + 1) * 512)
    tk.dma("sp", Vt[:, 0:2, :], dr["Vc"][:, cs].rearrange("(t p) c -> p t c", p=P),
           reads=[dr["b_kv"]], writes=[bufs[0]])
    for r in range(2):
        for q4 in range(4):
            t0 = 2 + 16 * r + 4 * q4
            tk.dma("sp", Vt[:, t0:t0 + 4, :], dr["v_src"](r, q4)[:, cs].rearrange("(t p) c -> p t c", p=P),
                   reads=[dr["b_kvall"]], writes=[bufs[1 + 4 * r + q4]])


def load_kt(tk, dr, KT, b_KT, h):
    tk.dma("sp", KT[:, 0:CTXN], dr["KTc"][h], reads=[dr["b_kv"]], writes=[b_KT])
    for r in range(2):
        tk.dma("sp", KT[:, CTXN + r * TL:CTXN + (r + 1) * TL], dr["kt_src"](r, h),
               reads=[dr["b_kvall"]], writes=[b_KT])


def emit_phase_B(nc, tk, ps, psb, dr, layer, with_ctx_q):
    lam_init = lam_init_of(layer)
    with ExitStack() as es:
        sb = lambda name, shape, dt: es.enter_context(nc.sbuf_tensor(uname(name), shape, dt))
        pre = dr.get("pre")
        if pre is not None:
            Vh = [pre["Vfa"], sb("Vfb", [P, NKT, 512], BF16)]
            b_vh = [pre["b_vfa"], [Buf("vfb%d" % i) for i in range(9)]]
        else:
            Vh = [sb("Vfa", [P, NKT, 512], BF16), sb("Vfb", [P, NKT, 512], BF16)]
            b_vh = [[Buf("vfa%d" % i) for i in range(9)], [Buf("vfb%d" % i) for i in range(9)]]
        ones_bf = sb("ones_bf", [P, P], BF16)
        ones_f = sb("ones_f", [P, P], F32)
        epsb = sb("epsb", [P, 1], F32)
        lamb = sb("lamb", [P, 256], F32)
        lt = sb("lt", [P, 128], F32)
        sm = sb("sm", [P, 4], F32)
        neg_lam = sb("neg_lam", [P, 1], F32)
        gsub = sb("gsub", [P, 1], F32)
        b_c = Buf("constsB")
        tk.op("dve", lambda e: e.memset(ones_bf[:], 1.0), writes=[b_c])
        tk.op("dve", lambda e: e.memset(ones_f[:], 1.0 / 128.0), writes=[b_c])
        tk.op("dve", lambda e: e.memset(epsb[:], RMS_EPS), writes=[b_c])
        b_lam = Buf("lam")
        tk.dma("sp", lamb[:], dr["lamv"].partition_broadcast(P), writes=[b_lam])
        tk.dma("sp", gsub[:], dr["subg"][:, :], writes=[b_c])
        def vf_buf(kt, h):
            bl = b_vh[h // 4]
            return bl[0] if kt < 2 else bl[1 + (kt - 2) // 4]

        tk.op("dve", lambda e: e.tensor_tensor(lt[:, 0:64], lamb[:, 0:64], lamb[:, 64:128], op=ALU.mult),
              reads=[b_lam], writes=[b_lam])
        tk.op("dve", lambda e: e.tensor_tensor(lt[:, 64:128], lamb[:, 128:192], lamb[:, 192:256], op=ALU.mult),
              reads=[b_lam], writes=[b_lam])
        tk.op("dve", lambda e: e.tensor_reduce(sm[:, 0:1], lt[:, 0:64], axis=mybir.AxisListType.X, op=ALU.add),
              reads=[b_lam], writes=[b_lam])
        tk.op("dve", lambda e: e.tensor_reduce(sm[:, 1:2], lt[:, 64:128], axis=mybir.AxisListType.X, op=ALU.add),
              reads=[b_lam], writes=[b_lam])
        tk.op("act", lambda e: e.activation(out=sm[:, 2:4], in_=sm[:, 0:2], func=AF.Exp),
              reads=[b_lam], writes=[b_lam])
        tk.op("dve", lambda e: e.tensor_tensor(neg_lam[:], sm[:, 3:4], sm[:, 2:3], op=ALU.subtract),
              reads=[b_lam], writes=[b_lam])
        tk.op("dve", lambda e: e.tensor_scalar(neg_lam[:], neg_lam[:], -lam_init, None, op0=ALU.add),
              reads=[b_lam], writes=[b_lam])
        tk.op("dve", lambda e: e.tensor_scalar(gsub[:], gsub[:], 1.0 - lam_init, None, op0=ALU.mult),
              reads=[b_c], writes=[b_c])

        kt_ring = Ring(nc, es, "KT", 2, [P, NKEY], BF16)
        q_ring = [Ring(nc, es, "Qp%d" % m, 2, [P, NQ], BF16) for m in range(2)]
        for m in range(2):
            for (t, b) in q_ring[m].items:
                tk.op("dve", lambda e, t=t: e.memset(t[:], 0.0), writes=[b])
        sag_ring = Ring(nc, es, "sagB", 2, [P, NQ], BF16)
        e_ring = Ring(nc, es, "E", 6, [P, 512], BF16)
        f_ring = Ring(nc, es, "fB", 8, [P, 512], F32)
        yo_ring = Ring(nc, es, "yoB", 2, [P, 512], BF16)

        qtiles = [(512 * t, 512, list(range(NKT))) for t in range(4)]
        if with_ctx_q:
            qtiles.append((TL, CTXN, [0, 1]))

        heads = {}

        def load_head(h):
            if h >= N_HEADS or h in heads:
                return
            KT, b_KT = kt_ring.next()
            b_KT.accum = True
            load_kt(tk, dr, KT, b_KT, h)
            Q0, b_Q0 = q_ring[0].next()
            Q1, b_Q1 = q_ring[1].next()
            tk.dma("sp", Q0[0:64, :], dr["QT"][h][0:64, :], reads=[dr["b_q"]], writes=[b_Q0])
            tk.dma("sp", Q1[64:128, :], dr["QT"][h][64:128, :], reads=[dr["b_q"]], writes=[b_Q1])
            SG, b_SG = sag_ring.next()
            tk.dma("sp", SG[:], dr["SAG"][h], reads=[dr["b_q"]], writes=[b_SG])
            heads[h] = (KT, b_KT, (Q0, Q1), (b_Q0, b_Q1), SG, b_SG)

        load_head(0)
        for half in range(2):
            if pre is not None and half == 0:
                continue
            load_v_half(tk, dr, Vh[half], b_vh[half], half)
        sidx = [0]
        pending_tail = []
        for h in range(N_HEADS):
            need_load = [h + 1]
            if not pending_tail:
                load_head(h + 1)
                need_load = []
            KT, b_KT, Qp, b_Qp, SG, b_SG = heads[h]
            for (q0, nq, kts) in qtiles:
                def s_mm(kt):
                    banks = []
                    for m in range(2):
                        bank = 2 * (sidx[0] % 2) + m
                        tk.op("pe", lambda e, m=m, bank=bank: e.matmul(
                            ps[bank][:, 0:nq], KT[:, kt * P:(kt + 1) * P], Qp[m][:, q0:q0 + nq],
                            start=True, stop=True),
                            reads=[b_KT, b_Qp[m]], writes=[psb[bank]])
                        banks.append(bank)
                    sidx[0] += 1
                    return banks

                pend = s_mm(kts[0])
                for i, kt in enumerate(kts):
                    if pending_tail and (i == 12 or (i == len(kts) - 1 and len(kts) < 13)):
                        pending_tail.pop(0)()
                    if need_load and not pending_tail:
                        load_head(need_load.pop())
                    cur = pend
                    if i + 1 < len(kts):
                        pend = s_mm(kts[i + 1])
                    es_ = []
                    for m in range(2):
                        E, b_E = e_ring.next()
                        tk.op("act", lambda e, E=E, m=m: e.activation(
                            out=E[:, 0:nq], in_=ps[cur[m]][:, 0:nq], func=AF.Exp, scale=ATTN_SCALE),
                            reads=[psb[cur[m]]], writes=[b_E])
                        es_.append((E, b_E))
                    first, last = (i == 0), (i == len(kts) - 1)
                    for m in range(2):
                        E, b_E = es_[m]
                        tk.op("pe", lambda e, E=E, m=m: e.matmul(
                            ps[4 + m][:, 0:nq], Vh[h // 4][:, kt, (h % 4) * P:(h % 4 + 1) * P], E[:, 0:nq],
                            start=first, stop=last),
                            reads=[b_E, vf_buf(kt, h)], writes=[psb[4 + m]], inc=last)
                    for m in range(2):
                        E, b_E = es_[m]
                        tk.op("pe", lambda e, E=E, m=m: e.matmul(
                            ps[6 + m][:, 0:nq], ones_bf[:], E[:, 0:nq], start=first, stop=last),
                            reads=[b_E, b_c], writes=[psb[6 + m]], inc=last)
                oc, zc = [], []
                for m in range(2):
                    o_, b_o = f_ring.next()
                    tk.op("act", lambda e, o_=o_, m=m: e.copy(out=o_[:, 0:nq], in_=ps[4 + m][:, 0:nq]),
                          reads=[psb[4 + m]], writes=[b_o])
                    oc.append((o_, b_o))
                    z_, b_z = f_ring.next()
                    tk.op("dve", lambda e, z_=z_, m=m: e.tensor_copy(z_[:, 0:nq], ps[6 + m][:, 0:nq]),
                          reads=[psb[6 + m]], writes=[b_z])
                    zc.append((z_, b_z))
                for m in range(2):
                    z_, b_z = zc[m]
                    o_, b_o = oc[m]
                    tk.op("dve", lambda e, z_=z_: e.reciprocal(z_[:, 0:nq], z_[:, 0:nq]),
                          reads=[b_z], writes=[b_z])
                    tk.op("pool", lambda e, z_=z_, o_=o_: e.tensor_tensor(
                        o_[:, 0:nq], o_[:, 0:nq], z_[:, 0:nq], op=ALU.mult),
                        reads=[b_z, b_o], writes=[b_o])
                att, b_att = oc[0]
                tk.op("dve", lambda e: e.scalar_tensor_tensor(
                    att[:, 0:nq], oc[1][0][:, 0:nq], neg_lam[:, 0:1], att[:, 0:nq],
                    op0=ALU.mult, op1=ALU.add),
                    reads=[oc[1][1], b_lam], writes=[b_att])
                def make_tail(att=att, b_att=b_att, oc=oc, zc=zc, nq=nq, q0=q0, h=h, SG=SG, b_SG=b_SG):
                    sq, b_sq = oc[1]
                    tk.op("pool", lambda e: e.tensor_tensor(sq[:, 0:nq], att[:, 0:nq], att[:, 0:nq], op=ALU.mult),
                          reads=[b_att], writes=[b_sq])
                    mbank = 2 * (sidx[0] % 2)
                    tk.op("pe", lambda e: e.matmul(ps[mbank][:, 0:nq], ones_f[:], sq[:, 0:nq], start=True, stop=True),
                          reads=[b_sq, b_c], writes=[psb[mbank]])
                    lnv, b_lnv = zc[0]
                    tk.op("act", lambda e: e.activation(out=lnv[:, 0:nq], in_=ps[mbank][:, 0:nq], func=AF.Ln,
                                                        bias=epsb[:, 0:1]),
                          reads=[psb[mbank], b_c], writes=[b_lnv])
                    tk.op("act", lambda e: e.activation(out=lnv[:, 0:nq], in_=lnv[:, 0:nq], func=AF.Exp, scale=-0.5),
                          reads=[b_lnv], writes=[b_lnv])
                    tk.op("dve", lambda e: e.scalar_tensor_tensor(
                        att[:, 0:nq], att[:, 0:nq], gsub[:, 0:1], lnv[:, 0:nq], op0=ALU.mult, op1=ALU.mult),
                        reads=[b_att, b_lnv, b_c], writes=[b_att])
                    yo, b_yo = yo_ring.next()
                    tk.op("pool", lambda e: e.tensor_tensor(yo[:, 0:nq], att[:, 0:nq], SG[:, q0:q0 + nq], op=ALU.mult),
                          reads=[b_att, b_SG], writes=[b_yo])
                    tk.dma("sp", dr["YTat"][h][:, q0:q0 + nq], yo[:, 0:nq], reads=[b_yo], writes=[dr["b_yat"]])
                    return None
                pending_tail.append(make_tail)
        while pending_tail:
            pending_tail.pop(0)()
        tk.barrier()


def load_wout(nc, tk, dr, Wo):
    b_wo = [Buf("wo%d" % i) for i in range(4)]
    for i in range(4):
        tk.dma("pool", Wo[:, 4 * i:4 * i + 4, :], dr["wout"][:, 4 * i:4 * i + 4, :], writes=[b_wo[i]])
    return b_wo


def emit_phase_C(nc, tk, ps, psb, dr, layer, with_ctx):
    with ExitStack() as es:
        sb = lambda name, shape, dt: es.enter_context(nc.sbuf_tensor(uname(name), shape, dt))
        if dr.get("Wo") is not None:
            Wo, b_wo = dr["Wo"]
        else:
            Wo = sb("Wo", [P, KC, D], BF16)
            b_wo = load_wout(nc, tk, dr, Wo)
        gB = [sb("gB0", [P, D], F32)]
        gB.append(gB[0])
        b_g = Buf("gB")
        g_loaded = [0]
        lnG = sb("lnG", [P, D], F32)
        lnB = sb("lnB", [P, D], F32)
        epsb = sb("epsC", [P, 1], F32)
        b_c = Buf("constsC", accum=True)
        tk.op("dve", lambda e: e.memset(epsb[:], LN_EPS), writes=[b_c])
        y_ring = Ring(nc, es, "Ys", 2, [P, KC, 512], BF16)
        x_ring = Ring(nc, es, "xC", 2, [P, D], F32)
        T_ring = Ring(nc, es, "TC", 2, [P, D], F32)
        o_ring = Ring(nc, es, "oC", 2, [P, D], F32)
        st_ring = Ring(nc, es, "stC", 2, [P, 32], F32)

        groups = [(512 * t, 512, 512 * t, 512 * t, 0) for t in range(4)]
        if with_ctx:
            groups.append((TL, CTXN, dr["xin_ctx_row0"], dr["xout_ctx_row0"], 1))
        pbank = [0]
        ys_loaded = {}

        def load_ys(g):
            if g >= len(groups) or g in ys_loaded:
                return
            c0_, ncol_ = groups[g][0], groups[g][1]
            Ys_, b_Ys_ = y_ring.next()
            b_Ys_.accum = True
            for k in range(KC):
                src_ap, src_buf = dr["y_src"](k)
                tk.dma("sp", Ys_[:, k, 0:ncol_], src_ap[:, c0_:c0_ + ncol_], reads=[src_buf], writes=[b_Ys_])
            ys_loaded[g] = (Ys_, b_Ys_)

        tiles = []
        for g, (c0, ncol, xr0, or0, gi) in enumerate(groups):
            for st in range(ncol // P):
                tiles.append((g, st, xr0, or0, gi))
        xs_loaded = {}

        def load_x(i):
            if i >= len(tiles) or i in xs_loaded:
                return
            g_, st_, xr0_, or0_, gi_ = tiles[i]
            xs_, b_xs_ = x_ring.next()
            tk.dma("sp", xs_[:], dr["xin"][xr0_ + st_ * P:xr0_ + (st_ + 1) * P, :], reads=[dr["b_xin"]], writes=[b_xs_])
            xs_loaded[i] = (xs_, b_xs_)

        load_ys(0)
        load_x(0)
        tk.dma("sp", gB[0][:], dr["modv"][0:1, 4096:6144].partition_broadcast(P), reads=[dr["b_modv"]], writes=[b_g])
        tk.dma("sp", lnG[:], dr["lng"].partition_broadcast(P), writes=[b_c])
        tk.dma("sp", lnB[:], dr["lnb"].partition_broadcast(P), writes=[b_c])
        for ti, (g, st, xr0, or0, gi) in enumerate(tiles):
            if True:
                load_x(ti + 1)
                if st == 0:
                    load_ys(g + 1)
                if gi != g_loaded[0]:
                    tk.dma("sp", gB[0][:], dr["modv"][gi:gi + 1, 4096:6144].partition_broadcast(P),
                           reads=[dr["b_modv"]], writes=[b_g])
                    g_loaded[0] = gi
                Ys, b_Ys = ys_loaded[g]
                xs, b_xs = xs_loaded[ti]
                T, b_T = T_ring.next()
                half = pbank[0] % 2
                pbank[0] += 1
                for cg in range(4):
                    bank = 4 * half + cg
                    for k in range(KC):
                        tk.op("pe", lambda e, k=k, bank=bank, cg=cg: e.matmul(
                            ps[bank][:, :], Ys[:, k, st * P:(st + 1) * P], Wo[:, k, cg * 512:(cg + 1) * 512],
                            start=(k == 0), stop=(k == KC - 1)),
                            reads=[b_Ys, b_wo[k // 4]], writes=[psb[bank]], inc=(k == KC - 1))
                    tk.op("dve", lambda e, bank=bank, cg=cg: e.tensor_tensor(
                        T[:, cg * 512:(cg + 1) * 512], ps[bank][:, :], gB[gi][:, cg * 512:(cg + 1) * 512],
                        op=ALU.mult), reads=[psb[bank], b_g], writes=[b_T])
                tk.op("dve", lambda e: e.scalar_tensor_tensor(T[:], xs[:], ALPHA, T[:], op0=ALU.mult, op1=ALU.add),
                      reads=[b_xs, b_T], writes=[b_T])
                stt, b_st = st_ring.next()
                for cg in range(4):
                    tk.op("dve", lambda e, cg=cg: e.bn_stats(stt[:, cg * 6:(cg + 1) * 6], T[:, cg * 512:(cg + 1) * 512]),
                          reads=[b_T], writes=[b_st])
                tk.op("dve", lambda e: e.bn_aggr(stt[:, 24:26], stt[:, 0:24]), reads=[b_st], writes=[b_st])
                tk.op("act", lambda e: e.activation(out=stt[:, 26:27], in_=stt[:, 25:26], func=AF.Ln, bias=epsb[:, 0:1]),
                      reads=[b_st, b_c], writes=[b_st])
                tk.op("act", lambda e: e.activation(out=stt[:, 26:27], in_=stt[:, 26:27], func=AF.Exp, scale=-0.5),
                      reads=[b_st], writes=[b_st])
                tk.op("dve", lambda e: e.scalar_tensor_tensor(
                    stt[:, 27:28], stt[:, 24:25], -1.0, stt[:, 26:27], op0=ALU.mult, op1=ALU.mult),
                    reads=[b_st], writes=[b_st])
                o, b_o = o_ring.next()
                tk.op("act", lambda e: e.activation(out=o[:], in_=T[:], func=AF.Identity,
                                                    scale=stt[:, 26:27], bias=stt[:, 27:28]),
                      reads=[b_T, b_st], writes=[b_o])
                tk.op("pool", lambda e: e.tensor_tensor(o[:], o[:], lnG[:], op=ALU.mult),
                      reads=[b_o, b_c], writes=[b_o])
                tk.op("pool", lambda e: e.tensor_tensor(o[:], o[:], lnB[:], op=ALU.add),
                      reads=[b_o, b_c], writes=[b_o])
                tk.dma("sp", dr["xout"][or0 + st * P:or0 + (st + 1) * P, :], o[:], reads=[b_o], writes=[dr["b_xout"]])
                if dr["hx"] is not None and gi == 0:
                    if or0 + st * P == 0:
                        tk.dma("sp", dr["hx"][0:HALO, :], o[0:HALO, :], reads=[b_o], writes=[dr["b_hx"]])
                    if or0 + (st + 1) * P == TL:
                        tk.dma("sp", dr["hx"][HALO:2 * HALO, :], o[P - HALO:P, :], reads=[b_o], writes=[dr["b_hx"]])
        tk.barrier()


PHASE_BC_INPUTS = [
    ("KTall", (2, 8, P, TL), BF16), ("Vall", (2, TL, 1024), BF16), ("KTc", (8, P, CTXN), BF16),
    ("Vc", (CTXN, 1024), BF16), ("QT", (8, P, NQ), BF16), ("SAG", (8, P, NQ), BF16),
    ("YT", (16, P, NQ), BF16), ("modv", (2, 6144), F32), ("xin", (NT, D), F32),
    ("wout", (P, KC, D), F32), ("lng", (1, D), F32), ("lnb", (1, D), F32),
    ("lamv", (1, 256), F32), ("subg", (P, 1), F32),
]


def build_phase_BC_program(layer, with_ctx, do_B=True, do_C=True):
    nc = bass.Bass("TRN2", target_bir_lowering=False)
    dr = {}
    for name, shape, dt in PHASE_BC_INPUTS:
        dr[name] = declare(nc, name, shape, dt, "ExternalInput")
    dr["xout"] = declare(nc, "xout", (NQ, D), F32, "ExternalOutput")
    dr["YTat"] = declare(nc, "YTat", (8, P, NQ), BF16, "ExternalOutput")
    for b in ("b_kv", "b_kvall", "b_q", "b_y", "b_modv", "b_yat", "b_xout"):
        dr[b] = Buf(b, accum=True)
    dr["b_xin"] = Buf("b_xin")
    dr["xin_ctx_row0"] = COL_CTX
    dr["xout_ctx_row0"] = TL
    dr["hx"] = None

    def y_src(k):
        if 4 <= k < 12:
            return dr["YTat"][k - 4], dr["b_yat"]
        return dr["YT"][k], dr["b_y"]
    dr["y_src"] = y_src
    dr["kt_src"] = lambda r, h: dr["KTall"][r][h]
    dr["v_src"] = lambda r, q4: dr["Vall"][r][q4 * 512:(q4 + 1) * 512, :]
    with ExitStack() as es:
        ps = [es.enter_context(nc.psum_tensor("ps%d" % i, [P, 512], F32)) for i in range(8)]
        psb = [Buf("ps%d" % i, exclusive=True) for i in range(8)]
        tk = Tracker(nc, es)
        if do_B:
            emit_phase_B(nc, tk, ps, psb, dr, layer, with_ctx)
        if do_C:
            emit_phase_C(nc, tk, ps, psb, dr, layer, with_ctx)
        tk.barrier()
    return nc


WKEYS = ["w_mod", "b_mod", "w_in", "conv_w", "lam_q1", "lam_k1", "lam_q2", "lam_k2",
         "subln_g", "pool_w", "pool_scale", "w_out", "ln_g", "ln_b"]


def make_xin(core, xfull_b, ctx_b):
    s = core % 2
    z = np.zeros((HALO, D), np.float32)
    hl = xfull_b[s * TL - HALO:s * TL] if s == 1 else z
    hr = xfull_b[(s + 1) * TL:(s + 1) * TL + HALO] if s == 0 else z
    return np.ascontiguousarray(np.concatenate([xfull_b[s * TL:(s + 1) * TL], hl, hr, ctx_b], 0))


_PROGS = {}


def get_prog(kind, layer, flag):
    key = (kind, layer, flag)
    if key not in _PROGS:
        if kind == "A":
            _PROGS[key] = build_phase_A_program(layer, flag)
        else:
            _PROGS[key] = build_phase_BC_program(layer, flag)
    return _PROGS[key]


def kernel_unfused(**inputs):
    inp = {k: np.asarray(v) for k, v in inputs.items()}
    cores = list(range(NCORES))
    consts = [host_core_consts(c) for c in cores]
    x_cur = [inp["x"][b] for b in range(BATCH)]
    ctx_cur = [inp["ctx"][b] for b in range(BATCH)]
    for l in range(DEPTH):
        last = (l == DEPTH - 1)
        lw = host_layer_weights(l, *[inp[k] for k in WKEYS])
        xins = [make_xin(c, x_cur[c // 2], ctx_cur[c // 2]) for c in cores]
        in_maps = []
        for c in cores:
            b = c // 2
            cvec = np.ascontiguousarray(
                np.stack([inp["c"][b], inp["c_ctx"]], -1).reshape(KC, P, 2).transpose(1, 0, 2))
            m = dict(xin=xins[c], cvec=cvec)
            for k in ("wmod", "bmod", "wA", "wV", "convw", "poolw", "pscale"):
                m[k] = lw[k]
            for k in ("cosT", "sinT", "pm", "ident", "masks", "poolrc"):
                m[k] = consts[c][k]
            in_maps.append(m)
        resA = run_bass_kernel_spmd(get_prog("A", l, not last), in_maps, core_ids=cores).results
        in_maps = []
        for c in cores:
            pr = (c // 2) * 2
            r = resA[c]
            m = dict(KTall=np.stack([np.asarray(resA[pr]["KTg"]), np.asarray(resA[pr + 1]["KTg"])], 0),
                     Vall=np.stack([np.asarray(resA[pr]["Vg"]), np.asarray(resA[pr + 1]["Vg"])], 0),
                     KTc=np.asarray(r["KTc"]), Vc=np.asarray(r["Vc"]), QT=np.asarray(r["QT"]),
                     SAG=np.asarray(r["SAG"]), YT=np.asarray(r["YT"]), modv=np.asarray(r["modv"]),
                     xin=xins[c])
            for k in ("wout", "lng", "lnb", "lamv", "subg"):
                m[k] = lw[k]
            in_maps.append(m)
        resBC = run_bass_kernel_spmd(get_prog("BC", l, not last), in_maps, core_ids=cores).results
        x_cur = [np.concatenate([np.asarray(resBC[2 * b]["xout"])[:TL], np.asarray(resBC[2 * b + 1]["xout"])[:TL]], 0)
                 for b in range(BATCH)]
        if not last:
            ctx_cur = [np.asarray(resBC[2 * b]["xout"])[TL:NQ] for b in range(BATCH)]
    return np.ascontiguousarray(np.stack(x_cur, 0).astype(np.float32))


PAIRS = [[0, 1], [2, 3], [4, 5], [6, 7]]
LAYER_INPUTS = [
    ("wmodH", (6, P, KC, 512), F32), ("bmod", (2, 6144), F32), ("wA", (NBLK, P, KC, 128), F32),
    ("wV", (2, P, KC, 512), F32), ("convw", (P, 4, 3), F32), ("poolw", (P, 4, 128), F32),
    ("pscale", (P, 4), F32), ("wout", (P, KC, D), F32), ("lng", (1, D), F32), ("lnb", (1, D), F32),
    ("lamv", (1, 256), F32), ("subg", (P, 1), F32),
]
CORE_INPUTS = [
    ("xin0", (NT, D), F32), ("cvec", (P, KC, 2), F32), ("cosT", (P, TL), F32), ("sinT", (P, TL), F32),
    ("pm", (P, P), F32), ("ident", (P, P), F32), ("masks", (P, 2), F32), ("poolrc", (P, 128), F32),
]


def build_fused_program():
    nc = bass.Bass("TRN2", target_bir_lowering=False)
    base = {}
    for name, shape, dt in CORE_INPUTS:
        base[name] = declare(nc, name, shape, dt, "ExternalInput")
    lay = []
    for l in range(DEPTH):
        d = {}
        for name, shape, dt in LAYER_INPUTS:
            d[name] = declare(nc, "%s%d" % (name, l), shape, dt, "ExternalInput")
        lay.append(d)
    out = declare(nc, "out", (TL, D), F32, "ExternalOutput")
    internal = lambda name, shape, dt: nc.dram_tensor(name, list(shape), dt).ap()
    it = {
        "KTg": internal("KTg", (8, P, TL), BF16), "Vg": internal("Vg", (TL, 1024), BF16),
        "KTall": internal("KTall", (2, 2, 4, P, TL), BF16), "Vall": internal("Vall", (2, 2, TL // 2, 1024), BF16),
        "KTc": internal("KTc", (8, P, CTXN), BF16), "Vc": internal("Vc", (CTXN, 1024), BF16),
        "QT": internal("QT", (8, P, NQ), BF16), "SAG": internal("SAG", (8, P, NQ), BF16),
        "YT": internal("YT", (16, P, NQ), BF16), "modv": internal("modv", (2, 6144), F32),
        "X1": internal("X1", (NT, D), F32), "hx": internal("hx", (2 * HALO, D), F32),
        "hxall": internal("hxall", (4 * HALO, D), F32),
    }
    mod_dist = {"part": internal("modpart", (2, 3072), F32), "all": internal("modall", (4, 3072), F32),
                "b_part": Buf("b_modpart"), "b_all": Buf("b_modall")}
    bufs = {b: Buf(b, accum=True) for b in ("b_kv", "b_kvall", "b_q", "b_y", "b_modv", "b_yat", "b_x1", "b_hx",
                                             "b_hxall", "b_out")}
    with ExitStack() as es:
        ps = [es.enter_context(nc.psum_tensor("ps%d" % i, [P, 512], F32)) for i in range(8)]
        psb = [Buf("ps%d" % i, exclusive=True) for i in range(8)]
        tk = Tracker(nc, es)
        for l in range(DEPTH):
            last = (l == DEPTH - 1)
            dr = dict(base)
            dr.update(lay[l])
            dr.update(it)
            dr.update(bufs)
            dr["b_yat"] = bufs["b_y"]
            dr["mod_dist"] = mod_dist
            dr["YTat"] = it["YT"][4:12]
            dr["y_src"] = lambda k: (it["YT"][k], bufs["b_y"])
            dr["kt_src"] = lambda r, h: it["KTall"][h // 4][r][h % 4]
            dr["v_src"] = lambda r, q4: it["Vall"][q4 // 2][r][(q4 % 2) * 512:(q4 % 2 + 1) * 512, :]
            if l == 0:
                dr["xin"] = base["xin0"]
                dr["b_xin"] = Buf("b_xin0")
                dr["xout"] = it["X1"]
                dr["b_xout"] = bufs["b_x1"]
                dr["xout_ctx_row0"] = COL_CTX
            else:
                dr["xin"] = it["X1"]
                dr["b_xin"] = bufs["b_x1"]
                dr["xout"] = out
                dr["b_xout"] = bufs["b_out"]
                dr["xout_ctx_row0"] = TL
                dr["hx"] = None
            dr["xin_ctx_row0"] = COL_CTX
            def exchange_kv():
                for ch in range(2):
                    tk.collective("AllGather", PAIRS, it["KTg"][4 * ch:4 * ch + 4].rearrange("h p t -> (h p) t"),
                                  it["KTall"][ch].rearrange("r h p t -> (r h p) t"),
                                  reads=[bufs["b_kv"]], writes=[bufs["b_kvall"]])
                for ch in range(2):
                    tk.collective("AllGather", PAIRS, it["Vg"][ch * (TL // 2):(ch + 1) * (TL // 2), :],
                                  it["Vall"][ch].rearrange("r t c -> (r t) c"),
                                  reads=[bufs["b_kv"]], writes=[bufs["b_kvall"]])
            dr["after_kv"] = exchange_kv
            with ExitStack() as esl0:
                Vfa = esl0.enter_context(nc.sbuf_tensor(uname("Vfa"), [P, NKT, 512], BF16))
                pre = {"Vfa": Vfa, "b_vfa": [Buf("vfa%d" % i) for i in range(9)]}
                dr["pre"] = pre

                def pre_pool(dr=dr, pre=pre):
                    load_v_half(tk, dr, pre["Vfa"], pre["b_vfa"], 0)
                dr["pre_pool"] = pre_pool
                emit_phase_A(nc, tk, ps, psb, dr, l, not last)
                with ExitStack() as esl:
                    Wo = esl.enter_context(nc.sbuf_tensor(uname("Wo"), [P, KC, D], BF16))
                    dr["Wo"] = (Wo, load_wout(nc, tk, dr, Wo))
                    emit_phase_B(nc, tk, ps, psb, dr, l, not last)
                    emit_phase_C(nc, tk, ps, psb, dr, l, not last)
            if not last:
                tk.collective("AllGather", PAIRS, it["hx"], it["hxall"],
                              reads=[bufs["b_hx"]], writes=[bufs["b_hxall"]])
                tk.dma("sp", it["X1"][COL_HL:COL_HL + HALO, :], it["hxall"][HALO:2 * HALO, :],
                       reads=[bufs["b_hxall"]], writes=[bufs["b_x1"]])
                tk.dma("sp", it["X1"][COL_HR:COL_HR + HALO, :], it["hxall"][2 * HALO:3 * HALO, :],
                       reads=[bufs["b_hxall"]], writes=[bufs["b_x1"]])
                tk.barrier()
        tk.barrier()
    return nc


_FUSED = []


def kernel_fused(**inputs):
    inp = {k: np.asarray(v) for k, v in inputs.items()}
    cores = list(range(NCORES))
    lws = [host_layer_weights(l, *[inp[k] for k in WKEYS]) for l in range(DEPTH)]
    in_maps = []
    for c in cores:
        b = c // 2
        cc = host_core_consts(c)
        m = {"xin0": make_xin(c, inp["x"][b], inp["ctx"][b]),
             "cvec": np.ascontiguousarray(
                 np.stack([inp["c"][b], inp["c_ctx"]], -1).reshape(KC, P, 2).transpose(1, 0, 2))}
        for k in ("cosT", "sinT", "pm", "ident", "masks", "poolrc"):
            m[k] = cc[k]
        for l in range(DEPTH):
            for name, _, _ in LAYER_INPUTS:
                if name == "wmodH":
                    m["wmodH%d" % l] = lws[l]["wmod"][6 * (c % 2):6 * (c % 2) + 6]
                else:
                    m["%s%d" % (name, l)] = lws[l][name]
        in_maps.append(m)
    if not _FUSED:
        _FUSED.append(build_fused_program())
    res = run_bass_kernel_spmd(_FUSED[0], in_maps, core_ids=cores).results
    out = np.stack([np.concatenate([np.asarray(res[2 * b]["out"]), np.asarray(res[2 * b + 1]["out"])], 0)
                    for b in range(BATCH)], 0)
    return np.ascontiguousarray(out.astype(np.float32))


def kernel(**inputs):
    return kernel_fused(**inputs)
```

```python
import math
from contextlib import ExitStack

import numpy as np
import ml_dtypes

import concourse.bass as bass
import concourse.mybir as mybir
from concourse.bass_utils import run_bass_kernel_spmd

F32 = mybir.dt.float32
BF16 = mybir.dt.bfloat16
AF = mybir.ActivationFunctionType
ALU = mybir.AluOpType

P = 128
D = 2048
KC = 16
DEPTH = 2
BATCH = 4
SEQ = 4096
NCORES = 8
TL = 2048
HALO = 8
CTXN = 256
NT = TL + 2 * HALO + CTXN
COL_HL = TL
COL_HR = TL + HALO
COL_CTX = TL + 2 * HALO
PADW = 2336
PAD_HL, PAD_LAT, PAD_HR, PAD_Z1, PAD_CTX, PAD_Z2 = 0, 8, 2056, 2064, 2072, 2328
NQ = TL + CTXN
NKEY = CTXN + 2 * TL
NKT = NKEY // P
GRID_W = 64
N_HEADS = 8
POOL_WINDOWS = (2, 4, 8, 16)
ROPE_THETA = 10000.0
LN_EPS = 1e-5
RMS_EPS = 1e-5
ALPHA = (2 * DEPTH) ** 0.25
ATTN_SCALE = 64 ** -0.5
D_IN = 7168
NBLK = 48


def lam_init_of(l):
    return 0.8 - 0.6 * math.exp(-0.3 * l)


class Buf:
    __slots__ = ("name", "writers", "readers", "exclusive", "accum")

    def __init__(self, name="", exclusive=False, accum=False):
        self.name = name
        self.writers = {}
        self.readers = {}
        self.exclusive = exclusive
        self.accum = accum


class EngState:
    def __init__(self, name, eng, sem, semidx):
        self.name = name
        self.eng = eng
        self.sem = sem
        self.semidx = semidx
        self.count = 0
        self.waited = {}


class Tracker:
    def __init__(self, nc, es, n_dma_sems=12):
        self.nc = nc
        self.sems = []
        self.E = {}
        for name, eng in (("pe", nc.tensor), ("act", nc.scalar), ("dve", nc.vector),
                          ("pool", nc.gpsimd), ("sp", nc.sync)):
            sem = es.enter_context(nc.semaphore("c_" + name))
            self.sems.append(sem)
            self.E[name] = EngState(name, eng, sem, len(self.sems) - 1)
        self.dq = {}
        for q in ("sp", "pool"):
            lst = []
            for i in range(n_dma_sems):
                sem = es.enter_context(nc.semaphore("d_%s%d" % (q, i)))
                self.sems.append(sem)
                lst.append([len(self.sems) - 1, 0])
            self.dq[q] = {"sems": lst, "next": 0}
        sem = es.enter_context(nc.semaphore("cc_sem"))
        self.sems.append(sem)
        self.cc_sem = len(self.sems) - 1
        self.cc_count = 0

    def _wait(self, E, deps):
        for si, val in deps.items():
            if si == E.semidx and (E.name == "pe" or val > E.count):
                continue
            if E.waited.get(si, 0) >= val:
                continue
            E.eng.wait_ge(self.sems[si], val)
            E.waited[si] = val

    @staticmethod
    def _deps(reads, writes, own=None):
        deps = {}
        for b in reads:
            for si, v in b.writers.items():
                if deps.get(si, 0) < v:
                    deps[si] = v
            if b.exclusive:
                for si, v in b.readers.items():
                    if si != own and deps.get(si, 0) < v:
                        deps[si] = v
        for b in writes:
            srcs = () if (b.accum and not b.readers) else (b.writers, b.readers)
            for d in srcs:
                for si, v in d.items():
                    if deps.get(si, 0) < v:
                        deps[si] = v
        return deps

    @staticmethod
    def _record(reads, writes, si, val):
        for b in reads:
            if b.readers.get(si, 0) < val:
                b.readers[si] = val
        for b in writes:
            if b.accum and not b.readers:
                if b.writers.get(si, 0) < val:
                    b.writers[si] = val
            else:
                b.writers = {si: val}
                b.readers = {}

    def op(self, engname, fn, reads=(), writes=(), inc=True):
        E = self.E[engname]
        self._wait(E, self._deps(reads, writes, E.semidx))
        inst = fn(E.eng)
        if inc:
            E.count += 1
            inst.then_inc(E.sem, 1)
            val = E.count
        else:
            val = E.count + 1
        self._record(reads, writes, E.semidx, val)
        return inst

    def dma(self, q, out, in_, reads=(), writes=()):
        E = self.E[q]
        dq = self.dq[q]
        slot = dq["sems"][dq["next"]]
        dq["next"] = (dq["next"] + 1) % len(dq["sems"])
        deps = self._deps(reads, writes)
        if slot[1] > 0:
            deps[slot[0]] = max(deps.get(slot[0], 0), slot[1])
        self._wait(E, deps)
        slot[1] += 16
        E.eng.dma_start(out=out, in_=in_).then_inc(self.sems[slot[0]], 16)
        self._record(reads, writes, slot[0], slot[1])

    def collective(self, kind, groups, in_ap, out_ap, reads=(), writes=()):
        E = self.E["pool"]
        if self.cc_sem is None:
            raise RuntimeError("no cc semaphore")
        self._wait(E, self._deps(reads, writes))
        self.cc_count += 1
        E.eng.collective_compute(kind, ALU.bypass, replica_groups=groups,
                                 ins=[in_ap.opt()], outs=[out_ap.opt()]).then_inc(self.sems[self.cc_sem])
        self._record(reads, writes, self.cc_sem, self.cc_count)

    def barrier(self):
        deps = {}
        for E in self.E.values():
            if E.count > 0:
                deps[E.semidx] = E.count
        for dq in self.dq.values():
            for si, v in dq["sems"]:
                if v > 0:
                    deps[si] = v
        if self.cc_count > 0:
            deps[self.cc_sem] = self.cc_count
        for E in self.E.values():
            self._wait(E, deps)


_UNAME = [0]


def uname(name):
    _UNAME[0] += 1
    return "sb_%s_%d" % (name, _UNAME[0])


class Ring:
    def __init__(self, nc, es, name, n, shape, dtype):
        self.items = []
        for i in range(n):
            t = es.enter_context(nc.sbuf_tensor(uname("%s%d" % (name, i)), shape, dtype))
            self.items.append((t, Buf("%s%d" % (name, i))))
        self.i = 0

    def next(self):
        it = self.items[self.i]
        self.i = (self.i + 1) % len(self.items)
        return it


def _tile_w(wcols):
    C = wcols.shape[1]
    return np.ascontiguousarray(wcols.reshape(KC, P, C).transpose(1, 0, 2))


def block_col0():
    cols = []
    for h in range(8):
        cols.append(3072 + h * 128)
    for h in range(8):
        cols.append(2048 + h * 128)
        cols.append(5120 + h * 128)
    for j in range(4):
        cols += [j * 128, 1024 + j * 128, 512 + j * 128, 1536 + j * 128]
    for g in range(4):
        cols += [6144 + g * 128, 6656 + g * 128]
    return cols


def host_layer_weights(l, w_mod, b_mod, w_in, conv_w, lam_q1, lam_k1, lam_q2, lam_k2,
                       subln_g, pool_w, pool_scale, w_out, ln_g, ln_b):
    d = {}
    wm = w_mod[l]
    d["wmod"] = np.ascontiguousarray(
        wm.reshape(KC, P, 12, 512).transpose(2, 1, 0, 3))
    d["bmod"] = np.ascontiguousarray(np.stack([b_mod[l], b_mod[l]], 0))
    wi = w_in[l]
    cols = block_col0()
    d["wA"] = np.ascontiguousarray(
        np.stack([_tile_w(wi[:, c0:c0 + 128]) for c0 in cols], 0))
    d["wV"] = np.ascontiguousarray(
        np.stack([_tile_w(wi[:, 4096 + v * 512:4096 + (v + 1) * 512]) for v in range(2)], 0))
    d["convw"] = np.ascontiguousarray(conv_w[l].reshape(3, 4, P).transpose(2, 1, 0))
    d["poolw"] = np.ascontiguousarray(pool_w[l].transpose(1, 0, 2))
    d["pscale"] = np.ascontiguousarray(pool_scale[l].reshape(4, P).T)
    d["wout"] = _tile_w(w_out[l])
    d["lng"] = np.ascontiguousarray(ln_g[l].reshape(1, D))
    d["lnb"] = np.ascontiguousarray(ln_b[l].reshape(1, D))
    d["lamv"] = np.ascontiguousarray(
        np.stack([lam_q1[l], lam_k1[l], lam_q2[l], lam_k2[l]], 0).reshape(1, 256))
    d["subg"] = np.ascontiguousarray(subln_g[l].reshape(P, 1))
    return d


def host_core_consts(core):
    s = core % 2
    d = {}
    t = s * TL + np.arange(TL)
    rows = (t // GRID_W).astype(np.float64)
    colsp = (t % GRID_W).astype(np.float64)
    inv = ROPE_THETA ** (-np.arange(16, dtype=np.float64) / 16)
    cosT = np.zeros((P, TL), np.float64)
    sinT = np.zeros((P, TL), np.float64)
    for i in range(P):
        dd = i % 64
        pos = rows if dd < 32 else colsp
        e = dd % 32
        f = e % 16
        ang = (pos.astype(np.float32) * np.float32(inv[f])).astype(np.float64)
        cosT[i] = np.cos(ang)
        sinT[i] = np.sin(ang) * (-1.0 if e < 16 else 1.0)
    d["cosT"] = cosT.astype(np.float32)
    d["sinT"] = sinT.astype(np.float32)
    pm = np.zeros((P, P), np.float32)
    for i in range(P):
        e = i % 32
        j = i + 16 if e < 16 else i - 16
        pm[j, i] = 1.0
    d["pm"] = pm
    d["ident"] = np.eye(P, dtype=np.float32)
    d["masks"] = np.tile(np.array([[1.0 if s == 1 else 0.0, 1.0 if s == 0 else 0.0]], np.float32), (P, 1))
    rc = np.zeros((4, 4, 8), np.float32)
    for g, w in enumerate(POOL_WINDOWS):
        def cnt(tt, T):
            lo = np.clip(tt - w // 2, 0, T)
            hi = np.clip(tt + w - w // 2, 0, T)
            return (hi - lo).astype(np.float32)
        rc[g, 0] = 1.0 / cnt(s * TL + np.arange(8), SEQ)
        rc[g, 1] = 1.0 / cnt(s * TL + TL - 8 + np.arange(8), SEQ)
        rc[g, 2] = 1.0 / cnt(np.arange(8), CTXN)
        rc[g, 3] = 1.0 / cnt(CTXN - 8 + np.arange(8), CTXN)
    d["poolrc"] = np.ascontiguousarray(np.tile(rc.reshape(1, 128), (P, 1)))
    return d


def seg_tiles(with_ctx, with_halo):
    tiles = [(512 * t, 512, [(0, 512, "lat", 512 * t)]) for t in range(4)]
    if with_ctx and with_halo:
        tiles.append((TL, 272, [(0, 8, "hl", 0), (8, 8, "hr", 0), (16, 256, "ctx", 0)]))
    elif with_ctx:
        tiles.append((COL_CTX, 256, [(0, 256, "ctx", 0)]))
    elif with_halo:
        tiles.append((TL, 16, [(0, 8, "hl", 0), (8, 8, "hr", 0)]))
    return tiles


def pad_col(region, off):
    return {"lat": PAD_LAT, "hl": PAD_HL, "hr": PAD_HR, "ctx": PAD_CTX}[region] + off


def q_col(region, off):
    return {"lat": 0, "ctx": TL}[region] + off


def emit_phase_A(nc, tk, ps, psb, dr, layer, full_ctx, stop_after=None):
    es = ExitStack()
    with es:
        sb = lambda name, shape, dt: es.enter_context(nc.sbuf_tensor(uname(name), shape, dt))
        hT = sb("hT", [P, KC, NT], BF16)
        hbuf = {}

        def hb(tile_i, k):
            key = (tile_i, k)
            if key not in hbuf:
                hbuf[key] = Buf("hT%d_%d" % key)
            return hbuf[key]

        def hT_bufs(col0, n, k):
            out = []
            c = col0
            while c < col0 + n:
                if c < TL:
                    ti = c // 128
                    nxt = (ti + 1) * 128
                elif c < COL_CTX:
                    ti = 16
                    nxt = COL_CTX
                else:
                    ti = 17 + (c - COL_CTX) // 128
                    nxt = COL_CTX + (ti - 16) * 128
                out.append(hb(ti, k))
                c = nxt
            return out

        ident = sb("ident", [P, P], F32)
        b_ident = Buf("ident")
        modT = sb("modT", [P, 48, 2], F32)
        sc1T = sb("sc1T", [P, 16, 2], F32)
        b_mod = Buf("modT")
        tk.dma("sp", ident[:], dr["ident"][:, :], writes=[b_ident])

        with ExitStack() as es2:
            sb2 = lambda name, shape, dt: es2.enter_context(nc.sbuf_tensor(uname(name), shape, dt))
            bmod = sb2("bmodsb", [2, 6144], F32)
            b_bmod = Buf("bmod")
            modsb = sb2("modsb", [2, 6144], F32)
            b_modsb = Buf("modsb")
            tk.dma("sp", bmod[:], dr["bmod"][:, :], writes=[b_bmod])
            if dr.get("mod_dist"):
                md = dr["mod_dist"]
                csb = sb2("csb", [P, KC, 2], F32)
                b_csb = Buf("csb")
                msl = sb2("msl", [2, 3072], F32)
                b_msl = Buf("msl")
                wm_ring = Ring(nc, es2, "wm", 2, [P, KC, 512], F32)
                tk.dma("sp", csb[:], dr["cvec"][:, :, :], writes=[b_csb])
                tk.op("act", lambda e: e.activation(out=csb[:], in_=csb[:], func=AF.Silu),
                      reads=[b_csb], writes=[b_csb])
                for ct in range(6):
                    wm, b_wm = wm_ring.next()
                    tk.dma("sp", wm[:], dr["wmodH"][ct], writes=[b_wm])
                    bank = ct % 2
                    for k in range(KC):
                        tk.op("pe", lambda e, k=k, wm=wm, bank=bank: e.matmul(
                            ps[bank][0:2, :], csb[:, k, :], wm[:, k, :], start=(k == 0), stop=(k == KC - 1)),
                            reads=[b_csb, b_wm], writes=[psb[bank]], inc=(k == KC - 1))
                    tk.op("dve", lambda e, ct=ct, bank=bank: e.tensor_copy(
                        msl[0:2, ct * 512:(ct + 1) * 512], ps[bank][0:2, :]),
                        reads=[psb[bank]], writes=[b_msl])
                tk.dma("sp", md["part"][:, :], msl[:], reads=[b_msl], writes=[md["b_part"]])
                tk.collective("AllGather", PAIRS, md["part"], md["all"],
                              reads=[md["b_part"]], writes=[md["b_all"]])
                b_modsb.accum = True
                for r in range(2):
                    tk.dma("sp", modsb[0:2, r * 3072:(r + 1) * 3072], md["all"][2 * r:2 * r + 2, :],
                           reads=[md["b_all"]], writes=[b_modsb])
                tk.op("dve", lambda e: e.tensor_tensor(modsb[:], modsb[:], bmod[:], op=ALU.add),
                      reads=[b_modsb, b_bmod], writes=[b_modsb])
            else:
                csb = sb2("csb", [P, KC, 2], F32)
                b_csb = Buf("csb")
                wm_ring = Ring(nc, es2, "wm", 2, [P, KC, 512], F32)
                tk.dma("sp", csb[:], dr["cvec"][:, :, :], writes=[b_csb])
                tk.op("act", lambda e: e.activation(out=csb[:], in_=csb[:], func=AF.Silu),
                      reads=[b_csb], writes=[b_csb])
                for ct in range(12):
                    wm, b_wm = wm_ring.next()
                    tk.dma("sp", wm[:], dr["wmod"][ct], writes=[b_wm])
                    bank = ct % 2
                    for k in range(KC):
                        tk.op("pe", lambda e, k=k, wm=wm, bank=bank: e.matmul(
                            ps[bank][0:2, :], csb[:, k, :], wm[:, k, :], start=(k == 0), stop=(k == KC - 1)),
                            reads=[b_csb, b_wm], writes=[psb[bank]], inc=(k == KC - 1))
                    tk.op("dve", lambda e, ct=ct, bank=bank: e.tensor_tensor(
                        modsb[0:2, ct * 512:(ct + 1) * 512], ps[bank][0:2, :],
                        bmod[0:2, ct * 512:(ct + 1) * 512], op=ALU.add),
                        reads=[psb[bank], b_bmod], writes=[b_modsb])
            tk.dma("sp", dr["modv"][:, :], modsb[:], reads=[b_modsb], writes=[dr["b_modv"]])
            for i in range(48):
                tk.op("pe", lambda e, i=i: e.transpose(
                    ps[2][:, 2 * i:2 * i + 2], modsb[0:2, i * 128:(i + 1) * 128], ident[0:2, 0:2]),
                    reads=[b_modsb, b_ident], writes=[psb[2]], inc=(i == 47))
            tk.op("dve", lambda e: e.tensor_copy(modT[:].rearrange("p a b -> p (a b)"), ps[2][:, 0:96]),
                  reads=[psb[2]], writes=[b_mod])
            tk.op("dve", lambda e: e.tensor_scalar(
                sc1T[:].rearrange("p a b -> p (a b)"),
                modT[:, 16:32, :].rearrange("p a b -> p (a b)"), 1.0, None, op0=ALU.add),
                reads=[b_mod], writes=[b_mod])
            tk.barrier()
        if stop_after == "M":
            return

        with ExitStack() as es2:
            xs_ring = Ring(nc, es2, "xs", 2, [P, D], F32)
            row_tiles = [(i * 128, 128, i * 128, 0) for i in range(16)]
            row_tiles.append((TL, 16, TL, 0))
            row_tiles += [(TL + 16 + i * 128, 128, COL_CTX + i * 128, 1) for i in range(2)]
            ev = 0
            for ti, (r0, nr, c0, mc) in enumerate(row_tiles):
                xs, b_xs = xs_ring.next()
                tk.dma("sp", xs[0:nr, :], dr["xin"][r0:r0 + nr, :], reads=[dr["b_xin"]], writes=[b_xs])
                for jq in range(4):
                    bank = 4 + (jq % 2)
                    for q in range(4):
                        j = jq * 4 + q
                        tk.op("pe", lambda e, xs=xs, j=j, q=q, nr=nr, bank=bank: e.transpose(
                            ps[bank][:, q * nr:(q + 1) * nr], xs[0:nr, j * 128:(j + 1) * 128],
                            ident[0:nr, 0:nr]),
                            reads=[b_xs, b_ident], writes=[psb[bank]], inc=(q == 3))
                    for q in range(4):
                        j = jq * 4 + q
                        src = ps[bank][:, q * nr:(q + 1) * nr]
                        dst = hT[:, j, c0:c0 + nr]
                        if jq % 2 == 0:
                            tk.op("act", lambda e, src=src, dst=dst, j=j, mc=mc: e.activation(
                                out=dst, in_=src, func=AF.Identity,
                                scale=sc1T[:, j, mc:mc + 1], bias=modT[:, j, mc:mc + 1]),
                                reads=[psb[bank], b_mod], writes=[hb(ti, j)])
                        else:
                            tk.op("dve", lambda e, src=src, dst=dst, j=j, mc=mc: e.tensor_scalar(
                                dst, src, sc1T[:, j, mc:mc + 1], modT[:, j, mc:mc + 1],
                                op0=ALU.mult, op1=ALU.add),
                                reads=[psb[bank], b_mod], writes=[hb(ti, j)])
                        ev += 1
            tk.barrier()
        if stop_after == "A0":
            return

        w_ring = Ring(nc, es, "wr", 4, [P, KC, 128], BF16)
        w_slots = {}

        def load_w(blk):
            if blk >= NBLK or blk in w_slots:
                return
            wt, b_wt = w_ring.next()
            tk.dma("pool", wt[:], dr["wA"][blk], writes=[b_wt])
            w_slots[blk] = (wt, b_wt)

        proj_bank = [0]

        def project(blk, col0, ncols):
            wt, b_wt = w_slots[blk]
            bank = proj_bank[0]
            proj_bank[0] = (bank + 1) % 4
            for k in range(KC):
                tk.op("pe", lambda e, k=k: e.matmul(
                    ps[bank][:, 0:ncols], wt[:, k, :], hT[:, k, col0:col0 + ncols],
                    start=(k == 0), stop=(k == KC - 1)),
                    reads=[b_wt] + hT_bufs(col0, ncols, k), writes=[psb[bank]], inc=(k == KC - 1))
            return bank

        for blk in range(3):
            load_w(blk)

        with ExitStack() as es2:
            sb2 = lambda name, shape, dt: es2.enter_context(nc.sbuf_tensor(uname(name), shape, dt))
            cosT = sb2("cosT", [P, TL], F32)
            sinT = sb2("sinT", [P, TL], F32)
            pm = sb2("pm", [P, P], BF16)
            b_tab = Buf("tabs", accum=True)
            tk.dma("sp", cosT[:], dr["cosT"][:, :], writes=[b_tab])
            tk.dma("sp", sinT[:], dr["sinT"][:, :], writes=[b_tab])
            tk.dma("pool", pm[:], dr["pm"][:, :], writes=[b_tab])
            kb_ring = Ring(nc, es2, "kb", 2, [P, 512], BF16)
            t1_ring = Ring(nc, es2, "t1", 2, [P, 512], F32)
            t2_ring = Ring(nc, es2, "t2", 2, [P, 512], F32)
            ko_ring = Ring(nc, es2, "ko", 3, [P, 512], BF16)
            rope_bank = [0]

            def rope_block(blk, dst_lat, dst_ctx, b_dst, tiles):
                for (c0, n, segs) in tiles:
                    bank = project(blk, c0, n)
                    for (p0, sn, region, off) in segs:
                        if region == "lat":
                            kb, b_kb = kb_ring.next()
                            tk.op("act", lambda e, kb=kb: e.copy(out=kb[:, 0:sn], in_=ps[bank][:, p0:p0 + sn]),
                                  reads=[psb[bank]], writes=[b_kb])
                            rb = 4 + rope_bank[0]
                            rope_bank[0] = (rope_bank[0] + 1) % 2
                            tk.op("pe", lambda e, kb=kb, rb=rb: e.matmul(
                                ps[rb][:, 0:sn], pm[:], kb[:, 0:sn], start=True, stop=True),
                                reads=[b_kb, b_tab], writes=[psb[rb]])
                            t1, b_t1 = t1_ring.next()
                            t2, b_t2 = t2_ring.next()
                            tk.op("dve", lambda e, t1=t1: e.tensor_tensor(
                                t1[:, 0:sn], ps[bank][:, p0:p0 + sn], cosT[:, off:off + sn], op=ALU.mult),
                                reads=[psb[bank], b_tab], writes=[b_t1])
                            tk.op("dve", lambda e, t2=t2, rb=rb: e.tensor_tensor(
                                t2[:, 0:sn], ps[rb][:, 0:sn], sinT[:, off:off + sn], op=ALU.mult),
                                reads=[psb[rb], b_tab], writes=[b_t2])
                            ko, b_ko = ko_ring.next()
                            tk.op("dve", lambda e, t1=t1, t2=t2, ko=ko: e.tensor_tensor(
                                ko[:, 0:sn], t1[:, 0:sn], t2[:, 0:sn], op=ALU.add),
                                reads=[b_t1, b_t2], writes=[b_ko])
                            tk.dma("sp", dst_lat[:, off:off + sn], ko[:, 0:sn], reads=[b_ko], writes=[b_dst])
                        elif region == "ctx":
                            ko, b_ko = ko_ring.next()
                            tk.op("act", lambda e, ko=ko: e.copy(out=ko[:, 0:sn], in_=ps[bank][:, p0:p0 + sn]),
                                  reads=[psb[bank]], writes=[b_ko])
                            tk.dma("sp", dst_ctx[:, off:off + sn], ko[:, 0:sn], reads=[b_ko], writes=[b_dst])

            k_tiles = seg_tiles(True, False)
            for h in range(8):
                load_w(h + 3)
                rope_block(h, dr["KTg"][h], dr["KTc"][h], dr["b_kv"], k_tiles)

            if stop_after == "K":
                tk.barrier()
                return
            with ExitStack() as es3:
                wv_ring = Ring(nc, es3, "wv", 2, [P, KC, 512], BF16)
                vs_ring = Ring(nc, es3, "vs", 3, [P, 512], BF16)
                wvs = []
                for vb in range(2):
                    wv, b_wv = wv_ring.next()
                    tk.dma("pool", wv[:], dr["wV"][vb], writes=[b_wv])
                    wvs.append((wv, b_wv))
                vtiles = [(i * 128, dr["Vg"], i * 128) for i in range(16)]
                vtiles += [(COL_CTX + i * 128, dr["Vc"], i * 128) for i in range(2)]
                ev = 0
                for vb in range(2):
                    wv, b_wv = wvs[vb]
                    for (c0, dst, r0) in vtiles:
                        bank = proj_bank[0]
                        proj_bank[0] = (bank + 1) % 4
                        for k in range(KC):
                            tk.op("pe", lambda e, k=k, c0=c0, bank=bank, wv=wv: e.matmul(
                                ps[bank][:, :], hT[:, k, c0:c0 + 128], wv[:, k, :],
                                start=(k == 0), stop=(k == KC - 1)),
                                reads=[b_wv] + hT_bufs(c0, 128, k), writes=[psb[bank]], inc=(k == KC - 1))
                        vs, b_vs = vs_ring.next()
                        if ev % 2 == 0:
                            tk.op("act", lambda e, vs=vs, bank=bank: e.copy(out=vs[:], in_=ps[bank][:, :]),
                                  reads=[psb[bank]], writes=[b_vs])
                        else:
                            tk.op("dve", lambda e, vs=vs, bank=bank: e.tensor_copy(vs[:], ps[bank][:, :]),
                                  reads=[psb[bank]], writes=[b_vs])
                        ev += 1
                        tk.dma("sp", dst[r0:r0 + 128, vb * 512:(vb + 1) * 512], vs[:],
                               reads=[b_vs], writes=[dr["b_kv"]])
                tk.barrier()
            if dr.get("after_kv") is not None:
                dr["after_kv"]()

            if stop_after == "V":
                return
            q_tiles = seg_tiles(full_ctx, False)
            sg_ring = Ring(nc, es2, "sg", 3, [P, 512], BF16)
            for h in range(8):
                bq = 8 + 2 * h
                load_w(bq + 3)
                rope_block(bq, dr["QT"][h], dr["QT"][h][:, TL:NQ], dr["b_q"], q_tiles)
                load_w(bq + 4)
                for (c0, n, segs) in q_tiles:
                    bank = project(bq + 1, c0, n)
                    for (p0, sn, region, off) in segs:
                        sg, b_sg = sg_ring.next()
                        tk.op("act", lambda e, sg=sg, bank=bank: e.activation(
                            out=sg[:, 0:sn], in_=ps[bank][:, p0:p0 + sn], func=AF.Silu),
                            reads=[psb[bank]], writes=[b_sg])
                        qc = q_col(region, off)
                        tk.dma("sp", dr["SAG"][h][:, qc:qc + sn], sg[:, 0:sn], reads=[b_sg], writes=[dr["b_q"]])
            tk.barrier()

        if stop_after == "Q":
            return
        with ExitStack() as es2:
            sb2 = lambda name, shape, dt: es2.enter_context(nc.sbuf_tensor(uname(name), shape, dt))
            U = sb2("U", [P, PADW], F32)
            A = sb2("A", [P, PADW], F32)
            B = sb2("B", [P, PADW], F32)
            DT = sb2("DT", [P, PADW], BF16)
            b_U, b_A, b_B, b_DT = Buf("U"), Buf("A"), Buf("B"), Buf("DT")
            convw = sb2("convw", [P, 4, 3], F32)
            masks = sb2("masks", [P, 2], F32)
            poolrc = sb2("poolrc", [P, 128], F32)
            pscale = sb2("pscale", [P, 4], F32)
            poolw = sb2("poolw", [P, 4, 128], BF16)
            b_cst = Buf("cst", accum=True)
            tk.dma("sp", convw[:], dr["convw"][:, :, :], writes=[b_cst])
            tk.dma("sp", masks[:], dr["masks"][:, :], writes=[b_cst])
            tk.dma("sp", poolrc[:], dr["poolrc"][:, :], writes=[b_cst])
            tk.dma("sp", pscale[:], dr["pscale"][:, :], writes=[b_cst])
            tk.dma("pool", poolw[:], dr["poolw"][:, :, :], writes=[b_cst])
            tk.op("dve", lambda e: e.memset(U[:], 0.0), writes=[b_U])
            tk.op("dve", lambda e: e.memset(A[:], 0.0), writes=[b_A])
            tk.op("dve", lambda e: e.memset(B[:], 0.0), writes=[b_B])
            sf_ring = Ring(nc, es2, "sf", 3, [P, 512], F32)
            yo_ring = Ring(nc, es2, "yo", 3, [P, 512], BF16)
            c_tiles = seg_tiles(full_ctx, True)

            def mask_halos(T, b_T):
                tk.op("dve", lambda e: e.tensor_scalar(T[:, PAD_HL:PAD_HL + 8], T[:, PAD_HL:PAD_HL + 8],
                                                       masks[:, 0:1], None, op0=ALU.mult),
                      reads=[b_cst], writes=[b_T])
                tk.op("dve", lambda e: e.tensor_scalar(T[:, PAD_HR:PAD_HR + 8], T[:, PAD_HR:PAD_HR + 8],
                                                       masks[:, 1:2], None, op0=ALU.mult),
                      reads=[b_cst], writes=[b_T])

            def y_out(chunk, bank, p0, sn, region, off, mulsrc, b_mulsrc, scale_ap=None):
                yo, b_yo = yo_ring.next()
                if scale_ap is None:
                    tk.op("dve", lambda e: e.tensor_tensor(
                        yo[:, 0:sn], ps[bank][:, p0:p0 + sn], mulsrc, op=ALU.mult),
                        reads=[psb[bank], b_mulsrc], writes=[b_yo])
                else:
                    tk.op("dve", lambda e: e.scalar_tensor_tensor(
                        yo[:, 0:sn], ps[bank][:, p0:p0 + sn], scale_ap, mulsrc, op0=ALU.mult, op1=ALU.mult),
                        reads=[psb[bank], b_mulsrc, b_cst], writes=[b_yo])
                qc = q_col(region, off)
                tk.dma("sp", dr["YT"][chunk][:, qc:qc + sn], yo[:, 0:sn], reads=[b_yo], writes=[dr["b_y"]])

            for j in range(4):
                b0 = 24 + 4 * j
                load_w(b0 + 3)
                for (c0, n, segs) in c_tiles:
                    bank = project(b0, c0, n)
                    for (p0, sn, region, off) in segs:
                        pc = pad_col(region, off)
                        tk.op("act", lambda e, pc=pc: e.copy(out=U[:, pc:pc + sn], in_=ps[bank][:, p0:p0 + sn]),
                              reads=[psb[bank]], writes=[b_U])
                load_w(b0 + 4)
                for (c0, n, segs) in c_tiles:
                    bank = project(b0 + 1, c0, n)
                    for (p0, sn, region, off) in segs:
                        pc = pad_col(region, off)
                        tk.op("dve", lambda e, pc=pc: e.tensor_tensor(
                            U[:, pc:pc + sn], ps[bank][:, p0:p0 + sn], U[:, pc:pc + sn], op=ALU.mult),
                            reads=[psb[bank]], writes=[b_U])
                mask_halos(U, b_U)
                tk.op("dve", lambda e: e.tensor_scalar(A[:, 1:PADW - 1], U[:, 1:PADW - 1],
                                                       convw[:, j, 1:2], None, op0=ALU.mult),
                      reads=[b_U, b_cst], writes=[b_A])
                tk.op("dve", lambda e: e.scalar_tensor_tensor(
                    A[:, 1:PADW - 1], U[:, 0:PADW - 2], convw[:, j, 0:1], A[:, 1:PADW - 1],
                    op0=ALU.mult, op1=ALU.add), reads=[b_U, b_cst], writes=[b_A])
                tk.op("dve", lambda e: e.scalar_tensor_tensor(
                    A[:, 1:PADW - 1], U[:, 2:PADW], convw[:, j, 2:3], A[:, 1:PADW - 1],
                    op0=ALU.mult, op1=ALU.add), reads=[b_U, b_cst], writes=[b_A])
                load_w(b0 + 5)
                for (c0, n, segs) in c_tiles:
                    bank = project(b0 + 2, c0, n)
                    for (p0, sn, region, off) in segs:
                        if region in ("hl", "hr"):
                            continue
                        pc = pad_col(region, off)
                        tk.op("dve", lambda e, pc=pc: e.tensor_tensor(
                            A[:, pc:pc + sn], ps[bank][:, p0:p0 + sn], A[:, pc:pc + sn], op=ALU.mult),
                            reads=[psb[bank]], writes=[b_A])
                load_w(b0 + 6)
                for (c0, n, segs) in c_tiles:
                    bank = project(b0 + 3, c0, n)
                    for (p0, sn, region, off) in segs:
                        if region in ("hl", "hr"):
                            continue
                        pc = pad_col(region, off)
                        sf, b_sf = sf_ring.next()
                        tk.op("act", lambda e, sf=sf: e.activation(
                            out=sf[:, 0:sn], in_=ps[bank][:, p0:p0 + sn], func=AF.Silu),
                            reads=[psb[bank]], writes=[b_sf])
                        yo, b_yo = yo_ring.next()
                        tk.op("dve", lambda e, sf=sf, yo=yo, pc=pc: e.tensor_tensor(
                            yo[:, 0:sn], sf[:, 0:sn], A[:, pc:pc + sn], op=ALU.mult),
                            reads=[b_sf, b_A], writes=[b_yo])
                        qc = q_col(region, off)
                        tk.dma("sp", dr["YT"][j][:, qc:qc + sn], yo[:, 0:sn], reads=[b_yo], writes=[dr["b_y"]])

            for g in range(4):
                w = POOL_WINDOWS[g]
                b0 = 40 + 2 * g
                load_w(b0 + 3)
                for (c0, n, segs) in c_tiles:
                    bank = project(b0, c0, n)
                    for (p0, sn, region, off) in segs:
                        pc = pad_col(region, off)
                        tk.op("act", lambda e, pc=pc: e.copy(out=U[:, pc:pc + sn], in_=ps[bank][:, p0:p0 + sn]),
                              reads=[psb[bank]], writes=[b_U])
                mask_halos(U, b_U)
                src, b_src = U, b_U
                span = 1
                pp = [(A, b_A), (B, b_B)]
                pi = 0
                while span < w:
                    dst, b_dst = pp[pi]
                    pi ^= 1
                    n = PADW - span
                    tk.op("dve", lambda e, src=src, dst=dst, span=span, n=n: e.tensor_tensor(
                        dst[:, 0:n], src[:, 0:n], src[:, span:span + n], op=ALU.add),
                        reads=[b_src], writes=[b_dst])
                    src, b_src = dst, b_dst
                    span *= 2
                hw = w // 2
                tk.op("dve", lambda e, src=src, hw=hw, w=w: e.scalar_tensor_tensor(
                    DT[:, 8:PAD_Z2], src[:, 8 - hw:PAD_Z2 - hw], 1.0 / w, U[:, 8:PAD_Z2],
                    op0=ALU.mult, op1=ALU.subtract), reads=[b_src, b_U], writes=[b_DT])
                edges = [PAD_LAT, PAD_LAT + TL - 8, PAD_CTX, PAD_CTX + CTXN - 8]
                for ei, ec in enumerate(edges):
                    if ei >= 2 and not full_ctx:
                        continue
                    other, b_other = pp[pi]
                    ro = g * 32 + ei * 8
                    tk.op("dve", lambda e, src=src, other=other, ec=ec, ro=ro, hw=hw: e.tensor_tensor(
                        other[:, ec:ec + 8], src[:, ec - hw:ec - hw + 8], poolrc[:, ro:ro + 8], op=ALU.mult),
                        reads=[b_src, b_cst], writes=[b_other])
                    tk.op("dve", lambda e, other=other, ec=ec: e.tensor_tensor(
                        DT[:, ec:ec + 8], other[:, ec:ec + 8], U[:, ec:ec + 8], op=ALU.subtract),
                        reads=[b_other, b_U], writes=[b_DT])
                load_w(b0 + 4)
                for (c0, n, segs) in c_tiles:
                    segs2 = [sg_ for sg_ in segs if sg_[2] in ("lat", "ctx")]
                    if not segs2:
                        continue
                    bank = project(b0 + 1, c0, n)
                    for (p0, sn, region, off) in segs2:
                        pc = pad_col(region, off)
                        sf, b_sf = sf_ring.next()
                        tk.op("act", lambda e, sf=sf: e.activation(
                            out=sf[:, 0:sn], in_=ps[bank][:, p0:p0 + sn], func=AF.Silu),
                            reads=[psb[bank]], writes=[b_sf])
                        rb = 4 + (proj_bank[0] % 2)
                        tk.op("pe", lambda e, rb=rb, pc=pc: e.matmul(
                            ps[rb][:, 0:sn], poolw[:, g, :], DT[:, pc:pc + sn], start=True, stop=True),
                            reads=[b_DT, b_cst], writes=[psb[rb]])
                        y_out(12 + g, rb, 0, sn, region, off, sf[:, 0:sn], b_sf, scale_ap=pscale[:, g:g + 1])
            tk.barrier()
    tk.barrier()


def declare(nc, name, shape, dt, kind):
    return nc.dram_tensor(name, list(shape), dt, kind=kind).ap()


PHASE_A_INPUTS = [
    ("xin", (NT, D), F32), ("cvec", (P, KC, 2), F32), ("wmod", (12, P, KC, 512), F32),
    ("bmod", (2, 6144), F32), ("wA", (NBLK, P, KC, 128), F32), ("wV", (2, P, KC, 512), F32),
    ("cosT", (P, TL), F32), ("sinT", (P, TL), F32), ("pm", (P, P), F32), ("ident", (P, P), F32),
    ("convw", (P, 4, 3), F32), ("masks", (P, 2), F32), ("poolrc", (P, 128), F32),
    ("poolw", (P, 4, 128), F32), ("pscale", (P, 4), F32),
]
PHASE_A_OUTPUTS = [
    ("KTg", (8, P, TL), BF16), ("Vg", (TL, 1024), BF16), ("KTc", (8, P, CTXN), BF16),
    ("Vc", (CTXN, 1024), BF16), ("QT", (8, P, NQ), BF16), ("SAG", (8, P, NQ), BF16),
    ("YT", (16, P, NQ), BF16), ("modv", (2, 6144), F32),
]


def build_phase_A_program(layer, full_ctx, stop_after=None):
    nc = bass.Bass("TRN2", target_bir_lowering=False)
    dr = {}
    for name, shape, dt in PHASE_A_INPUTS:
        dr[name] = declare(nc, name, shape, dt, "ExternalInput")
    for name, shape, dt in PHASE_A_OUTPUTS:
        dr[name] = declare(nc, name, shape, dt, "ExternalOutput")
    for b in ("b_kv", "b_q", "b_y", "b_modv"):
        dr[b] = Buf(b, accum=True)
    dr["b_xin"] = Buf("b_xin")
    with ExitStack() as es:
        ps = [es.enter_context(nc.psum_tensor("ps%d" % i, [P, 512], F32)) for i in range(8)]
        psb = [Buf("ps%d" % i, exclusive=True) for i in range(8)]
        tk = Tracker(nc, es)
        emit_phase_A(nc, tk, ps, psb, dr, layer, full_ctx, stop_after)
    return nc


def emit_phase_B(nc, tk, ps, psb, dr, layer, with_ctx_q):
    lam_init = lam_init_of(layer)
    with ExitStack() as es:
        sb = lambda name, shape, dt: es.enter_context(nc.sbuf_tensor(uname(name), shape, dt))
        Vf = sb("Vf", [P, NKT, 1024], BF16)
        b_vf = [Buf("vf%d" % i) for i in range(9)]
        ones_bf = sb("ones_bf", [P, P], BF16)
        ones_f = sb("ones_f", [P, P], F32)
        epsb = sb("epsb", [P, 1], F32)
        lamb = sb("lamb", [P, 256], F32)
        lt = sb("lt", [P, 128], F32)
        sm = sb("sm", [P, 4], F32)
        neg_lam = sb("neg_lam", [P, 1], F32)
        gsub = sb("gsub", [P, 1], F32)
        b_c = Buf("constsB")
        tk.op("dve", lambda e: e.memset(ones_bf[:], 1.0), writes=[b_c])
        tk.op("dve", lambda e: e.memset(ones_f[:], 1.0 / 128.0), writes=[b_c])
        tk.op("dve", lambda e: e.memset(epsb[:], RMS_EPS), writes=[b_c])
        b_lam = Buf("lam")
        tk.dma("sp", lamb[:], dr["lamv"].partition_broadcast(P), writes=[b_lam])
        tk.dma("sp", gsub[:], dr["subg"][:, :], writes=[b_c])
        def vf_buf(kt):
            return b_vf[0] if kt < 2 else b_vf[1 + (kt - 2) // 4]

        tk.op("dve", lambda e: e.tensor_tensor(lt[:, 0:64], lamb[:, 0:64], lamb[:, 64:128], op=ALU.mult),
              reads=[b_lam], writes=[b_lam])
        tk.op("dve", lambda e: e.tensor_tensor(lt[:, 64:128], lamb[:, 128:192], lamb[:, 192:256], op=ALU.mult),
              reads=[b_lam], writes=[b_lam])
        tk.op("dve", lambda e: e.tensor_reduce(sm[:, 0:1], lt[:, 0:64], axis=mybir.AxisListType.X, op=ALU.add),
              reads=[b_lam], writes=[b_lam])
        tk.op("dve", lambda e: e.tensor_reduce(sm[:, 1:2], lt[:, 64:128], axis=mybir.AxisListType.X, op=ALU.add),
              reads=[b_lam], writes=[b_lam])
        tk.op("act", lambda e: e.activation(out=sm[:, 2:4], in_=sm[:, 0:2], func=AF.Exp),
              reads=[b_lam], writes=[b_lam])
        tk.op("dve", lambda e: e.tensor_tensor(neg_lam[:], sm[:, 3:4], sm[:, 2:3], op=ALU.subtract),
              reads=[b_lam], writes=[b_lam])
        tk.op("dve", lambda e: e.tensor_scalar(neg_lam[:], neg_lam[:], -lam_init, None, op0=ALU.add),
              reads=[b_lam], writes=[b_lam])
        tk.op("dve", lambda e: e.tensor_scalar(gsub[:], gsub[:], 1.0 - lam_init, None, op0=ALU.mult),
              reads=[b_c], writes=[b_c])

        kt_ring = Ring(nc, es, "KT", 2, [P, NKEY], BF16)
        q_ring = [Ring(nc, es, "Qp%d" % m, 2, [P, NQ], BF16) for m in range(2)]
        for m in range(2):
            for (t, b) in q_ring[m].items:
                tk.op("dve", lambda e, t=t: e.memset(t[:], 0.0), writes=[b])
        sag_ring = Ring(nc, es, "sagB", 2, [P, NQ], BF16)
        e_ring = Ring(nc, es, "E", 6, [P, 512], BF16)
        f_ring = Ring(nc, es, "fB", 8, [P, 512], F32)
        yo_ring = Ring(nc, es, "yoB", 2, [P, 512], BF16)

        qtiles = [(512 * t, 512, list(range(NKT))) for t in range(4)]
        if with_ctx_q:
            qtiles.append((TL, CTXN, [0, 1]))

        heads = {}

        def load_head(h):
            if h >= N_HEADS or h in heads:
                return
            KT, b_KT = kt_ring.next()
            b_KT.accum = True
            tk.dma("sp", KT[:, 0:CTXN], dr["KTc"][h], reads=[dr["b_kv"]], writes=[b_KT])
            for r in range(2):
                tk.dma("sp", KT[:, CTXN + r * TL:CTXN + (r + 1) * TL], dr["kt_src"](r, h),
                       reads=[dr["b_kvall"]], writes=[b_KT])
            Q0, b_Q0 = q_ring[0].next()
            Q1, b_Q1 = q_ring[1].next()
            tk.dma("sp", Q0[0:64, :], dr["QT"][h][0:64, :], reads=[dr["b_q"]], writes=[b_Q0])
            tk.dma("sp", Q1[64:128, :], dr["QT"][h][64:128, :], reads=[dr["b_q"]], writes=[b_Q1])
            SG, b_SG = sag_ring.next()
            tk.dma("sp", SG[:], dr["SAG"][h], reads=[dr["b_q"]], writes=[b_SG])
            heads[h] = (KT, b_KT, (Q0, Q1), (b_Q0, b_Q1), SG, b_SG)

        load_head(0)
        tk.dma("sp", Vf[:, 0:2, :], dr["Vc"].rearrange("(t p) c -> p t c", p=P), reads=[dr["b_kv"]], writes=[b_vf[0]])
        for r in range(2):
            for q4 in range(4):
                t0 = 2 + 16 * r + 4 * q4
                tk.dma("sp", Vf[:, t0:t0 + 4, :],
                       dr["v_src"](r, q4).rearrange("(t p) c -> p t c", p=P),
                       reads=[dr["b_kvall"]], writes=[b_vf[1 + 4 * r + q4]])

        sidx = [0]
        pending_tail = []
        for h in range(N_HEADS):
            need_load = [h + 1]
            if not pending_tail:
                load_head(h + 1)
                need_load = []
            KT, b_KT, Qp, b_Qp, SG, b_SG = heads[h]
            for (q0, nq, kts) in qtiles:
                def s_mm(kt):
                    banks = []
                    for m in range(2):
                        bank = 2 * (sidx[0] % 2) + m
                        tk.op("pe", lambda e, m=m, bank=bank: e.matmul(
                            ps[bank][:, 0:nq], KT[:, kt * P:(kt + 1) * P], Qp[m][:, q0:q0 + nq],
                            start=True, stop=True),
                            reads=[b_KT, b_Qp[m]], writes=[psb[bank]])
                        banks.append(bank)
                    sidx[0] += 1
                    return banks

                pend = s_mm(kts[0])
                for i, kt in enumerate(kts):
                    if pending_tail and (i == 12 or (i == len(kts) - 1 and len(kts) < 13)):
                        pending_tail.pop(0)()
                    if need_load and not pending_tail:
                        load_head(need_load.pop())
                    cur = pend
                    if i + 1 < len(kts):
                        pend = s_mm(kts[i + 1])
                    es_ = []
                    for m in range(2):
                        E, b_E = e_ring.next()
                        tk.op("act", lambda e, E=E, m=m: e.activation(
                            out=E[:, 0:nq], in_=ps[cur[m]][:, 0:nq], func=AF.Exp, scale=ATTN_SCALE),
                            reads=[psb[cur[m]]], writes=[b_E])
                        es_.append((E, b_E))
                    first, last = (i == 0), (i == len(kts) - 1)
                    for m in range(2):
                        E, b_E = es_[m]
                        tk.op("pe", lambda e, E=E, m=m: e.matmul(
                            ps[4 + m][:, 0:nq], Vf[:, kt, h * P:(h + 1) * P], E[:, 0:nq],
                            start=first, stop=last),
                            reads=[b_E, vf_buf(kt)], writes=[psb[4 + m]], inc=last)
                    for m in range(2):
                        E, b_E = es_[m]
                        tk.op("pe", lambda e, E=E, m=m: e.matmul(
                            ps[6 + m][:, 0:nq], ones_bf[:], E[:, 0:nq], start=first, stop=last),
                            reads=[b_E, b_c], writes=[psb[6 + m]], inc=last)
                oc, zc = [], []
                for m in range(2):
                    o_, b_o = f_ring.next()
                    tk.op("act", lambda e, o_=o_, m=m: e.copy(out=o_[:, 0:nq], in_=ps[4 + m][:, 0:nq]),
                          reads=[psb[4 + m]], writes=[b_o])
                    oc.append((o_, b_o))
                    z_, b_z = f_ring.next()
                    tk.op("dve", lambda e, z_=z_, m=m: e.tensor_copy(z_[:, 0:nq], ps[6 + m][:, 0:nq]),
                          reads=[psb[6 + m]], writes=[b_z])
                    zc.append((z_, b_z))
                for m in range(2):
                    z_, b_z = zc[m]
                    o_, b_o = oc[m]
                    tk.op("dve", lambda e, z_=z_: e.reciprocal(z_[:, 0:nq], z_[:, 0:nq]),
                          reads=[b_z], writes=[b_z])
                    tk.op("pool", lambda e, z_=z_, o_=o_: e.tensor_tensor(
                        o_[:, 0:nq], o_[:, 0:nq], z_[:, 0:nq], op=ALU.mult),
                        reads=[b_z, b_o], writes=[b_o])
                att, b_att = oc[0]
                tk.op("dve", lambda e: e.scalar_tensor_tensor(
                    att[:, 0:nq], oc[1][0][:, 0:nq], neg_lam[:, 0:1], att[:, 0:nq],
                    op0=ALU.mult, op1=ALU.add),
                    reads=[oc[1][1], b_lam], writes=[b_att])
                def make_tail(att=att, b_att=b_att, oc=oc, zc=zc, nq=nq, q0=q0, h=h, SG=SG, b_SG=b_SG):
                    sq, b_sq = oc[1]
                    tk.op("pool", lambda e: e.tensor_tensor(sq[:, 0:nq], att[:, 0:nq], att[:, 0:nq], op=ALU.mult),
                          reads=[b_att], writes=[b_sq])
                    mbank = 2 * (sidx[0] % 2)
                    tk.op("pe", lambda e: e.matmul(ps[mbank][:, 0:nq], ones_f[:], sq[:, 0:nq], start=True, stop=True),
                          reads=[b_sq, b_c], writes=[psb[mbank]])
                    lnv, b_lnv = zc[0]
                    tk.op("act", lambda e: e.activation(out=lnv[:, 0:nq], in_=ps[mbank][:, 0:nq], func=AF.Ln,
                                                        bias=epsb[:, 0:1]),
                          reads=[psb[mbank], b_c], writes=[b_lnv])
                    tk.op("act", lambda e: e.activation(out=lnv[:, 0:nq], in_=lnv[:, 0:nq], func=AF.Exp, scale=-0.5),
                          reads=[b_lnv], writes=[b_lnv])
                    tk.op("dve", lambda e: e.scalar_tensor_tensor(
                        att[:, 0:nq], att[:, 0:nq], gsub[:, 0:1], lnv[:, 0:nq], op0=ALU.mult, op1=ALU.mult),
                        reads=[b_att, b_lnv, b_c], writes=[b_att])
                    yo, b_yo = yo_ring.next()
                    tk.op("pool", lambda e: e.tensor_tensor(yo[:, 0:nq], att[:, 0:nq], SG[:, q0:q0 + nq], op=ALU.mult),
                          reads=[b_att, b_SG], writes=[b_yo])
                    tk.dma("sp", dr["YTat"][h][:, q0:q0 + nq], yo[:, 0:nq], reads=[b_yo], writes=[dr["b_yat"]])
                    return None
                pending_tail.append(make_tail)
        while pending_tail:
            pending_tail.pop(0)()
        tk.barrier()


def load_wout(nc, tk, dr, Wo):
    b_wo = [Buf("wo%d" % i) for i in range(4)]
    for i in range(4):
        tk.dma("pool", Wo[:, 4 * i:4 * i + 4, :], dr["wout"][:, 4 * i:4 * i + 4, :], writes=[b_wo[i]])
    return b_wo


def emit_phase_C(nc, tk, ps, psb, dr, layer, with_ctx):
    with ExitStack() as es:
        sb = lambda name, shape, dt: es.enter_context(nc.sbuf_tensor(uname(name), shape, dt))
        if dr.get("Wo") is not None:
            Wo, b_wo = dr["Wo"]
        else:
            Wo = sb("Wo", [P, KC, D], BF16)
            b_wo = load_wout(nc, tk, dr, Wo)
        gB = [sb("gB%d" % i, [P, D], F32) for i in range(2)]
        lnG = sb("lnG", [P, D], F32)
        lnB = sb("lnB", [P, D], F32)
        epsb = sb("epsC", [P, 1], F32)
        b_c = Buf("constsC", accum=True)
        tk.op("dve", lambda e: e.memset(epsb[:], LN_EPS), writes=[b_c])
        y_ring = Ring(nc, es, "Ys", 2, [P, KC, 512], BF16)
        x_ring = Ring(nc, es, "xC", 3, [P, D], F32)
        T_ring = Ring(nc, es, "TC", 2, [P, D], F32)
        o_ring = Ring(nc, es, "oC", 3, [P, D], F32)
        st_ring = Ring(nc, es, "stC", 2, [P, 32], F32)

        groups = [(512 * t, 512, 512 * t, 512 * t, 0) for t in range(4)]
        if with_ctx:
            groups.append((TL, CTXN, dr["xin_ctx_row0"], dr["xout_ctx_row0"], 1))
        pbank = [0]
        ys_loaded = {}

        def load_ys(g):
            if g >= len(groups) or g in ys_loaded:
                return
            c0_, ncol_ = groups[g][0], groups[g][1]
            Ys_, b_Ys_ = y_ring.next()
            b_Ys_.accum = True
            for k in range(KC):
                src_ap, src_buf = dr["y_src"](k)
                tk.dma("sp", Ys_[:, k, 0:ncol_], src_ap[:, c0_:c0_ + ncol_], reads=[src_buf], writes=[b_Ys_])
            ys_loaded[g] = (Ys_, b_Ys_)

        tiles = []
        for g, (c0, ncol, xr0, or0, gi) in enumerate(groups):
            for st in range(ncol // P):
                tiles.append((g, st, xr0, or0, gi))
        xs_loaded = {}

        def load_x(i):
            if i >= len(tiles) or i in xs_loaded:
                return
            g_, st_, xr0_, or0_, gi_ = tiles[i]
            xs_, b_xs_ = x_ring.next()
            tk.dma("sp", xs_[:], dr["xin"][xr0_ + st_ * P:xr0_ + (st_ + 1) * P, :], reads=[dr["b_xin"]], writes=[b_xs_])
            xs_loaded[i] = (xs_, b_xs_)

        load_ys(0)
        load_x(0)
        tk.dma("sp", gB[0][:], dr["modv"][0:1, 4096:6144].partition_broadcast(P), reads=[dr["b_modv"]], writes=[b_c])
        if with_ctx:
            tk.dma("sp", gB[1][:], dr["modv"][1:2, 4096:6144].partition_broadcast(P), reads=[dr["b_modv"]], writes=[b_c])
        tk.dma("sp", lnG[:], dr["lng"].partition_broadcast(P), writes=[b_c])
        tk.dma("sp", lnB[:], dr["lnb"].partition_broadcast(P), writes=[b_c])
        for ti, (g, st, xr0, or0, gi) in enumerate(tiles):
            if True:
                load_x(ti + 1)
                if st == 0:
                    load_ys(g + 1)
                Ys, b_Ys = ys_loaded[g]
                xs, b_xs = xs_loaded[ti]
                T, b_T = T_ring.next()
                half = pbank[0] % 2
                pbank[0] += 1
                for cg in range(4):
                    bank = 4 * half + cg
                    for k in range(KC):
                        tk.op("pe", lambda e, k=k, bank=bank, cg=cg: e.matmul(
                            ps[bank][:, :], Ys[:, k, st * P:(st + 1) * P], Wo[:, k, cg * 512:(cg + 1) * 512],
                            start=(k == 0), stop=(k == KC - 1)),
                            reads=[b_Ys, b_wo[k // 4]], writes=[psb[bank]], inc=(k == KC - 1))
                    tk.op("dve", lambda e, bank=bank, cg=cg: e.tensor_tensor(
                        T[:, cg * 512:(cg + 1) * 512], ps[bank][:, :], gB[gi][:, cg * 512:(cg + 1) * 512],
                        op=ALU.mult), reads=[psb[bank], b_c], writes=[b_T])
                tk.op("dve", lambda e: e.scalar_tensor_tensor(T[:], xs[:], ALPHA, T[:], op0=ALU.mult, op1=ALU.add),
                      reads=[b_xs, b_T], writes=[b_T])
                stt, b_st = st_ring.next()
                for cg in range(4):
                    tk.op("dve", lambda e, cg=cg: e.bn_stats(stt[:, cg * 6:(cg + 1) * 6], T[:, cg * 512:(cg + 1) * 512]),
                          reads=[b_T], writes=[b_st])
                tk.op("dve", lambda e: e.bn_aggr(stt[:, 24:26], stt[:, 0:24]), reads=[b_st], writes=[b_st])
                tk.op("act", lambda e: e.activation(out=stt[:, 26:27], in_=stt[:, 25:26], func=AF.Ln, bias=epsb[:, 0:1]),
                      reads=[b_st, b_c], writes=[b_st])
                tk.op("act", lambda e: e.activation(out=stt[:, 26:27], in_=stt[:, 26:27], func=AF.Exp, scale=-0.5),
                      reads=[b_st], writes=[b_st])
                tk.op("dve", lambda e: e.scalar_tensor_tensor(
                    stt[:, 27:28], stt[:, 24:25], -1.0, stt[:, 26:27], op0=ALU.mult, op1=ALU.mult),
                    reads=[b_st], writes=[b_st])
                o, b_o = o_ring.next()
                tk.op("act", lambda e: e.activation(out=o[:], in_=T[:], func=AF.Identity,
                                                    scale=stt[:, 26:27], bias=stt[:, 27:28]),
                      reads=[b_T, b_st], writes=[b_o])
                tk.op("pool", lambda e: e.tensor_tensor(o[:], o[:], lnG[:], op=ALU.mult),
                      reads=[b_o, b_c], writes=[b_o])
                tk.op("pool", lambda e: e.tensor_tensor(o[:], o[:], lnB[:], op=ALU.add),
                      reads=[b_o, b_c], writes=[b_o])
                tk.dma("sp", dr["xout"][or0 + st * P:or0 + (st + 1) * P, :], o[:], reads=[b_o], writes=[dr["b_xout"]])
                if dr["hx"] is not None and gi == 0:
                    if or0 + st * P == 0:
                        tk.dma("sp", dr["hx"][0:HALO, :], o[0:HALO, :], reads=[b_o], writes=[dr["b_hx"]])
                    if or0 + (st + 1) * P == TL:
                        tk.dma("sp", dr["hx"][HALO:2 * HALO, :], o[P - HALO:P, :], reads=[b_o], writes=[dr["b_hx"]])
        tk.barrier()


PHASE_BC_INPUTS = [
    ("KTall", (2, 8, P, TL), BF16), ("Vall", (2, TL, 1024), BF16), ("KTc", (8, P, CTXN), BF16),
    ("Vc", (CTXN, 1024), BF16), ("QT", (8, P, NQ), BF16), ("SAG", (8, P, NQ), BF16),
    ("YT", (16, P, NQ), BF16), ("modv", (2, 6144), F32), ("xin", (NT, D), F32),
    ("wout", (P, KC, D), F32), ("lng", (1, D), F32), ("lnb", (1, D), F32),
    ("lamv", (1, 256), F32), ("subg", (P, 1), F32),
]


def build_phase_BC_program(layer, with_ctx, do_B=True, do_C=True):
    nc = bass.Bass("TRN2", target_bir_lowering=False)
    dr = {}
    for name, shape, dt in PHASE_BC_INPUTS:
        dr[name] = declare(nc, name, shape, dt, "ExternalInput")
    dr["xout"] = declare(nc, "xout", (NQ, D), F32, "ExternalOutput")
    dr["YTat"] = declare(nc, "YTat", (8, P, NQ), BF16, "ExternalOutput")
    for b in ("b_kv", "b_kvall", "b_q", "b_y", "b_modv", "b_yat", "b_xout"):
        dr[b] = Buf(b, accum=True)
    dr["b_xin"] = Buf("b_xin")
    dr["xin_ctx_row0"] = COL_CTX
    dr["xout_ctx_row0"] = TL
    dr["hx"] = None

    def y_src(k):
        if 4 <= k < 12:
            return dr["YTat"][k - 4], dr["b_yat"]
        return dr["YT"][k], dr["b_y"]
    dr["y_src"] = y_src
    dr["kt_src"] = lambda r, h: dr["KTall"][r][h]
    dr["v_src"] = lambda r, q4: dr["Vall"][r][q4 * 512:(q4 + 1) * 512, :]
    with ExitStack() as es:
        ps = [es.enter_context(nc.psum_tensor("ps%d" % i, [P, 512], F32)) for i in range(8)]
        psb = [Buf("ps%d" % i, exclusive=True) for i in range(8)]
        tk = Tracker(nc, es)
        if do_B:
            emit_phase_B(nc, tk, ps, psb, dr, layer, with_ctx)
        if do_C:
            emit_phase_C(nc, tk, ps, psb, dr, layer, with_ctx)
        tk.barrier()
    return nc


WKEYS = ["w_mod", "b_mod", "w_in", "conv_w", "lam_q1", "lam_k1", "lam_q2", "lam_k2",
         "subln_g", "pool_w", "pool_scale", "w_out", "ln_g", "ln_b"]


def make_xin(core, xfull_b, ctx_b):
    s = core % 2
    z = np.zeros((HALO, D), np.float32)
    hl = xfull_b[s * TL - HALO:s * TL] if s == 1 else z
    hr = xfull_b[(s + 1) * TL:(s + 1) * TL + HALO] if s == 0 else z
    return np.ascontiguousarray(np.concatenate([xfull_b[s * TL:(s + 1) * TL], hl, hr, ctx_b], 0))


_PROGS = {}


def get_prog(kind, layer, flag):
    key = (kind, layer, flag)
    if key not in _PROGS:
        if kind == "A":
            _PROGS[key] = build_phase_A_program(layer, flag)
        else:
            _PROGS[key] = build_phase_BC_program(layer, flag)
    return _PROGS[key]


def kernel_unfused(**inputs):
    inp = {k: np.asarray(v) for k, v in inputs.items()}
    cores = list(range(NCORES))
    consts = [host_core_consts(c) for c in cores]
    x_cur = [inp["x"][b] for b in range(BATCH)]
    ctx_cur = [inp["ctx"][b] for b in range(BATCH)]
    for l in range(DEPTH):
        last = (l == DEPTH - 1)
        lw = host_layer_weights(l, *[inp[k] for k in WKEYS])
        xins = [make_xin(c, x_cur[c // 2], ctx_cur[c // 2]) for c in cores]
        in_maps = []
        for c in cores:
            b = c // 2
            cvec = np.ascontiguousarray(
                np.stack([inp["c"][b], inp["c_ctx"]], -1).reshape(KC, P, 2).transpose(1, 0, 2))
            m = dict(xin=xins[c], cvec=cvec)
            for k in ("wmod", "bmod", "wA", "wV", "convw", "poolw", "pscale"):
                m[k] = lw[k]
            for k in ("cosT", "sinT", "pm", "ident", "masks", "poolrc"):
                m[k] = consts[c][k]
            in_maps.append(m)
        resA = run_bass_kernel_spmd(get_prog("A", l, not last), in_maps, core_ids=cores).results
        in_maps = []
        for c in cores:
            pr = (c // 2) * 2
            r = resA[c]
            m = dict(KTall=np.stack([np.asarray(resA[pr]["KTg"]), np.asarray(resA[pr + 1]["KTg"])], 0),
                     Vall=np.stack([np.asarray(resA[pr]["Vg"]), np.asarray(resA[pr + 1]["Vg"])], 0),
                     KTc=np.asarray(r["KTc"]), Vc=np.asarray(r["Vc"]), QT=np.asarray(r["QT"]),
                     SAG=np.asarray(r["SAG"]), YT=np.asarray(r["YT"]), modv=np.asarray(r["modv"]),
                     xin=xins[c])
            for k in ("wout", "lng", "lnb", "lamv", "subg"):
                m[k] = lw[k]
            in_maps.append(m)
        resBC = run_bass_kernel_spmd(get_prog("BC", l, not last), in_maps, core_ids=cores).results
        x_cur = [np.concatenate([np.asarray(resBC[2 * b]["xout"])[:TL], np.asarray(resBC[2 * b + 1]["xout"])[:TL]], 0)
                 for b in range(BATCH)]
        if not last:
            ctx_cur = [np.asarray(resBC[2 * b]["xout"])[TL:NQ] for b in range(BATCH)]
    return np.ascontiguousarray(np.stack(x_cur, 0).astype(np.float32))


PAIRS = [[0, 1], [2, 3], [4, 5], [6, 7]]
LAYER_INPUTS = [
    ("wmodH", (6, P, KC, 512), F32), ("bmod", (2, 6144), F32), ("wA", (NBLK, P, KC, 128), F32),
    ("wV", (2, P, KC, 512), F32), ("convw", (P, 4, 3), F32), ("poolw", (P, 4, 128), F32),
    ("pscale", (P, 4), F32), ("wout", (P, KC, D), F32), ("lng", (1, D), F32), ("lnb", (1, D), F32),
    ("lamv", (1, 256), F32), ("subg", (P, 1), F32),
]
CORE_INPUTS = [
    ("xin0", (NT, D), F32), ("cvec", (P, KC, 2), F32), ("cosT", (P, TL), F32), ("sinT", (P, TL), F32),
    ("pm", (P, P), F32), ("ident", (P, P), F32), ("masks", (P, 2), F32), ("poolrc", (P, 128), F32),
]


def build_fused_program():
    nc = bass.Bass("TRN2", target_bir_lowering=False)
    base = {}
    for name, shape, dt in CORE_INPUTS:
        base[name] = declare(nc, name, shape, dt, "ExternalInput")
    lay = []
    for l in range(DEPTH):
        d = {}
        for name, shape, dt in LAYER_INPUTS:
            d[name] = declare(nc, "%s%d" % (name, l), shape, dt, "ExternalInput")
        lay.append(d)
    out = declare(nc, "out", (TL, D), F32, "ExternalOutput")
    internal = lambda name, shape, dt: nc.dram_tensor(name, list(shape), dt).ap()
    it = {
        "KTg": internal("KTg", (8, P, TL), BF16), "Vg": internal("Vg", (TL, 1024), BF16),
        "KTall": internal("KTall", (2, 2, 4, P, TL), BF16), "Vall": internal("Vall", (2, 2, TL // 2, 1024), BF16),
        "KTc": internal("KTc", (8, P, CTXN), BF16), "Vc": internal("Vc", (CTXN, 1024), BF16),
        "QT": internal("QT", (8, P, NQ), BF16), "SAG": internal("SAG", (8, P, NQ), BF16),
        "YT": internal("YT", (16, P, NQ), BF16), "modv": internal("modv", (2, 6144), F32),
        "X1": internal("X1", (NT, D), F32), "hx": internal("hx", (2 * HALO, D), F32),
        "hxall": internal("hxall", (4 * HALO, D), F32),
    }
    mod_dist = {"part": internal("modpart", (2, 3072), F32), "all": internal("modall", (4, 3072), F32),
                "b_part": Buf("b_modpart"), "b_all": Buf("b_modall")}
    bufs = {b: Buf(b, accum=True) for b in ("b_kv", "b_kvall", "b_q", "b_y", "b_modv", "b_yat", "b_x1", "b_hx",
                                             "b_hxall", "b_out")}
    with ExitStack() as es:
        ps = [es.enter_context(nc.psum_tensor("ps%d" % i, [P, 512], F32)) for i in range(8)]
        psb = [Buf("ps%d" % i, exclusive=True) for i in range(8)]
        tk = Tracker(nc, es)
        for l in range(DEPTH):
            last = (l == DEPTH - 1)
            dr = dict(base)
            dr.update(lay[l])
            dr.update(it)
            dr.update(bufs)
            dr["b_yat"] = bufs["b_y"]
            dr["mod_dist"] = mod_dist
            dr["YTat"] = it["YT"][4:12]
            dr["y_src"] = lambda k: (it["YT"][k], bufs["b_y"])
            dr["kt_src"] = lambda r, h: it["KTall"][h // 4][r][h % 4]
            dr["v_src"] = lambda r, q4: it["Vall"][q4 // 2][r][(q4 % 2) * 512:(q4 % 2 + 1) * 512, :]
            if l == 0:
                dr["xin"] = base["xin0"]
                dr["b_xin"] = Buf("b_xin0")
                dr["xout"] = it["X1"]
                dr["b_xout"] = bufs["b_x1"]
                dr["xout_ctx_row0"] = COL_CTX
            else:
                dr["xin"] = it["X1"]
                dr["b_xin"] = bufs["b_x1"]
                dr["xout"] = out
                dr["b_xout"] = bufs["b_out"]
                dr["xout_ctx_row0"] = TL
                dr["hx"] = None
            dr["xin_ctx_row0"] = COL_CTX
            def exchange_kv():
                for ch in range(2):
                    tk.collective("AllGather", PAIRS, it["KTg"][4 * ch:4 * ch + 4].rearrange("h p t -> (h p) t"),
                                  it["KTall"][ch].rearrange("r h p t -> (r h p) t"),
                                  reads=[bufs["b_kv"]], writes=[bufs["b_kvall"]])
                for ch in range(2):
                    tk.collective("AllGather", PAIRS, it["Vg"][ch * (TL // 2):(ch + 1) * (TL // 2), :],
                                  it["Vall"][ch].rearrange("r t c -> (r t) c"),
                                  reads=[bufs["b_kv"]], writes=[bufs["b_kvall"]])
            dr["after_kv"] = exchange_kv
            emit_phase_A(nc, tk, ps, psb, dr, l, not last)
            with ExitStack() as esl:
                Wo = esl.enter_context(nc.sbuf_tensor(uname("Wo"), [P, KC, D], BF16))
                dr["Wo"] = (Wo, load_wout(nc, tk, dr, Wo))
                emit_phase_B(nc, tk, ps, psb, dr, l, not last)
                emit_phase_C(nc, tk, ps, psb, dr, l, not last)
            if not last:
                tk.collective("AllGather", PAIRS, it["hx"], it["hxall"],
                              reads=[bufs["b_hx"]], writes=[bufs["b_hxall"]])
                tk.dma("sp", it["X1"][COL_HL:COL_HL + HALO, :], it["hxall"][HALO:2 * HALO, :],
                       reads=[bufs["b_hxall"]], writes=[bufs["b_x1"]])
                tk.dma("sp", it["X1"][COL_HR:COL_HR + HALO, :], it["hxall"][2 * HALO:3 * HALO, :],
                       reads=[bufs["b_hxall"]], writes=[bufs["b_x1"]])
                tk.barrier()
        tk.barrier()
    return nc


_FUSED = []


def kernel_fused(**inputs):
    inp = {k: np.asarray(v) for k, v in inputs.items()}
    cores = list(range(NCORES))
    lws = [host_layer_weights(l, *[inp[k] for k in WKEYS]) for l in range(DEPTH)]
    in_maps = []
    for c in cores:
        b = c // 2
        cc = host_core_consts(c)
        m = {"xin0": make_xin(c, inp["x"][b], inp["ctx"][b]),
             "cvec": np.ascontiguousarray(
                 np.stack([inp["c"][b], inp["c_ctx"]], -1).reshape(KC, P, 2).transpose(1, 0, 2))}
        for k in ("cosT", "sinT", "pm", "ident", "masks", "poolrc"):
            m[k] = cc[k]
        for l in range(DEPTH):
            for name, _, _ in LAYER_INPUTS:
                if name == "wmodH":
                    m["wmodH%d" % l] = lws[l]["wmod"][6 * (c % 2):6 * (c % 2) + 6]
                else:
                    m["%s%d" % (name, l)] = lws[l][name]
        in_maps.append(m)
    if not _FUSED:
        _FUSED.append(build_fused_program())
    res = run_bass_kernel_spmd(_FUSED[0], in_maps, core_ids=cores).results
    out = np.stack([np.concatenate([np.asarray(res[2 * b]["out"]), np.asarray(res[2 * b + 1]["out"])], 0)
                    for b in range(BATCH)], 0)
    return np.ascontiguousarray(out.astype(np.float32))


def kernel(**inputs):
    return kernel_fused(**inputs)
```

```python
import math
from contextlib import ExitStack

import numpy as np
import ml_dtypes

import concourse.bass as bass
import concourse.mybir as mybir
from concourse.bass_utils import run_bass_kernel_spmd

F32 = mybir.dt.float32
BF16 = mybir.dt.bfloat16
AF = mybir.ActivationFunctionType
ALU = mybir.AluOpType

P = 128
D = 2048
KC = 16
DEPTH = 2
BATCH = 4
SEQ = 4096
NCORES = 8
TL = 2048
HALO = 8
CTXN = 256
NT = TL + 2 * HALO + CTXN
COL_HL = TL
COL_HR = TL + HALO
COL_CTX = TL + 2 * HALO
PADW = 2336
PAD_HL, PAD_LAT, PAD_HR, PAD_Z1, PAD_CTX, PAD_Z2 = 0, 8, 2056, 2064, 2072, 2328
NQ = TL + CTXN
NKEY = CTXN + 2 * TL
NKT = NKEY // P
GRID_W = 64
N_HEADS = 8
POOL_WINDOWS = (2, 4, 8, 16)
ROPE_THETA = 10000.0
LN_EPS = 1e-5
RMS_EPS = 1e-5
ALPHA = (2 * DEPTH) ** 0.25
ATTN_SCALE = 64 ** -0.5
D_IN = 7168
NBLK = 48


def lam_init_of(l):
    return 0.8 - 0.6 * math.exp(-0.3 * l)


class Buf:
    __slots__ = ("name", "writers", "readers", "exclusive", "accum")

    def __init__(self, name="", exclusive=False, accum=False):
        self.name = name
        self.writers = {}
        self.readers = {}
        self.exclusive = exclusive
        self.accum = accum


class EngState:
    def __init__(self, name, eng, sem, semidx):
        self.name = name
        self.eng = eng
        self.sem = sem
        self.semidx = semidx
        self.count = 0
        self.waited = {}


class Tracker:
    def __init__(self, nc, es, n_dma_sems=12):
        self.nc = nc
        self.sems = []
        self.E = {}
        for name, eng in (("pe", nc.tensor), ("act", nc.scalar), ("dve", nc.vector),
                          ("pool", nc.gpsimd), ("sp", nc.sync)):
            sem = es.enter_context(nc.semaphore("c_" + name))
            self.sems.append(sem)
            self.E[name] = EngState(name, eng, sem, len(self.sems) - 1)
        self.dq = {}
        for q in ("sp", "pool"):
            lst = []
            for i in range(n_dma_sems):
                sem = es.enter_context(nc.semaphore("d_%s%d" % (q, i)))
                self.sems.append(sem)
                lst.append([len(self.sems) - 1, 0])
            self.dq[q] = {"sems": lst, "next": 0}
        sem = es.enter_context(nc.semaphore("cc_sem"))
        self.sems.append(sem)
        self.cc_sem = len(self.sems) - 1
        self.cc_count = 0

    def _wait(self, E, deps):
        for si, val in deps.items():
            if si == E.semidx and (E.name == "pe" or val > E.count):
                continue
            if E.waited.get(si, 0) >= val:
                continue
            E.eng.wait_ge(self.sems[si], val)
            E.waited[si] = val

    @staticmethod
    def _deps(reads, writes, own=None):
        deps = {}
        for b in reads:
            for si, v in b.writers.items():
                if deps.get(si, 0) < v:
                    deps[si] = v
            if b.exclusive:
                for si, v in b.readers.items():
                    if si != own and deps.get(si, 0) < v:
                        deps[si] = v
        for b in writes:
            srcs = () if (b.accum and not b.readers) else (b.writers, b.readers)
            for d in srcs:
                for si, v in d.items():
                    if deps.get(si, 0) < v:
                        deps[si] = v
        return deps

    @staticmethod
    def _record(reads, writes, si, val):
        for b in reads:
            if b.readers.get(si, 0) < val:
                b.readers[si] = val
        for b in writes:
            if b.accum and not b.readers:
                if b.writers.get(si, 0) < val:
                    b.writers[si] = val
            else:
                b.writers = {si: val}
                b.readers = {}

    def op(self, engname, fn, reads=(), writes=(), inc=True):
        E = self.E[engname]
        self._wait(E, self._deps(reads, writes, E.semidx))
        inst = fn(E.eng)
        if inc:
            E.count += 1
            inst.then_inc(E.sem, 1)
            val = E.count
        else:
            val = E.count + 1
        self._record(reads, writes, E.semidx, val)
        return inst

    def dma(self, q, out, in_, reads=(), writes=()):
        E = self.E[q]
        dq = self.dq[q]
        slot = dq["sems"][dq["next"]]
        dq["next"] = (dq["next"] + 1) % len(dq["sems"])
        deps = self._deps(reads, writes)
        if slot[1] > 0:
            deps[slot[0]] = max(deps.get(slot[0], 0), slot[1])
        self._wait(E, deps)
        slot[1] += 16
        E.eng.dma_start(out=out, in_=in_).then_inc(self.sems[slot[0]], 16)
        self._record(reads, writes, slot[0], slot[1])

    def collective(self, kind, groups, in_ap, out_ap, reads=(), writes=()):
        E = self.E["pool"]
        if self.cc_sem is None:
            raise RuntimeError("no cc semaphore")
        self._wait(E, self._deps(reads, writes))
        self.cc_count += 1
        E.eng.collective_compute(kind, ALU.bypass, replica_groups=groups,
                                 ins=[in_ap.opt()], outs=[out_ap.opt()]).then_inc(self.sems[self.cc_sem])
        self._record(reads, writes, self.cc_sem, self.cc_count)

    def barrier(self):
        deps = {}
        for E in self.E.values():
            if E.count > 0:
                deps[E.semidx] = E.count
        for dq in self.dq.values():
            for si, v in dq["sems"]:
                if v > 0:
                    deps[si] = v
        if self.cc_count > 0:
            deps[self.cc_sem] = self.cc_count
        for E in self.E.values():
            self._wait(E, deps)


_UNAME = [0]


def uname(name):
    _UNAME[0] += 1
    return "sb_%s_%d" % (name, _UNAME[0])


class Ring:
    def __init__(self, nc, es, name, n, shape, dtype):
        self.items = []
        for i in range(n):
            t = es.enter_context(nc.sbuf_tensor(uname("%s%d" % (name, i)), shape, dtype))
            self.items.append((t, Buf("%s%d" % (name, i))))
        self.i = 0

    def next(self):
        it = self.items[self.i]
        self.i = (self.i + 1) % len(self.items)
        return it


def _tile_w(wcols):
    C = wcols.shape[1]
    return np.ascontiguousarray(wcols.reshape(KC, P, C).transpose(1, 0, 2))


def block_col0():
    cols = []
    for h in range(8):
        cols.append(3072 + h * 128)
    for h in range(8):
        cols.append(2048 + h * 128)
        cols.append(5120 + h * 128)
    for j in range(4):
        cols += [j * 128, 1024 + j * 128, 512 + j * 128, 1536 + j * 128]
    for g in range(4):
        cols += [6144 + g * 128, 6656 + g * 128]
    return cols


def host_layer_weights(l, w_mod, b_mod, w_in, conv_w, lam_q1, lam_k1, lam_q2, lam_k2,
                       subln_g, pool_w, pool_scale, w_out, ln_g, ln_b):
    d = {}
    wm = w_mod[l]
    d["wmod"] = np.ascontiguousarray(
        wm.reshape(KC, P, 12, 512).transpose(2, 1, 0, 3))
    d["bmod"] = np.ascontiguousarray(np.stack([b_mod[l], b_mod[l]], 0))
    wi = w_in[l]
    cols = block_col0()
    d["wA"] = np.ascontiguousarray(
        np.stack([_tile_w(wi[:, c0:c0 + 128]) for c0 in cols], 0))
    d["wV"] = np.ascontiguousarray(
        np.stack([_tile_w(wi[:, 4096 + v * 512:4096 + (v + 1) * 512]) for v in range(2)], 0))
    d["convw"] = np.ascontiguousarray(conv_w[l].reshape(3, 4, P).transpose(2, 1, 0))
    d["poolw"] = np.ascontiguousarray(pool_w[l].transpose(1, 0, 2))
    d["pscale"] = np.ascontiguousarray(pool_scale[l].reshape(4, P).T)
    d["wout"] = _tile_w(w_out[l])
    d["lng"] = np.ascontiguousarray(ln_g[l].reshape(1, D))
    d["lnb"] = np.ascontiguousarray(ln_b[l].reshape(1, D))
    d["lamv"] = np.ascontiguousarray(
        np.stack([lam_q1[l], lam_k1[l], lam_q2[l], lam_k2[l]], 0).reshape(1, 256))
    d["subg"] = np.ascontiguousarray(subln_g[l].reshape(P, 1))
    return d


def host_core_consts(core):
    s = core % 2
    d = {}
    t = s * TL + np.arange(TL)
    rows = (t // GRID_W).astype(np.float64)
    colsp = (t % GRID_W).astype(np.float64)
    inv = ROPE_THETA ** (-np.arange(16, dtype=np.float64) / 16)
    cosT = np.zeros((P, TL), np.float64)
    sinT = np.zeros((P, TL), np.float64)
    for i in range(P):
        dd = i % 64
        pos = rows if dd < 32 else colsp
        e = dd % 32
        f = e % 16
        ang = (pos.astype(np.float32) * np.float32(inv[f])).astype(np.float64)
        cosT[i] = np.cos(ang)
        sinT[i] = np.sin(ang) * (-1.0 if e < 16 else 1.0)
    d["cosT"] = cosT.astype(np.float32)
    d["sinT"] = sinT.astype(np.float32)
    pm = np.zeros((P, P), np.float32)
    for i in range(P):
        e = i % 32
        j = i + 16 if e < 16 else i - 16
        pm[j, i] = 1.0
    d["pm"] = pm
    d["ident"] = np.eye(P, dtype=np.float32)
    d["masks"] = np.tile(np.array([[1.0 if s == 1 else 0.0, 1.0 if s == 0 else 0.0]], np.float32), (P, 1))
    rc = np.zeros((4, 4, 8), np.float32)
    for g, w in enumerate(POOL_WINDOWS):
        def cnt(tt, T):
            lo = np.clip(tt - w // 2, 0, T)
            hi = np.clip(tt + w - w // 2, 0, T)
            return (hi - lo).astype(np.float32)
        rc[g, 0] = 1.0 / cnt(s * TL + np.arange(8), SEQ)
        rc[g, 1] = 1.0 / cnt(s * TL + TL - 8 + np.arange(8), SEQ)
        rc[g, 2] = 1.0 / cnt(np.arange(8), CTXN)
        rc[g, 3] = 1.0 / cnt(CTXN - 8 + np.arange(8), CTXN)
    d["poolrc"] = np.ascontiguousarray(np.tile(rc.reshape(1, 128), (P, 1)))
    return d


def seg_tiles(with_ctx, with_halo):
    tiles = [(512 * t, 512, [(0, 512, "lat", 512 * t)]) for t in range(4)]
    if with_ctx and with_halo:
        tiles.append((TL, 272, [(0, 8, "hl", 0), (8, 8, "hr", 0), (16, 256, "ctx", 0)]))
    elif with_ctx:
        tiles.append((COL_CTX, 256, [(0, 256, "ctx", 0)]))
    elif with_halo:
        tiles.append((TL, 16, [(0, 8, "hl", 0), (8, 8, "hr", 0)]))
    return tiles


def pad_col(region, off):
    return {"lat": PAD_LAT, "hl": PAD_HL, "hr": PAD_HR, "ctx": PAD_CTX}[region] + off


def q_col(region, off):
    return {"lat": 0, "ctx": TL}[region] + off


def emit_phase_A(nc, tk, ps, psb, dr, layer, full_ctx, stop_after=None):
    es = ExitStack()
    with es:
        sb = lambda name, shape, dt: es.enter_context(nc.sbuf_tensor(uname(name), shape, dt))
        hT = sb("hT", [P, KC, NT], BF16)
        hbuf = {}

        def hb(tile_i, k):
            key = (tile_i, k)
            if key not in hbuf:
                hbuf[key] = Buf("hT%d_%d" % key)
            return hbuf[key]

        def hT_bufs(col0, n, k):
            out = []
            c = col0
            while c < col0 + n:
                if c < TL:
                    ti = c // 128
                    nxt = (ti + 1) * 128
                elif c < COL_CTX:
                    ti = 16
                    nxt = COL_CTX
                else:
                    ti = 17 + (c - COL_CTX) // 128
                    nxt = COL_CTX + (ti - 16) * 128
                out.append(hb(ti, k))
                c = nxt
            return out

        ident = sb("ident", [P, P], F32)
        b_ident = Buf("ident")
        modT = sb("modT", [P, 48, 2], F32)
        sc1T = sb("sc1T", [P, 16, 2], F32)
        b_mod = Buf("modT")
        tk.dma("sp", ident[:], dr["ident"][:, :], writes=[b_ident])

        with ExitStack() as es2:
            sb2 = lambda name, shape, dt: es2.enter_context(nc.sbuf_tensor(uname(name), shape, dt))
            bmod = sb2("bmodsb", [2, 6144], F32)
            b_bmod = Buf("bmod")
            modsb = sb2("modsb", [2, 6144], F32)
            b_modsb = Buf("modsb")
            tk.dma("sp", bmod[:], dr["bmod"][:, :], writes=[b_bmod])
            if dr.get("mod_dist"):
                md = dr["mod_dist"]
                csb = sb2("csb", [P, KC, 2], F32)
                b_csb = Buf("csb")
                msl = sb2("msl", [2, 3072], F32)
                b_msl = Buf("msl")
                wm_ring = Ring(nc, es2, "wm", 2, [P, KC, 512], F32)
                tk.dma("sp", csb[:], dr["cvec"][:, :, :], writes=[b_csb])
                tk.op("act", lambda e: e.activation(out=csb[:], in_=csb[:], func=AF.Silu),
                      reads=[b_csb], writes=[b_csb])
                for ct in range(6):
                    wm, b_wm = wm_ring.next()
                    tk.dma("sp", wm[:], dr["wmodH"][ct], writes=[b_wm])
                    bank = ct % 2
                    for k in range(KC):
                        tk.op("pe", lambda e, k=k, wm=wm, bank=bank: e.matmul(
                            ps[bank][0:2, :], csb[:, k, :], wm[:, k, :], start=(k == 0), stop=(k == KC - 1)),
                            reads=[b_csb, b_wm], writes=[psb[bank]], inc=(k == KC - 1))
                    tk.op("dve", lambda e, ct=ct, bank=bank: e.tensor_copy(
                        msl[0:2, ct * 512:(ct + 1) * 512], ps[bank][0:2, :]),
                        reads=[psb[bank]], writes=[b_msl])
                tk.dma("sp", md["part"][:, :], msl[:], reads=[b_msl], writes=[md["b_part"]])
                tk.collective("AllGather", PAIRS, md["part"], md["all"],
                              reads=[md["b_part"]], writes=[md["b_all"]])
                b_modsb.accum = True
                for r in range(2):
                    tk.dma("sp", modsb[0:2, r * 3072:(r + 1) * 3072], md["all"][2 * r:2 * r + 2, :],
                           reads=[md["b_all"]], writes=[b_modsb])
                tk.op("dve", lambda e: e.tensor_tensor(modsb[:], modsb[:], bmod[:], op=ALU.add),
                      reads=[b_modsb, b_bmod], writes=[b_modsb])
            else:
                csb = sb2("csb", [P, KC, 2], F32)
                b_csb = Buf("csb")
                wm_ring = Ring(nc, es2, "wm", 2, [P, KC, 512], F32)
                tk.dma("sp", csb[:], dr["cvec"][:, :, :], writes=[b_csb])
                tk.op("act", lambda e: e.activation(out=csb[:], in_=csb[:], func=AF.Silu),
                      reads=[b_csb], writes=[b_csb])
                for ct in range(12):
                    wm, b_wm = wm_ring.next()
                    tk.dma("sp", wm[:], dr["wmod"][ct], writes=[b_wm])
                    bank = ct % 2
                    for k in range(KC):
                        tk.op("pe", lambda e, k=k, wm=wm, bank=bank: e.matmul(
                            ps[bank][0:2, :], csb[:, k, :], wm[:, k, :], start=(k == 0), stop=(k == KC - 1)),
                            reads=[b_csb, b_wm], writes=[psb[bank]], inc=(k == KC - 1))
                    tk.op("dve", lambda e, ct=ct, bank=bank: e.tensor_tensor(
                        modsb[0:2, ct * 512:(ct + 1) * 512], ps[bank][0:2, :],
                        bmod[0:2, ct * 512:(ct + 1) * 512], op=ALU.add),
                        reads=[psb[bank], b_bmod], writes=[b_modsb])
            tk.dma("sp", dr["modv"][:, :], modsb[:], reads=[b_modsb], writes=[dr["b_modv"]])
            for i in range(48):
                tk.op("pe", lambda e, i=i: e.transpose(
                    ps[2][:, 2 * i:2 * i + 2], modsb[0:2, i * 128:(i + 1) * 128], ident[0:2, 0:2]),
                    reads=[b_modsb, b_ident], writes=[psb[2]], inc=(i == 47))
            tk.op("dve", lambda e: e.tensor_copy(modT[:].rearrange("p a b -> p (a b)"), ps[2][:, 0:96]),
                  reads=[psb[2]], writes=[b_mod])
            tk.op("dve", lambda e: e.tensor_scalar(
                sc1T[:].rearrange("p a b -> p (a b)"),
                modT[:, 16:32, :].rearrange("p a b -> p (a b)"), 1.0, None, op0=ALU.add),
                reads=[b_mod], writes=[b_mod])
            tk.barrier()
        if stop_after == "M":
            return

        with ExitStack() as es2:
            xs_ring = Ring(nc, es2, "xs", 2, [P, D], F32)
            row_tiles = [(i * 128, 128, i * 128, 0) for i in range(16)]
            row_tiles.append((TL, 16, TL, 0))
            row_tiles += [(TL + 16 + i * 128, 128, COL_CTX + i * 128, 1) for i in range(2)]
            ev = 0
            for ti, (r0, nr, c0, mc) in enumerate(row_tiles):
                xs, b_xs = xs_ring.next()
                tk.dma("sp", xs[0:nr, :], dr["xin"][r0:r0 + nr, :], reads=[dr["b_xin"]], writes=[b_xs])
                for jq in range(4):
                    bank = 4 + (jq % 2)
                    for q in range(4):
                        j = jq * 4 + q
                        tk.op("pe", lambda e, xs=xs, j=j, q=q, nr=nr, bank=bank: e.transpose(
                            ps[bank][:, q * nr:(q + 1) * nr], xs[0:nr, j * 128:(j + 1) * 128],
                            ident[0:nr, 0:nr]),
                            reads=[b_xs, b_ident], writes=[psb[bank]], inc=(q == 3))
                    for q in range(4):
                        j = jq * 4 + q
                        src = ps[bank][:, q * nr:(q + 1) * nr]
                        dst = hT[:, j, c0:c0 + nr]
                        if jq % 2 == 0:
                            tk.op("act", lambda e, src=src, dst=dst, j=j, mc=mc: e.activation(
                                out=dst, in_=src, func=AF.Identity,
                                scale=sc1T[:, j, mc:mc + 1], bias=modT[:, j, mc:mc + 1]),
                                reads=[psb[bank], b_mod], writes=[hb(ti, j)])
                        else:
                            tk.op("dve", lambda e, src=src, dst=dst, j=j, mc=mc: e.tensor_scalar(
                                dst, src, sc1T[:, j, mc:mc + 1], modT[:, j, mc:mc + 1],
                                op0=ALU.mult, op1=ALU.add),
                                reads=[psb[bank], b_mod], writes=[hb(ti, j)])
                        ev += 1
            tk.barrier()
        if stop_after == "A0":
            return

        w_ring = Ring(nc, es, "wr", 4, [P, KC, 128], BF16)
        w_slots = {}

        def load_w(blk):
            if blk >= NBLK or blk in w_slots:
                return
            wt, b_wt = w_ring.next()
            tk.dma("pool", wt[:], dr["wA"][blk], writes=[b_wt])
            w_slots[blk] = (wt, b_wt)

        proj_bank = [0]

        def project(blk, col0, ncols):
            wt, b_wt = w_slots[blk]
            bank = proj_bank[0]
            proj_bank[0] = (bank + 1) % 4
            for k in range(KC):
                tk.op("pe", lambda e, k=k: e.matmul(
                    ps[bank][:, 0:ncols], wt[:, k, :], hT[:, k, col0:col0 + ncols],
                    start=(k == 0), stop=(k == KC - 1)),
                    reads=[b_wt] + hT_bufs(col0, ncols, k), writes=[psb[bank]], inc=(k == KC - 1))
            return bank

        for blk in range(3):
            load_w(blk)

        with ExitStack() as es2:
            sb2 = lambda name, shape, dt: es2.enter_context(nc.sbuf_tensor(uname(name), shape, dt))
            cosT = sb2("cosT", [P, TL], F32)
            sinT = sb2("sinT", [P, TL], F32)
            pm = sb2("pm", [P, P], BF16)
            b_tab = Buf("tabs", accum=True)
            tk.dma("sp", cosT[:], dr["cosT"][:, :], writes=[b_tab])
            tk.dma("sp", sinT[:], dr["sinT"][:, :], writes=[b_tab])
            tk.dma("pool", pm[:], dr["pm"][:, :], writes=[b_tab])
            kb_ring = Ring(nc, es2, "kb", 2, [P, 512], BF16)
            t1_ring = Ring(nc, es2, "t1", 2, [P, 512], F32)
            t2_ring = Ring(nc, es2, "t2", 2, [P, 512], F32)
            ko_ring = Ring(nc, es2, "ko", 3, [P, 512], BF16)
            rope_bank = [0]

            def rope_block(blk, dst_lat, dst_ctx, b_dst, tiles):
                for (c0, n, segs) in tiles:
                    bank = project(blk, c0, n)
                    for (p0, sn, region, off) in segs:
                        if region == "lat":
                            kb, b_kb = kb_ring.next()
                            tk.op("act", lambda e, kb=kb: e.copy(out=kb[:, 0:sn], in_=ps[bank][:, p0:p0 + sn]),
                                  reads=[psb[bank]], writes=[b_kb])
                            rb = 4 + rope_bank[0]
                            rope_bank[0] = (rope_bank[0] + 1) % 2
                            tk.op("pe", lambda e, kb=kb, rb=rb: e.matmul(
                                ps[rb][:, 0:sn], pm[:], kb[:, 0:sn], start=True, stop=True),
                                reads=[b_kb, b_tab], writes=[psb[rb]])
                            t1, b_t1 = t1_ring.next()
                            t2, b_t2 = t2_ring.next()
                            tk.op("dve", lambda e, t1=t1: e.tensor_tensor(
                                t1[:, 0:sn], ps[bank][:, p0:p0 + sn], cosT[:, off:off + sn], op=ALU.mult),
                                reads=[psb[bank], b_tab], writes=[b_t1])
                            tk.op("dve", lambda e, t2=t2, rb=rb: e.tensor_tensor(
                                t2[:, 0:sn], ps[rb][:, 0:sn], sinT[:, off:off + sn], op=ALU.mult),
                                reads=[psb[rb], b_tab], writes=[b_t2])
                            ko, b_ko = ko_ring.next()
                            tk.op("dve", lambda e, t1=t1, t2=t2, ko=ko: e.tensor_tensor(
                                ko[:, 0:sn], t1[:, 0:sn], t2[:, 0:sn], op=ALU.add),
                                reads=[b_t1, b_t2], writes=[b_ko])
                            tk.dma("sp", dst_lat[:, off:off + sn], ko[:, 0:sn], reads=[b_ko], writes=[b_dst])
                        elif region == "ctx":
                            ko, b_ko = ko_ring.next()
                            tk.op("act", lambda e, ko=ko: e.copy(out=ko[:, 0:sn], in_=ps[bank][:, p0:p0 + sn]),
                                  reads=[psb[bank]], writes=[b_ko])
                            tk.dma("sp", dst_ctx[:, off:off + sn], ko[:, 0:sn], reads=[b_ko], writes=[b_dst])

            k_tiles = seg_tiles(True, False)
            for h in range(8):
                load_w(h + 3)
                rope_block(h, dr["KTg"][h], dr["KTc"][h], dr["b_kv"], k_tiles)

            if stop_after == "K":
                tk.barrier()
                return
            sg_ring = Ring(nc, es2, "sg", 3, [P, 512], BF16)
            with ExitStack() as es3:
                wv_ring = Ring(nc, es3, "wv", 2, [P, KC, 512], BF16)
                vs_ring = Ring(nc, es3, "vs", 3, [P, 512], BF16)
                wvs = []
                for vb in range(2):
                    wv, b_wv = wv_ring.next()
                    tk.dma("pool", wv[:], dr["wV"][vb], writes=[b_wv])
                    wvs.append((wv, b_wv))
                vtiles = [(i * 128, dr["Vg"], i * 128) for i in range(16)]
                vtiles += [(COL_CTX + i * 128, dr["Vc"], i * 128) for i in range(2)]
                ev = 0
                for vb in range(2):
                    wv, b_wv = wvs[vb]
                    for (c0, dst, r0) in vtiles:
                        bank = proj_bank[0]
                        proj_bank[0] = (bank + 1) % 4
                        for k in range(KC):
                            tk.op("pe", lambda e, k=k, c0=c0, bank=bank, wv=wv: e.matmul(
                                ps[bank][:, :], hT[:, k, c0:c0 + 128], wv[:, k, :],
                                start=(k == 0), stop=(k == KC - 1)),
                                reads=[b_wv] + hT_bufs(c0, 128, k), writes=[psb[bank]], inc=(k == KC - 1))
                        vs, b_vs = vs_ring.next()
                        if ev % 2 == 0:
                            tk.op("act", lambda e, vs=vs, bank=bank: e.copy(out=vs[:], in_=ps[bank][:, :]),
                                  reads=[psb[bank]], writes=[b_vs])
                        else:
                            tk.op("dve", lambda e, vs=vs, bank=bank: e.tensor_copy(vs[:], ps[bank][:, :]),
                                  reads=[psb[bank]], writes=[b_vs])
                        ev += 1
                        tk.dma("sp", dst[r0:r0 + 128, vb * 512:(vb + 1) * 512], vs[:],
                               reads=[b_vs], writes=[dr["b_kv"]])
            if dr.get("after_kv") is not None:
                dr["after_kv"]()

            if stop_after == "V":
                return
            q_tiles = seg_tiles(full_ctx, False)
            for h in range(8):
                bq = 8 + 2 * h
                load_w(bq + 3)
                rope_block(bq, dr["QT"][h], dr["QT"][h][:, TL:NQ], dr["b_q"], q_tiles)
                load_w(bq + 4)
                for (c0, n, segs) in q_tiles:
                    bank = project(bq + 1, c0, n)
                    for (p0, sn, region, off) in segs:
                        sg, b_sg = sg_ring.next()
                        tk.op("act", lambda e, sg=sg, bank=bank: e.activation(
                            out=sg[:, 0:sn], in_=ps[bank][:, p0:p0 + sn], func=AF.Silu),
                            reads=[psb[bank]], writes=[b_sg])
                        qc = q_col(region, off)
                        tk.dma("sp", dr["SAG"][h][:, qc:qc + sn], sg[:, 0:sn], reads=[b_sg], writes=[dr["b_q"]])
            tk.barrier()

        if stop_after == "Q":
            return
        with ExitStack() as es2:
            sb2 = lambda name, shape, dt: es2.enter_context(nc.sbuf_tensor(uname(name), shape, dt))
            U = sb2("U", [P, PADW], F32)
            A = sb2("A", [P, PADW], F32)
            B = sb2("B", [P, PADW], F32)
            DT = sb2("DT", [P, PADW], BF16)
            b_U, b_A, b_B, b_DT = Buf("U"), Buf("A"), Buf("B"), Buf("DT")
            convw = sb2("convw", [P, 4, 3], F32)
            masks = sb2("masks", [P, 2], F32)
            poolrc = sb2("poolrc", [P, 128], F32)
            pscale = sb2("pscale", [P, 4], F32)
            poolw = sb2("poolw", [P, 4, 128], BF16)
            b_cst = Buf("cst", accum=True)
            tk.dma("sp", convw[:], dr["convw"][:, :, :], writes=[b_cst])
            tk.dma("sp", masks[:], dr["masks"][:, :], writes=[b_cst])
            tk.dma("sp", poolrc[:], dr["poolrc"][:, :], writes=[b_cst])
            tk.dma("sp", pscale[:], dr["pscale"][:, :], writes=[b_cst])
            tk.dma("pool", poolw[:], dr["poolw"][:, :, :], writes=[b_cst])
            tk.op("dve", lambda e: e.memset(U[:], 0.0), writes=[b_U])
            tk.op("dve", lambda e: e.memset(A[:], 0.0), writes=[b_A])
            tk.op("dve", lambda e: e.memset(B[:], 0.0), writes=[b_B])
            sf_ring = Ring(nc, es2, "sf", 3, [P, 512], F32)
            yo_ring = Ring(nc, es2, "yo", 3, [P, 512], BF16)
            c_tiles = seg_tiles(full_ctx, True)

            def mask_halos(T, b_T):
                tk.op("dve", lambda e: e.tensor_scalar(T[:, PAD_HL:PAD_HL + 8], T[:, PAD_HL:PAD_HL + 8],
                                                       masks[:, 0:1], None, op0=ALU.mult),
                      reads=[b_cst], writes=[b_T])
                tk.op("dve", lambda e: e.tensor_scalar(T[:, PAD_HR:PAD_HR + 8], T[:, PAD_HR:PAD_HR + 8],
                                                       masks[:, 1:2], None, op0=ALU.mult),
                      reads=[b_cst], writes=[b_T])

            def y_out(chunk, bank, p0, sn, region, off, mulsrc, b_mulsrc, scale_ap=None):
                yo, b_yo = yo_ring.next()
                if scale_ap is None:
                    tk.op("dve", lambda e: e.tensor_tensor(
                        yo[:, 0:sn], ps[bank][:, p0:p0 + sn], mulsrc, op=ALU.mult),
                        reads=[psb[bank], b_mulsrc], writes=[b_yo])
                else:
                    tk.op("dve", lambda e: e.scalar_tensor_tensor(
                        yo[:, 0:sn], ps[bank][:, p0:p0 + sn], scale_ap, mulsrc, op0=ALU.mult, op1=ALU.mult),
                        reads=[psb[bank], b_mulsrc, b_cst], writes=[b_yo])
                qc = q_col(region, off)
                tk.dma("sp", dr["YT"][chunk][:, qc:qc + sn], yo[:, 0:sn], reads=[b_yo], writes=[dr["b_y"]])

            for j in range(4):
                b0 = 24 + 4 * j
                load_w(b0 + 3)
                for (c0, n, segs) in c_tiles:
                    bank = project(b0, c0, n)
                    for (p0, sn, region, off) in segs:
                        pc = pad_col(region, off)
                        tk.op("act", lambda e, pc=pc: e.copy(out=U[:, pc:pc + sn], in_=ps[bank][:, p0:p0 + sn]),
                              reads=[psb[bank]], writes=[b_U])
                load_w(b0 + 4)
                for (c0, n, segs) in c_tiles:
                    bank = project(b0 + 1, c0, n)
                    for (p0, sn, region, off) in segs:
                        pc = pad_col(region, off)
                        tk.op("dve", lambda e, pc=pc: e.tensor_tensor(
                            U[:, pc:pc + sn], ps[bank][:, p0:p0 + sn], U[:, pc:pc + sn], op=ALU.mult),
                            reads=[psb[bank]], writes=[b_U])
                mask_halos(U, b_U)
                tk.op("dve", lambda e: e.tensor_scalar(A[:, 1:PADW - 1], U[:, 1:PADW - 1],
                                                       convw[:, j, 1:2], None, op0=ALU.mult),
                      reads=[b_U, b_cst], writes=[b_A])
                tk.op("dve", lambda e: e.scalar_tensor_tensor(
                    A[:, 1:PADW - 1], U[:, 0:PADW - 2], convw[:, j, 0:1], A[:, 1:PADW - 1],
                    op0=ALU.mult, op1=ALU.add), reads=[b_U, b_cst], writes=[b_A])
                tk.op("dve", lambda e: e.scalar_tensor_tensor(
                    A[:, 1:PADW - 1], U[:, 2:PADW], convw[:, j, 2:3], A[:, 1:PADW - 1],
                    op0=ALU.mult, op1=ALU.add), reads=[b_U, b_cst], writes=[b_A])
                load_w(b0 + 5)
                for (c0, n, segs) in c_tiles:
                    bank = project(b0 + 2, c0, n)
                    for (p0, sn, region, off) in segs:
                        if region in ("hl", "hr"):
                            continue
                        pc = pad_col(region, off)
                        tk.op("dve", lambda e, pc=pc: e.tensor_tensor(
                            A[:, pc:pc + sn], ps[bank][:, p0:p0 + sn], A[:, pc:pc + sn], op=ALU.mult),
                            reads=[psb[bank]], writes=[b_A])
                load_w(b0 + 6)
                for (c0, n, segs) in c_tiles:
                    bank = project(b0 + 3, c0, n)
                    for (p0, sn, region, off) in segs:
                        if region in ("hl", "hr"):
                            continue
                        pc = pad_col(region, off)
                        sf, b_sf = sf_ring.next()
                        tk.op("act", lambda e, sf=sf: e.activation(
                            out=sf[:, 0:sn], in_=ps[bank][:, p0:p0 + sn], func=AF.Silu),
                            reads=[psb[bank]], writes=[b_sf])
                        yo, b_yo = yo_ring.next()
                        tk.op("dve", lambda e, sf=sf, yo=yo, pc=pc: e.tensor_tensor(
                            yo[:, 0:sn], sf[:, 0:sn], A[:, pc:pc + sn], op=ALU.mult),
                            reads=[b_sf, b_A], writes=[b_yo])
                        qc = q_col(region, off)
                        tk.dma("sp", dr["YT"][j][:, qc:qc + sn], yo[:, 0:sn], reads=[b_yo], writes=[dr["b_y"]])

            for g in range(4):
                w = POOL_WINDOWS[g]
                b0 = 40 + 2 * g
                load_w(b0 + 3)
                for (c0, n, segs) in c_tiles:
                    bank = project(b0, c0, n)
                    for (p0, sn, region, off) in segs:
                        pc = pad_col(region, off)
                        tk.op("act", lambda e, pc=pc: e.copy(out=U[:, pc:pc + sn], in_=ps[bank][:, p0:p0 + sn]),
                              reads=[psb[bank]], writes=[b_U])
                mask_halos(U, b_U)
                src, b_src = U, b_U
                span = 1
                pp = [(A, b_A), (B, b_B)]
                pi = 0
                while span < w:
                    dst, b_dst = pp[pi]
                    pi ^= 1
                    n = PADW - span
                    tk.op("dve", lambda e, src=src, dst=dst, span=span, n=n: e.tensor_tensor(
                        dst[:, 0:n], src[:, 0:n], src[:, span:span + n], op=ALU.add),
                        reads=[b_src], writes=[b_dst])
                    src, b_src = dst, b_dst
                    span *= 2
                hw = w // 2
                tk.op("dve", lambda e, src=src, hw=hw, w=w: e.scalar_tensor_tensor(
                    DT[:, 8:PAD_Z2], src[:, 8 - hw:PAD_Z2 - hw], 1.0 / w, U[:, 8:PAD_Z2],
                    op0=ALU.mult, op1=ALU.subtract), reads=[b_src, b_U], writes=[b_DT])
                edges = [PAD_LAT, PAD_LAT + TL - 8, PAD_CTX, PAD_CTX + CTXN - 8]
                for ei, ec in enumerate(edges):
                    if ei >= 2 and not full_ctx:
                        continue
                    other, b_other = pp[pi]
                    ro = g * 32 + ei * 8
                    tk.op("dve", lambda e, src=src, other=other, ec=ec, ro=ro, hw=hw: e.tensor_tensor(
                        other[:, ec:ec + 8], src[:, ec - hw:ec - hw + 8], poolrc[:, ro:ro + 8], op=ALU.mult),
                        reads=[b_src, b_cst], writes=[b_other])
                    tk.op("dve", lambda e, other=other, ec=ec: e.tensor_tensor(
                        DT[:, ec:ec + 8], other[:, ec:ec + 8], U[:, ec:ec + 8], op=ALU.subtract),
                        reads=[b_other, b_U], writes=[b_DT])
                load_w(b0 + 4)
                for (c0, n, segs) in c_tiles:
                    segs2 = [sg_ for sg_ in segs if sg_[2] in ("lat", "ctx")]
                    if not segs2:
                        continue
                    bank = project(b0 + 1, c0, n)
                    for (p0, sn, region, off) in segs2:
                        pc = pad_col(region, off)
                        sf, b_sf = sf_ring.next()
                        tk.op("act", lambda e, sf=sf: e.activation(
                            out=sf[:, 0:sn], in_=ps[bank][:, p0:p0 + sn], func=AF.Silu),
                            reads=[psb[bank]], writes=[b_sf])
                        rb = 4 + (proj_bank[0] % 2)
                        tk.op("pe", lambda e, rb=rb, pc=pc: e.matmul(
                            ps[rb][:, 0:sn], poolw[:, g, :], DT[:, pc:pc + sn], start=True, stop=True),
                            reads=[b_DT, b_cst], writes=[psb[rb]])
                        y_out(12 + g, rb, 0, sn, region, off, sf[:, 0:sn], b_sf, scale_ap=pscale[:, g:g + 1])
            tk.barrier()
    tk.barrier()


def declare(nc, name, shape, dt, kind):
    return nc.dram_tensor(name, list(shape), dt, kind=kind).ap()


PHASE_A_INPUTS = [
    ("xin", (NT, D), F32), ("cvec", (P, KC, 2), F32), ("wmod", (12, P, KC, 512), F32),
    ("bmod", (2, 6144), F32), ("wA", (NBLK, P, KC, 128), F32), ("wV", (2, P, KC, 512), F32),
    ("cosT", (P, TL), F32), ("sinT", (P, TL), F32), ("pm", (P, P), F32), ("ident", (P, P), F32),
    ("convw", (P, 4, 3), F32), ("masks", (P, 2), F32), ("poolrc", (P, 128), F32),
    ("poolw", (P, 4, 128), F32), ("pscale", (P, 4), F32),
]
PHASE_A_OUTPUTS = [
    ("KTg", (8, P, TL), BF16), ("Vg", (TL, 1024), BF16), ("KTc", (8, P, CTXN), BF16),
    ("Vc", (CTXN, 1024), BF16), ("QT", (8, P, NQ), BF16), ("SAG", (8, P, NQ), BF16),
    ("YT", (16, P, NQ), BF16), ("modv", (2, 6144), F32),
]


def build_phase_A_program(layer, full_ctx, stop_after=None):
    nc = bass.Bass("TRN2", target_bir_lowering=False)
    dr = {}
    for name, shape, dt in PHASE_A_INPUTS:
        dr[name] = declare(nc, name, shape, dt, "ExternalInput")
    for name, shape, dt in PHASE_A_OUTPUTS:
        dr[name] = declare(nc, name, shape, dt, "ExternalOutput")
    for b in ("b_kv", "b_q", "b_y", "b_modv"):
        dr[b] = Buf(b, accum=True)
    dr["b_xin"] = Buf("b_xin")
    with ExitStack() as es:
        ps = [es.enter_context(nc.psum_tensor("ps%d" % i, [P, 512], F32)) for i in range(8)]
        psb = [Buf("ps%d" % i, exclusive=True) for i in range(8)]
        tk = Tracker(nc, es)
        emit_phase_A(nc, tk, ps, psb, dr, layer, full_ctx, stop_after)
    return nc


def emit_phase_B(nc, tk, ps, psb, dr, layer, with_ctx_q):
    lam_init = lam_init_of(layer)
    with ExitStack() as es:
        sb = lambda name, shape, dt: es.enter_context(nc.sbuf_tensor(uname(name), shape, dt))
        Vf = sb("Vf", [P, NKT, 1024], BF16)
        b_vf = [Buf("vf%d" % i) for i in range(9)]
        ones_bf = sb("ones_bf", [P, P], BF16)
        ones_f = sb("ones_f", [P, P], F32)
        epsb = sb("epsb", [P, 1], F32)
        lamb = sb("lamb", [P, 256], F32)
        lt = sb("lt", [P, 128], F32)
        sm = sb("sm", [P, 4], F32)
        neg_lam = sb("neg_lam", [P, 1], F32)
        gsub = sb("gsub", [P, 1], F32)
        b_c = Buf("constsB")
        tk.op("dve", lambda e: e.memset(ones_bf[:], 1.0), writes=[b_c])
        tk.op("dve", lambda e: e.memset(ones_f[:], 1.0 / 128.0), writes=[b_c])
        tk.op("dve", lambda e: e.memset(epsb[:], RMS_EPS), writes=[b_c])
        b_lam = Buf("lam")
        tk.dma("sp", lamb[:], dr["lamv"].partition_broadcast(P), writes=[b_lam])
        tk.dma("sp", gsub[:], dr["subg"][:, :], writes=[b_c])
        def vf_buf(kt):
            return b_vf[0] if kt < 2 else b_vf[1 + (kt - 2) // 4]

        tk.op("dve", lambda e: e.tensor_tensor(lt[:, 0:64], lamb[:, 0:64], lamb[:, 64:128], op=ALU.mult),
              reads=[b_lam], writes=[b_lam])
        tk.op("dve", lambda e: e.tensor_tensor(lt[:, 64:128], lamb[:, 128:192], lamb[:, 192:256], op=ALU.mult),
              reads=[b_lam], writes=[b_lam])
        tk.op("dve", lambda e: e.tensor_reduce(sm[:, 0:1], lt[:, 0:64], axis=mybir.AxisListType.X, op=ALU.add),
              reads=[b_lam], writes=[b_lam])
        tk.op("dve", lambda e: e.tensor_reduce(sm[:, 1:2], lt[:, 64:128], axis=mybir.AxisListType.X, op=ALU.add),
              reads=[b_lam], writes=[b_lam])
        tk.op("act", lambda e: e.activation(out=sm[:, 2:4], in_=sm[:, 0:2], func=AF.Exp),
              reads=[b_lam], writes=[b_lam])
        tk.op("dve", lambda e: e.tensor_tensor(neg_lam[:], sm[:, 3:4], sm[:, 2:3], op=ALU.subtract),
              reads=[b_lam], writes=[b_lam])
        tk.op("dve", lambda e: e.tensor_scalar(neg_lam[:], neg_lam[:], -lam_init, None, op0=ALU.add),
              reads=[b_lam], writes=[b_lam])
        tk.op("dve", lambda e: e.tensor_scalar(gsub[:], gsub[:], 1.0 - lam_init, None, op0=ALU.mult),
              reads=[b_c], writes=[b_c])

        kt_ring = Ring(nc, es, "KT", 2, [P, NKEY], BF16)
        q_ring = [Ring(nc, es, "Qp%d" % m, 2, [P, NQ], BF16) for m in range(2)]
        for m in range(2):
            for (t, b) in q_ring[m].items:
                tk.op("dve", lambda e, t=t: e.memset(t[:], 0.0), writes=[b])
        sag_ring = Ring(nc, es, "sagB", 2, [P, NQ], BF16)
        e_ring = Ring(nc, es, "E", 6, [P, 512], BF16)
        f_ring = Ring(nc, es, "fB", 8, [P, 512], F32)
        yo_ring = Ring(nc, es, "yoB", 2, [P, 512], BF16)

        qtiles = [(512 * t, 512, list(range(NKT))) for t in range(4)]
        if with_ctx_q:
            qtiles.append((TL, CTXN, [0, 1]))

        heads = {}

        def load_head(h):
            if h >= N_HEADS or h in heads:
                return
            KT, b_KT = kt_ring.next()
            b_KT.accum = True
            tk.dma("sp", KT[:, 0:CTXN], dr["KTc"][h], reads=[dr["b_kv"]], writes=[b_KT])
            for r in range(2):
                tk.dma("sp", KT[:, CTXN + r * TL:CTXN + (r + 1) * TL], dr["kt_src"](r, h),
                       reads=[dr["b_kvall"]], writes=[b_KT])
            Q0, b_Q0 = q_ring[0].next()
            Q1, b_Q1 = q_ring[1].next()
            tk.dma("sp", Q0[0:64, :], dr["QT"][h][0:64, :], reads=[dr["b_q"]], writes=[b_Q0])
            tk.dma("sp", Q1[64:128, :], dr["QT"][h][64:128, :], reads=[dr["b_q"]], writes=[b_Q1])
            SG, b_SG = sag_ring.next()
            tk.dma("sp", SG[:], dr["SAG"][h], reads=[dr["b_q"]], writes=[b_SG])
            heads[h] = (KT, b_KT, (Q0, Q1), (b_Q0, b_Q1), SG, b_SG)

        load_head(0)
        tk.dma("sp", Vf[:, 0:2, :], dr["Vc"].rearrange("(t p) c -> p t c", p=P), reads=[dr["b_kv"]], writes=[b_vf[0]])
        for r in range(2):
            for q4 in range(4):
                t0 = 2 + 16 * r + 4 * q4
                tk.dma("sp", Vf[:, t0:t0 + 4, :],
                       dr["v_src"](r, q4).rearrange("(t p) c -> p t c", p=P),
                       reads=[dr["b_kvall"]], writes=[b_vf[1 + 4 * r + q4]])

        sidx = [0]
        pending_tail = []
        for h in range(N_HEADS):
            need_load = [h + 1]
            if not pending_tail:
                load_head(h + 1)
                need_load = []
            KT, b_KT, Qp, b_Qp, SG, b_SG = heads[h]
            for (q0, nq, kts) in qtiles:
                def s_mm(kt):
                    banks = []
                    for m in range(2):
                        bank = 2 * (sidx[0] % 2) + m
                        tk.op("pe", lambda e, m=m, bank=bank: e.matmul(
                            ps[bank][:, 0:nq], KT[:, kt * P:(kt + 1) * P], Qp[m][:, q0:q0 + nq],
                            start=True, stop=True),
                            reads=[b_KT, b_Qp[m]], writes=[psb[bank]])
                        banks.append(bank)
                    sidx[0] += 1
                    return banks

                pend = s_mm(kts[0])
                for i, kt in enumerate(kts):
                    if pending_tail and (i == 12 or (i == len(kts) - 1 and len(kts) < 13)):
                        pending_tail.pop(0)()
                    if need_load and not pending_tail:
                        load_head(need_load.pop())
                    cur = pend
                    if i + 1 < len(kts):
                        pend = s_mm(kts[i + 1])
                    es_ = []
                    for m in range(2):
                        E, b_E = e_ring.next()
                        tk.op("act", lambda e, E=E, m=m: e.activation(
                            out=E[:, 0:nq], in_=ps[cur[m]][:, 0:nq], func=AF.Exp, scale=ATTN_SCALE),
                            reads=[psb[cur[m]]], writes=[b_E])
                        es_.append((E, b_E))
                    first, last = (i == 0), (i == len(kts) - 1)
                    for m in range(2):
                        E, b_E = es_[m]
                        tk.op("pe", lambda e, E=E, m=m: e.matmul(
                            ps[4 + m][:, 0:nq], Vf[:, kt, h * P:(h + 1) * P], E[:, 0:nq],
                            start=first, stop=last),
                            reads=[b_E, vf_buf(kt)], writes=[psb[4 + m]], inc=last)
                    for m in range(2):
                        E, b_E = es_[m]
                        tk.op("pe", lambda e, E=E, m=m: e.matmul(
                            ps[6 + m][:, 0:nq], ones_bf[:], E[:, 0:nq], start=first, stop=last),
                            reads=[b_E, b_c], writes=[psb[6 + m]], inc=last)
                oc, zc = [], []
                for m in range(2):
                    o_, b_o = f_ring.next()
                    tk.op("act", lambda e, o_=o_, m=m: e.copy(out=o_[:, 0:nq], in_=ps[4 + m][:, 0:nq]),
                          reads=[psb[4 + m]], writes=[b_o])
                    oc.append((o_, b_o))
                    z_, b_z = f_ring.next()
                    tk.op("dve", lambda e, z_=z_, m=m: e.tensor_copy(z_[:, 0:nq], ps[6 + m][:, 0:nq]),
                          reads=[psb[6 + m]], writes=[b_z])
                    zc.append((z_, b_z))
                for m in range(2):
                    z_, b_z = zc[m]
                    o_, b_o = oc[m]
                    tk.op("dve", lambda e, z_=z_: e.reciprocal(z_[:, 0:nq], z_[:, 0:nq]),
                          reads=[b_z], writes=[b_z])
                    tk.op("pool", lambda e, z_=z_, o_=o_: e.tensor_tensor(
                        o_[:, 0:nq], o_[:, 0:nq], z_[:, 0:nq], op=ALU.mult),
                        reads=[b_z, b_o], writes=[b_o])
                att, b_att = oc[0]
                tk.op("dve", lambda e: e.scalar_tensor_tensor(
                    att[:, 0:nq], oc[1][0][:, 0:nq], neg_lam[:, 0:1], att[:, 0:nq],
                    op0=ALU.mult, op1=ALU.add),
                    reads=[oc[1][1], b_lam], writes=[b_att])
                def make_tail(att=att, b_att=b_att, oc=oc, zc=zc, nq=nq, q0=q0, h=h, SG=SG, b_SG=b_SG):
                    sq, b_sq = oc[1]
                    tk.op("pool", lambda e: e.tensor_tensor(sq[:, 0:nq], att[:, 0:nq], att[:, 0:nq], op=ALU.mult),
                          reads=[b_att], writes=[b_sq])
                    mbank = 2 * (sidx[0] % 2)
                    tk.op("pe", lambda e: e.matmul(ps[mbank][:, 0:nq], ones_f[:], sq[:, 0:nq], start=True, stop=True),
                          reads=[b_sq, b_c], writes=[psb[mbank]])
                    lnv, b_lnv = zc[0]
                    tk.op("act", lambda e: e.activation(out=lnv[:, 0:nq], in_=ps[mbank][:, 0:nq], func=AF.Ln,
                                                        bias=epsb[:, 0:1]),
                          reads=[psb[mbank], b_c], writes=[b_lnv])
                    tk.op("act", lambda e: e.activation(out=lnv[:, 0:nq], in_=lnv[:, 0:nq], func=AF.Exp, scale=-0.5),
                          reads=[b_lnv], writes=[b_lnv])
                    tk.op("dve", lambda e: e.scalar_tensor_tensor(
                        att[:, 0:nq], att[:, 0:nq], gsub[:, 0:1], lnv[:, 0:nq], op0=ALU.mult, op1=ALU.mult),
                        reads=[b_att, b_lnv, b_c], writes=[b_att])
                    yo, b_yo = yo_ring.next()
                    tk.op("pool", lambda e: e.tensor_tensor(yo[:, 0:nq], att[:, 0:nq], SG[:, q0:q0 + nq], op=ALU.mult),
                          reads=[b_att, b_SG], writes=[b_yo])
                    tk.dma("sp", dr["YTat"][h][:, q0:q0 + nq], yo[:, 0:nq], reads=[b_yo], writes=[dr["b_yat"]])
                    return None
                pending_tail.append(make_tail)
        while pending_tail:
            pending_tail.pop(0)()
        tk.barrier()


def load_wout(nc, tk, dr, Wo):
    b_wo = [Buf("wo%d" % i) for i in range(4)]
    for i in range(4):
        tk.dma("pool", Wo[:, 4 * i:4 * i + 4, :], dr["wout"][:, 4 * i:4 * i + 4, :], writes=[b_wo[i]])
    return b_wo


def emit_phase_C(nc, tk, ps, psb, dr, layer, with_ctx):
    with ExitStack() as es:
        sb = lambda name, shape, dt: es.enter_context(nc.sbuf_tensor(uname(name), shape, dt))
        if dr.get("Wo") is not None:
            Wo, b_wo = dr["Wo"]
        else:
            Wo = sb("Wo", [P, KC, D], BF16)
            b_wo = load_wout(nc, tk, dr, Wo)
        gB = [sb("gB%d" % i, [P, D], F32) for i in range(2)]
        lnG = sb("lnG", [P, D], F32)
        lnB = sb("lnB", [P, D], F32)
        epsb = sb("epsC", [P, 1], F32)
        b_c = Buf("constsC", accum=True)
        tk.op("dve", lambda e: e.memset(epsb[:], LN_EPS), writes=[b_c])
        y_ring = Ring(nc, es, "Ys", 2, [P, KC, 512], BF16)
        x_ring = Ring(nc, es, "xC", 3, [P, D], F32)
        T_ring = Ring(nc, es, "TC", 2, [P, D], F32)
        o_ring = Ring(nc, es, "oC", 3, [P, D], F32)
        st_ring = Ring(nc, es, "stC", 2, [P, 32], F32)

        groups = [(512 * t, 512, 512 * t, 512 * t, 0) for t in range(4)]
        if with_ctx:
            groups.append((TL, CTXN, dr["xin_ctx_row0"], dr["xout_ctx_row0"], 1))
        pbank = [0]
        ys_loaded = {}

        def load_ys(g):
            if g >= len(groups) or g in ys_loaded:
                return
            c0_, ncol_ = groups[g][0], groups[g][1]
            Ys_, b_Ys_ = y_ring.next()
            b_Ys_.accum = True
            for k in range(KC):
                src_ap, src_buf = dr["y_src"](k)
                tk.dma("sp", Ys_[:, k, 0:ncol_], src_ap[:, c0_:c0_ + ncol_], reads=[src_buf], writes=[b_Ys_])
            ys_loaded[g] = (Ys_, b_Ys_)

        tiles = []
        for g, (c0, ncol, xr0, or0, gi) in enumerate(groups):
            for st in range(ncol // P):
                tiles.append((g, st, xr0, or0, gi))
        xs_loaded = {}

        def load_x(i):
            if i >= len(tiles) or i in xs_loaded:
                return
            g_, st_, xr0_, or0_, gi_ = tiles[i]
            xs_, b_xs_ = x_ring.next()
            tk.dma("sp", xs_[:], dr["xin"][xr0_ + st_ * P:xr0_ + (st_ + 1) * P, :], reads=[dr["b_xin"]], writes=[b_xs_])
            xs_loaded[i] = (xs_, b_xs_)

        load_ys(0)
        load_x(0)
        tk.dma("sp", gB[0][:], dr["modv"][0:1, 4096:6144].partition_broadcast(P), reads=[dr["b_modv"]], writes=[b_c])
        if with_ctx:
            tk.dma("sp", gB[1][:], dr["modv"][1:2, 4096:6144].partition_broadcast(P), reads=[dr["b_modv"]], writes=[b_c])
        tk.dma("sp", lnG[:], dr["lng"].partition_broadcast(P), writes=[b_c])
        tk.dma("sp", lnB[:], dr["lnb"].partition_broadcast(P), writes=[b_c])
        for ti, (g, st, xr0, or0, gi) in enumerate(tiles):
            if True:
                load_x(ti + 1)
                if st == 0:
                    load_ys(g + 1)
                Ys, b_Ys = ys_loaded[g]
                xs, b_xs = xs_loaded[ti]
                T, b_T = T_ring.next()
                half = pbank[0] % 2
                pbank[0] += 1
                for cg in range(4):
                    bank = 4 * half + cg
                    for k in range(KC):
                        tk.op("pe", lambda e, k=k, bank=bank, cg=cg: e.matmul(
                            ps[bank][:, :], Ys[:, k, st * P:(st + 1) * P], Wo[:, k, cg * 512:(cg + 1) * 512],
                            start=(k == 0), stop=(k == KC - 1)),
                            reads=[b_Ys, b_wo[k // 4]], writes=[psb[bank]], inc=(k == KC - 1))
                    tk.op("dve", lambda e, bank=bank, cg=cg: e.tensor_tensor(
                        T[:, cg * 512:(cg + 1) * 512], ps[bank][:, :], gB[gi][:, cg * 512:(cg + 1) * 512],
                        op=ALU.mult), reads=[psb[bank], b_c], writes=[b_T])
                tk.op("dve", lambda e: e.scalar_tensor_tensor(T[:], xs[:], ALPHA, T[:], op0=ALU.mult, op1=ALU.add),
                      reads=[b_xs, b_T], writes=[b_T])
                stt, b_st = st_ring.next()
                for cg in range(4):
                    tk.op("dve", lambda e, cg=cg: e.bn_stats(stt[:, cg * 6:(cg + 1) * 6], T[:, cg * 512:(cg + 1) * 512]),
                          reads=[b_T], writes=[b_st])
                tk.op("dve", lambda e: e.bn_aggr(stt[:, 24:26], stt[:, 0:24]), reads=[b_st], writes=[b_st])
                tk.op("act", lambda e: e.activation(out=stt[:, 26:27], in_=stt[:, 25:26], func=AF.Ln, bias=epsb[:, 0:1]),
                      reads=[b_st, b_c], writes=[b_st])
                tk.op("act", lambda e: e.activation(out=stt[:, 26:27], in_=stt[:, 26:27], func=AF.Exp, scale=-0.5),
                      reads=[b_st], writes=[b_st])
                tk.op("dve", lambda e: e.scalar_tensor_tensor(
                    stt[:, 27:28], stt[:, 24:25], -1.0, stt[:, 26:27], op0=ALU.mult, op1=ALU.mult),
                    reads=[b_st], writes=[b_st])
                o, b_o = o_ring.next()
                tk.op("act", lambda e: e.activation(out=o[:], in_=T[:], func=AF.Identity,
                                                    scale=stt[:, 26:27], bias=stt[:, 27:28]),
                      reads=[b_T, b_st], writes=[b_o])
                tk.op("pool", lambda e: e.tensor_tensor(o[:], o[:], lnG[:], op=ALU.mult),
                      reads=[b_o, b_c], writes=[b_o])
                tk.op("pool", lambda e: e.tensor_tensor(o[:], o[:], lnB[:], op=ALU.add),
                      reads=[b_o, b_c], writes=[b_o])
                tk.dma("sp", dr["xout"][or0 + st * P:or0 + (st + 1) * P, :], o[:], reads=[b_o], writes=[dr["b_xout"]])
                if dr["hx"] is not None and gi == 0:
                    if or0 + st * P == 0:
                        tk.dma("sp", dr["hx"][0:HALO, :], o[0:HALO, :], reads=[b_o], writes=[dr["b_hx"]])
                    if or0 + (st + 1) * P == TL:
                        tk.dma("sp", dr["hx"][HALO:2 * HALO, :], o[P - HALO:P, :], reads=[b_o], writes=[dr["b_hx"]])
        tk.barrier()


PHASE_BC_INPUTS = [
    ("KTall", (2, 8, P, TL), BF16), ("Vall", (2, TL, 1024), BF16), ("KTc", (8, P, CTXN), BF16),
    ("Vc", (CTXN, 1024), BF16), ("QT", (8, P, NQ), BF16), ("SAG", (8, P, NQ), BF16),
    ("YT", (16, P, NQ), BF16), ("modv", (2, 6144), F32), ("xin", (NT, D), F32),
    ("wout", (P, KC, D), F32), ("lng", (1, D), F32), ("lnb", (1, D), F32),
    ("lamv", (1, 256), F32), ("subg", (P, 1), F32),
]


def build_phase_BC_program(layer, with_ctx, do_B=True, do_C=True):
    nc = bass.Bass("TRN2", target_bir_lowering=False)
    dr = {}
    for name, shape, dt in PHASE_BC_INPUTS:
        dr[name] = declare(nc, name, shape, dt, "ExternalInput")
    dr["xout"] = declare(nc, "xout", (NQ, D), F32, "ExternalOutput")
    dr["YTat"] = declare(nc, "YTat", (8, P, NQ), BF16, "ExternalOutput")
    for b in ("b_kv", "b_kvall", "b_q", "b_y", "b_modv", "b_yat", "b_xout"):
        dr[b] = Buf(b, accum=True)
    dr["b_xin"] = Buf("b_xin")
    dr["xin_ctx_row0"] = COL_CTX
    dr["xout_ctx_row0"] = TL
    dr["hx"] = None

    def y_src(k):
        if 4 <= k < 12:
            return dr["YTat"][k - 4], dr["b_yat"]
        return dr["YT"][k], dr["b_y"]
    dr["y_src"] = y_src
    dr["kt_src"] = lambda r, h: dr["KTall"][r][h]
    dr["v_src"] = lambda r, q4: dr["Vall"][r][q4 * 512:(q4 + 1) * 512, :]
    with ExitStack() as es:
        ps = [es.enter_context(nc.psum_tensor("ps%d" % i, [P, 512], F32)) for i in range(8)]
        psb = [Buf("ps%d" % i, exclusive=True) for i in range(8)]
        tk = Tracker(nc, es)
        if do_B:
            emit_phase_B(nc, tk, ps, psb, dr, layer, with_ctx)
        if do_C:
            emit_phase_C(nc, tk, ps, psb, dr, layer, with_ctx)
        tk.barrier()
    return nc


WKEYS = ["w_mod", "b_mod", "w_in", "conv_w", "lam_q1", "lam_k1", "lam_q2", "lam_k2",
         "subln_g", "pool_w", "pool_scale", "w_out", "ln_g", "ln_b"]


def make_xin(core, xfull_b, ctx_b):
    s = core % 2
    z = np.zeros((HALO, D), np.float32)
    hl = xfull_b[s * TL - HALO:s * TL] if s == 1 else z
    hr = xfull_b[(s + 1) * TL:(s + 1) * TL + HALO] if s == 0 else z
    return np.ascontiguousarray(np.concatenate([xfull_b[s * TL:(s + 1) * TL], hl, hr, ctx_b], 0))


_PROGS = {}


def get_prog(kind, layer, flag):
    key = (kind, layer, flag)
    if key not in _PROGS:
        if kind == "A":
            _PROGS[key] = build_phase_A_program(layer, flag)
        else:
            _PROGS[key] = build_phase_BC_program(layer, flag)
    return _PROGS[key]


def kernel_unfused(**inputs):
    inp = {k: np.asarray(v) for k, v in inputs.items()}
    cores = list(range(NCORES))
    consts = [host_core_consts(c) for c in cores]
    x_cur = [inp["x"][b] for b in range(BATCH)]
    ctx_cur = [inp["ctx"][b] for b in range(BATCH)]
    for l in range(DEPTH):
        last = (l == DEPTH - 1)
        lw = host_layer_weights(l, *[inp[k] for k in WKEYS])
        xins = [make_xin(c, x_cur[c // 2], ctx_cur[c // 2]) for c in cores]
        in_maps = []
        for c in cores:
            b = c // 2
            cvec = np.ascontiguousarray(
                np.stack([inp["c"][b], inp["c_ctx"]], -1).reshape(KC, P, 2).transpose(1, 0, 2))
            m = dict(xin=xins[c], cvec=cvec)
            for k in ("wmod", "bmod", "wA", "wV", "convw", "poolw", "pscale"):
                m[k] = lw[k]
            for k in ("cosT", "sinT", "pm", "ident", "masks", "poolrc"):
                m[k] = consts[c][k]
            in_maps.append(m)
        resA = run_bass_kernel_spmd(get_prog("A", l, not last), in_maps, core_ids=cores).results
        in_maps = []
        for c in cores:
            pr = (c // 2) * 2
            r = resA[c]
            m = dict(KTall=np.stack([np.asarray(resA[pr]["KTg"]), np.asarray(resA[pr + 1]["KTg"])], 0),
                     Vall=np.stack([np.asarray(resA[pr]["Vg"]), np.asarray(resA[pr + 1]["Vg"])], 0),
                     KTc=np.asarray(r["KTc"]), Vc=np.asarray(r["Vc"]), QT=np.asarray(r["QT"]),
                     SAG=np.asarray(r["SAG"]), YT=np.asarray(r["YT"]), modv=np.asarray(r["modv"]),
                     xin=xins[c])
            for k in ("wout", "lng", "lnb", "lamv", "subg"):
                m[k] = lw[k]
            in_maps.append(m)
        resBC = run_bass_kernel_spmd(get_prog("BC", l, not last), in_maps, core_ids=cores).results
        x_cur = [np.concatenate([np.asarray(resBC[2 * b]["xout"])[:TL], np.asarray(resBC[2 * b + 1]["xout"])[:TL]], 0)
                 for b in range(BATCH)]
        if not last:
            ctx_cur = [np.asarray(resBC[2 * b]["xout"])[TL:NQ] for b in range(BATCH)]
    return np.ascontiguousarray(np.stack(x_cur, 0).astype(np.float32))


PAIRS = [[0, 1], [2, 3], [4, 5], [6, 7]]
LAYER_INPUTS = [
    ("wmodH", (6, P, KC, 512), F32), ("bmod", (2, 6144), F32), ("wA", (NBLK, P, KC, 128), F32),
    ("wV", (2, P, KC, 512), F32), ("convw", (P, 4, 3), F32), ("poolw", (P, 4, 128), F32),
    ("pscale", (P, 4), F32), ("wout", (P, KC, D), F32), ("lng", (1, D), F32), ("lnb", (1, D), F32),
    ("lamv", (1, 256), F32), ("subg", (P, 1), F32),
]
CORE_INPUTS = [
    ("xin0", (NT, D), F32), ("cvec", (P, KC, 2), F32), ("cosT", (P, TL), F32), ("sinT", (P, TL), F32),
    ("pm", (P, P), F32), ("ident", (P, P), F32), ("masks", (P, 2), F32), ("poolrc", (P, 128), F32),
]


def build_fused_program():
    nc = bass.Bass("TRN2", target_bir_lowering=False)
    base = {}
    for name, shape, dt in CORE_INPUTS:
        base[name] = declare(nc, name, shape, dt, "ExternalInput")
    lay = []
    for l in range(DEPTH):
        d = {}
        for name, shape, dt in LAYER_INPUTS:
            d[name] = declare(nc, "%s%d" % (name, l), shape, dt, "ExternalInput")
        lay.append(d)
    out = declare(nc, "out", (TL, D), F32, "ExternalOutput")
    internal = lambda name, shape, dt: nc.dram_tensor(name, list(shape), dt).ap()
    it = {
        "KTg": internal("KTg", (8, P, TL), BF16), "Vg": internal("Vg", (TL, 1024), BF16),
        "KTall": internal("KTall", (2, 2, 4, P, TL), BF16), "Vall": internal("Vall", (2, 2, TL // 2, 1024), BF16),
        "KTc": internal("KTc", (8, P, CTXN), BF16), "Vc": internal("Vc", (CTXN, 1024), BF16),
        "QT": internal("QT", (8, P, NQ), BF16), "SAG": internal("SAG", (8, P, NQ), BF16),
        "YT": internal("YT", (16, P, NQ), BF16), "modv": internal("modv", (2, 6144), F32),
        "X1": internal("X1", (NT, D), F32), "hx": internal("hx", (2 * HALO, D), F32),
        "hxall": internal("hxall", (4 * HALO, D), F32),
    }
    mod_dist = {"part": internal("modpart", (2, 3072), F32), "all": internal("modall", (4, 3072), F32),
                "b_part": Buf("b_modpart"), "b_all": Buf("b_modall")}
    bufs = {b: Buf(b, accum=True) for b in ("b_kv", "b_kvall", "b_q", "b_y", "b_modv", "b_yat", "b_x1", "b_hx",
                                             "b_hxall", "b_out")}
    with ExitStack() as es:
        ps = [es.enter_context(nc.psum_tensor("ps%d" % i, [P, 512], F32)) for i in range(8)]
        psb = [Buf("ps%d" % i, exclusive=True) for i in range(8)]
        tk = Tracker(nc, es)
        for l in range(DEPTH):
            last = (l == DEPTH - 1)
            dr = dict(base)
            dr.update(lay[l])
            dr.update(it)
            dr.update(bufs)
            dr["b_yat"] = bufs["b_y"]
            dr["mod_dist"] = mod_dist
            dr["YTat"] = it["YT"][4:12]
            dr["y_src"] = lambda k: (it["YT"][k], bufs["b_y"])
            dr["kt_src"] = lambda r, h: it["KTall"][h // 4][r][h % 4]
            dr["v_src"] = lambda r, q4: it["Vall"][q4 // 2][r][(q4 % 2) * 512:(q4 % 2 + 1) * 512, :]
            if l == 0:
                dr["xin"] = base["xin0"]
                dr["b_xin"] = Buf("b_xin0")
                dr["xout"] = it["X1"]
                dr["b_xout"] = bufs["b_x1"]
                dr["xout_ctx_row0"] = COL_CTX
            else:
                dr["xin"] = it["X1"]
                dr["b_xin"] = bufs["b_x1"]
                dr["xout"] = out
                dr["b_xout"] = bufs["b_out"]
                dr["xout_ctx_row0"] = TL
                dr["hx"] = None
            dr["xin_ctx_row0"] = COL_CTX
            def exchange_kv():
                for ch in range(2):
                    tk.collective("AllGather", PAIRS, it["KTg"][4 * ch:4 * ch + 4].rearrange("h p t -> (h p) t"),
                                  it["KTall"][ch].rearrange("r h p t -> (r h p) t"),
                                  reads=[bufs["b_kv"]], writes=[bufs["b_kvall"]])
                for ch in range(2):
                    tk.collective("AllGather", PAIRS, it["Vg"][ch * (TL // 2):(ch + 1) * (TL // 2), :],
                                  it["Vall"][ch].rearrange("r t c -> (r t) c"),
                                  reads=[bufs["b_kv"]], writes=[bufs["b_kvall"]])
            dr["after_kv"] = exchange_kv
            emit_phase_A(nc, tk, ps, psb, dr, l, not last)
            with ExitStack() as esl:
                Wo = esl.enter_context(nc.sbuf_tensor(uname("Wo"), [P, KC, D], BF16))
                dr["Wo"] = (Wo, load_wout(nc, tk, dr, Wo))
                emit_phase_B(nc, tk, ps, psb, dr, l, not last)
                emit_phase_C(nc, tk, ps, psb, dr, l, not last)
            if not last:
                tk.collective("AllGather", PAIRS, it["hx"], it["hxall"],
                              reads=[bufs["b_hx"]], writes=[bufs["b_hxall"]])
                tk.dma("sp", it["X1"][COL_HL:COL_HL + HALO, :], it["hxall"][HALO:2 * HALO, :],
                       reads=[bufs["b_hxall"]], writes=[bufs["b_x1"]])
                tk.dma("sp", it["X1"][COL_HR:COL_HR + HALO, :], it["hxall"][2 * HALO:3 * HALO, :],
                       reads=[bufs["b_hxall"]], writes=[bufs["b_x1"]])
        tk.barrier()
    return nc


_FUSED = []


def kernel_fused(**inputs):
    inp = {k: np.asarray(v) for k, v in inputs.items()}
    cores = list(range(NCORES))
    lws = [host_layer_weights(l, *[inp[k] for k in WKEYS]) for l in range(DEPTH)]
    in_maps = []
    for c in cores:
        b = c // 2
        cc = host_core_consts(c)
        m = {"xin0": make_xin(c, inp["x"][b], inp["ctx"][b]),
             "cvec": np.ascontiguousarray(
                 np.stack([inp["c"][b], inp["c_ctx"]], -1).reshape(KC, P, 2).transpose(1, 0, 2))}
        for k in ("cosT", "sinT", "pm", "ident", "masks", "poolrc"):
            m[k] = cc[k]
        for l in range(DEPTH):
            for name, _, _ in LAYER_INPUTS:
                if name == "wmodH":
                    m["wmodH%d" % l] = lws[l]["wmod"][6 * (c % 2):6 * (c % 2) + 6]
                else:
                    m["%s%d" % (name, l)] = lws[l][name]
        in_maps.append(m)
    if not _FUSED:
        _FUSED.append(build_fused_program())
    res = run_bass_kernel_spmd(_FUSED[0], in_maps, core_ids=cores).results
    out = np.stack([np.concatenate([np.asarray(res[2 * b]["out"]), np.asarray(res[2 * b + 1]["out"])], 0)
                    for b in range(BATCH)], 0)
    return np.ascontiguousarray(out.astype(np.float32))


def kernel(**inputs):
    return kernel_fused(**inputs)
```
